# Optimizing a Trainium2 kernel written in Bass

```python
import math
import jax, jax.numpy as jnp
from jax import lax
import numpy as np

D_MODEL = 1024
BATCH = 4
SEQ = 8192
DEPTH = 2
DEC_BATCH = 4
DEC_SEQ = 4096
PAST_LEN = 128

D_INNER = 2 * D_MODEL
SSM_HEADDIM = 64
SSM_HEADS = D_INNER // SSM_HEADDIM
SSM_GROUPS = 4
SSM_STATE = 128
SSM_CONV = 5
SSM_CHUNK = 128
SSM_CONV_DIM = D_INNER + 2 * SSM_GROUPS * SSM_STATE
D_CONV = D_MODEL
CONF_KERNEL = 31
FFN_DIM = 2816
HALF_STEP = 0.5
N_MOD = 9
EPS = 1e-6
OFF_XBC = D_INNER
OFF_DT = OFF_XBC + SSM_CONV_DIM
OFF_GLU = OFF_DT + 2 * SSM_HEADS
OFF_GATE = OFF_GLU + 2 * D_CONV
IN_COLS = OFF_GATE + 2 * D_MODEL

kernel_name = 'hybrid_ssd_conformer_encoder'


def rms_norm(x, g):
    xf = x.astype(jnp.float32)
    y = xf * lax.rsqrt(jnp.mean(xf * xf, axis=-1, keepdims=True) + EPS)
    return (y * g.astype(jnp.float32)).astype(x.dtype)


def layer_norm(x, g, b):
    xf = x.astype(jnp.float32)
    mu = jnp.mean(xf, axis=-1, keepdims=True)
    var = jnp.mean(jnp.square(xf - mu), axis=-1, keepdims=True)
    y = (xf - mu) * lax.rsqrt(var + EPS)
    return (y * g.astype(jnp.float32) + b.astype(jnp.float32)).astype(x.dtype)


def modulate(h, shift, scale):
    return h * (1 + scale[:, None, :]) + shift[:, None, :]


def swiglu(h, w_up, w_down):
    a, b = jnp.split(h @ w_up, 2, axis=-1)
    return (jax.nn.silu(a) * b) @ w_down


def depthwise_conv(x, w, b):
    k, ch = w.shape
    y = lax.conv_general_dilated(x, w[:, None, :], window_strides=(1,),
                                 padding=[(k // 2, k // 2)],
                                 dimension_numbers=('NWC', 'WIO', 'NWC'),
                                 feature_group_count=ch)
    return y + b


def segsum(a):
    t = a.shape[-1]
    a_rep = jnp.broadcast_to(a[..., None], a.shape + (t,))
    a_rep = jnp.where(jnp.tril(jnp.ones((t, t), bool), -1), a_rep, 0.0)
    ss = jnp.cumsum(a_rep, axis=-2)
    return jnp.where(jnp.tril(jnp.ones((t, t), bool)), ss, -jnp.inf)


def ssd_chunked(x, dt, a, bm, cm):
    bsz, t, h, p = x.shape
    g, n = bm.shape[2], bm.shape[3]
    k = h // g
    c = t // SSM_CHUNK
    L = SSM_CHUNK
    xs = (x * dt[..., None]).reshape(bsz, c, L, g, k, p)
    da = (dt * a).reshape(bsz, c, L, g, k).transpose(0, 3, 4, 1, 2)
    bc = bm.reshape(bsz, c, L, g, n)
    cc = cm.reshape(bsz, c, L, g, n)
    a_cs = jnp.cumsum(da, axis=-1)
    lmat = jnp.exp(segsum(da))
    cb = jnp.einsum('bclgn,bcsgn->bgcls', cc, bc)
    y_diag = jnp.einsum('bgcls,bgkcls,bcsgkp->bclgkp', cb, lmat, xs)
    decay_states = jnp.exp(a_cs[..., -1:] - a_cs)
    states = jnp.einsum('bclgn,bgkcl,bclgkp->bcgkpn', bc, decay_states, xs)
    a_last = jnp.pad(a_cs[..., -1], [(0, 0), (0, 0), (0, 0), (1, 0)])
    decay_chunk = jnp.exp(segsum(a_last))
    states = jnp.concatenate([jnp.zeros_like(states[:, :1]), states], axis=1)
    states = jnp.einsum('bgkzc,bcgkpn->bzgkpn', decay_chunk, states)[:, :-1]
    y_off = jnp.einsum('bclgn,bcgkpn,bgkcl->bclgkp', cc, states, jnp.exp(a_cs))
    return (y_diag + y_off).reshape(bsz, t, h, p)


def ssd_bidirectional(xs, dt_raw, bm, cm, dt_bias, a_log, d_skip):
    f32 = jnp.float32
    xf, bf, cf = xs.astype(f32), bm.astype(f32), cm.astype(f32)
    dtf = dt_raw.astype(f32)
    dt_fw = jax.nn.softplus(dtf[..., :SSM_HEADS] + dt_bias[0].astype(f32))
    dt_bw = jax.nn.softplus(dtf[..., SSM_HEADS:] + dt_bias[1].astype(f32))
    a_fw = -jnp.exp(a_log[0].astype(f32))
    a_bw = -jnp.exp(a_log[1].astype(f32))
    flip = lambda v: jnp.flip(v, axis=1)
    y_fw = ssd_chunked(xf, dt_fw, a_fw, bf, cf)
    y_bw = flip(ssd_chunked(flip(xf), flip(dt_bw), a_bw, flip(bf), flip(cf)))
    return y_fw + y_bw + d_skip.astype(f32)[:, None] * xf


def hybrid_mixer(u, w_in, ssm_conv_w, ssm_conv_b, dt_bias, a_log, d_skip, ssm_norm,
                 w_proj_ssd, conf_conv_w, conf_conv_b, conf_ln_g, conf_ln_b, w_proj_conv, w_out):
    bsz, t, _ = u.shape
    proj = u @ w_in
    z, xbc, dt_raw, glu_in, gate_in = jnp.split(proj, [OFF_XBC, OFF_DT, OFF_GLU, OFF_GATE], axis=-1)
    xbc = jax.nn.silu(depthwise_conv(xbc, ssm_conv_w, ssm_conv_b))
    xs, bm, cm = jnp.split(xbc, [D_INNER, D_INNER + SSM_GROUPS * SSM_STATE], axis=-1)
    xs = xs.reshape(bsz, t, SSM_HEADS, SSM_HEADDIM)
    bm = bm.reshape(bsz, t, SSM_GROUPS, SSM_STATE)
    cm = cm.reshape(bsz, t, SSM_GROUPS, SSM_STATE)
    y = ssd_bidirectional(xs, dt_raw, bm, cm, dt_bias, a_log, d_skip).reshape(bsz, t, D_INNER)
    y = rms_norm(y * jax.nn.silu(z.astype(jnp.float32)), ssm_norm).astype(u.dtype)
    br_a = y @ w_proj_ssd
    va, vg = jnp.split(glu_in, 2, axis=-1)
    v = va * jax.nn.sigmoid(vg)
    v = depthwise_conv(v, conf_conv_w, conf_conv_b)
    v = jax.nn.silu(layer_norm(v, conf_ln_g, conf_ln_b))
    br_b = v @ w_proj_conv
    g_a, g_b = jnp.split(jax.nn.sigmoid(gate_in), 2, axis=-1)
    return (g_a * br_a + g_b * br_b) @ w_out


def encoder_trunk(x, c, w_ada, b_ada, ffn1_norm, ffn1_up, ffn1_down, mix_norm, w_in,
                  ssm_conv_w, ssm_conv_b, dt_bias, a_log, d_skip, ssm_norm, w_proj_ssd,
                  conf_conv_w, conf_conv_b, conf_ln_g, conf_ln_b, w_proj_conv, w_out,
                  ffn2_norm, ffn2_up, ffn2_down, final_norm):
    h = x
    c_act = jax.nn.silu(c)
    for l in range(DEPTH):
        mods = c_act @ w_ada[l] + b_ada[l]
        sh1, sc1, g1, sh2, sc2, g2, sh3, sc3, g3 = jnp.split(mods, N_MOD, axis=-1)
        u = modulate(rms_norm(h, ffn1_norm[l]), sh1, sc1)
        h = h + HALF_STEP * g1[:, None, :] * swiglu(u, ffn1_up[l], ffn1_down[l])
        u = modulate(rms_norm(h, mix_norm[l]), sh2, sc2)
        h = h + g2[:, None, :] * hybrid_mixer(u, w_in[l], ssm_conv_w[l], ssm_conv_b[l], dt_bias[l],
                                             a_log[l], d_skip[l], ssm_norm[l], w_proj_ssd[l],
                                             conf_conv_w[l], conf_conv_b[l], conf_ln_g[l],
                                             conf_ln_b[l], w_proj_conv[l], w_out[l])
        u = modulate(rms_norm(h, ffn2_norm[l]), sh3, sc3)
        h = h + HALF_STEP * g3[:, None, :] * swiglu(u, ffn2_up[l], ffn2_down[l])
    return rms_norm(h, final_norm)


def setup_inputs(seed: int = 0) -> dict:
    key = jax.random.key(seed)
    ks = iter(list(jax.random.split(key, 48)))
    f32 = jnp.float32
    L, D = DEPTH, D_MODEL

    def nrm(shape, scale):
        return scale * jax.random.normal(next(ks), shape, f32)

    dt0 = jnp.exp(jax.random.uniform(next(ks), (L, 2, SSM_HEADS), f32, math.log(1e-3), math.log(1e-1)))
    dt_bias = dt0 + jnp.log(-jnp.expm1(-dt0))
    a_log = jnp.log(jax.random.uniform(next(ks), (L, 2, SSM_HEADS), f32, 1.0, 16.0))
    return {
        'x_prompt': nrm((BATCH, SEQ, D), 1.0),
        'x_sample': nrm((DEC_BATCH, DEC_SEQ, D), 1.0),
        'c_prompt': nrm((BATCH, D), 1.0),
        'c_sample': nrm((DEC_BATCH, D), 1.0),
        'w_ada': nrm((L, D, N_MOD * D), 0.5 * D ** -0.5),
        'b_ada': nrm((L, N_MOD * D), 0.02),
        'ffn1_norm': 1.0 + nrm((L, D), 0.02),
        'ffn1_up': nrm((L, D, 2 * FFN_DIM), D ** -0.5),
        'ffn1_down': nrm((L, FFN_DIM, D), FFN_DIM ** -0.5),
        'mix_norm': 1.0 + nrm((L, D), 0.02),
        'w_in': nrm((L, D, IN_COLS), D ** -0.5),
        'ssm_conv_w': nrm((L, SSM_CONV, SSM_CONV_DIM), SSM_CONV ** -0.5),
        'ssm_conv_b': nrm((L, SSM_CONV_DIM), 0.02),
        'dt_bias': dt_bias,
        'a_log': a_log,
        'd_skip': 1.0 + nrm((L, SSM_HEADS), 0.1),
        'ssm_norm': 1.0 + nrm((L, D_INNER), 0.02),
        'w_proj_ssd': nrm((L, D_INNER, D), D_INNER ** -0.5),
        'conf_conv_w': nrm((L, CONF_KERNEL, D_CONV), CONF_KERNEL ** -0.5),
        'conf_conv_b': nrm((L, D_CONV), 0.02),
        'conf_ln_g': 1.0 + nrm((L, D_CONV), 0.02),
        'conf_ln_b': nrm((L, D_CONV), 0.02),
        'w_proj_conv': nrm((L, D_CONV, D), D_CONV ** -0.5),
        'w_out': nrm((L, D, D), D ** -0.5),
        'ffn2_norm': 1.0 + nrm((L, D), 0.02),
        'ffn2_up': nrm((L, D, 2 * FFN_DIM), D ** -0.5),
        'ffn2_down': nrm((L, FFN_DIM, D), FFN_DIM ** -0.5),
        'final_norm': 1.0 + nrm((D,), 0.02),
    }


def reference(x_prompt, x_sample, c_prompt, c_sample, w_ada, b_ada, ffn1_norm, ffn1_up, ffn1_down,
              mix_norm, w_in, ssm_conv_w, ssm_conv_b, dt_bias, a_log, d_skip, ssm_norm, w_proj_ssd,
              conf_conv_w, conf_conv_b, conf_ln_g, conf_ln_b, w_proj_conv, w_out,
              ffn2_norm, ffn2_up, ffn2_down, final_norm):
    weights = (w_ada, b_ada, ffn1_norm, ffn1_up, ffn1_down, mix_norm, w_in, ssm_conv_w, ssm_conv_b,
               dt_bias, a_log, d_skip, ssm_norm, w_proj_ssd, conf_conv_w, conf_conv_b, conf_ln_g,
               conf_ln_b, w_proj_conv, w_out, ffn2_norm, ffn2_up, ffn2_down, final_norm)
    y_prompt = encoder_trunk(x_prompt, c_prompt, *weights)
    y_sample = encoder_trunk(x_sample, c_sample, *weights)
    return (y_prompt, y_sample)
```

```python
import numpy as np
import concourse.bass as bass
import concourse.mybir as mybir
from concourse.bass_utils import run_bass_kernel_spmd

F32 = mybir.dt.float32
BF16 = mybir.dt.bfloat16
U8 = mybir.dt.uint8
AF = mybir.ActivationFunctionType
ALU = mybir.AluOpType

D = 1024
DI = 2048
NH = 32
HP = 64
NGRP = 4
NS = 128
KS = 5
CD = 3072
KCF = 31
FF = 2816
IN_COLS = 9280
OFF_XBC = 2048
OFF_DT = 5120
OFF_GLU = 5184
OFF_GATE = 7232
DEPTH = 2
EPS = 1e-6
NCORES = 8
NTOK_FULL = 8192
LINK_FULL = 4096

V_C = 0
V_FN = 8
V_L0 = 16
V_LSZ = 544
V_BADA, V_N1, V_N2, V_N3, V_SCW, V_SCB, V_SNRM, V_CCW, V_CCB, V_LNG, V_LNB, V_DCOL = 0, 72, 80, 88, 96, 216, 240, 256, 504, 512, 520, 528
V_ROWS = 1152

ENGS = ("pe", "act", "dve", "pool", "sp")


class Buf:
    __slots__ = ("name", "last_w", "readers")

    def __init__(self, name):
        self.name = name
        self.last_w = None
        self.readers = {}


class V:
    __slots__ = ("ap", "buf")

    def __init__(self, ap, buf):
        self.ap = ap
        self.buf = buf


class Op:
    __slots__ = ("eng", "fn", "deps", "signals", "idx", "is_dma", "sem", "val", "ninc")

    def __init__(self, eng, fn, is_dma, ninc):
        self.eng = eng
        self.fn = fn
        self.deps = None
        self.signals = False
        self.idx = 0
        self.is_dma = is_dma
        self.sem = None
        self.val = 0
        self.ninc = ninc


class Prog:
    def __init__(self):
        self.ops = {e: [] for e in ENGS}
        self.pending = {e: [] for e in ENGS}
        self.dma_sem_vals = {}
        self.last_dma = {}
        self.nops = 0

    def add(self, eng, fn, reads=(), writes=(), dma_sem=None, ninc=1):
        is_dma = dma_sem is not None
        op = Op(eng, fn, is_dma, ninc)
        deps = {}

        def need(d, raw=False):
            if d is None:
                return
            if d.is_dma:
                k = ("d", d.sem)
                if k not in deps or deps[k].val < d.val:
                    deps[k] = d
            else:
                if d.eng == eng and (not is_dma) and (eng == "pe" or not raw):
                    return
                k = ("e", d.eng)
                if k not in deps or deps[k].idx < d.idx:
                    deps[k] = d

        for b in reads:
            need(b.last_w, True)
        for b in writes:
            need(b.last_w)
            for r in b.readers.values():
                need(r)
        for d in self.pending[eng]:
            need(d)
        self.pending[eng] = []
        if is_dma:
            need(self.last_dma.get(dma_sem))
            v = self.dma_sem_vals.get(dma_sem, 0) + 16 * ninc
            self.dma_sem_vals[dma_sem] = v
            op.sem = dma_sem
            op.val = v
            self.last_dma[dma_sem] = op
        op.deps = list(deps.values())
        op.idx = len(self.ops[eng])
        self.ops[eng].append(op)
        for b in reads:
            b.readers[id(op) if is_dma else eng] = op
        for b in writes:
            b.last_w = op
            b.readers = {}
        self.nops += 1
        return op

    def barrier(self):
        evs = []
        for e in ENGS:
            for o in reversed(self.ops[e]):
                if not o.is_dma:
                    evs.append(o)
                    break
        for o in self.last_dma.values():
            evs.append(o)
        for e in ENGS:
            self.pending[e] = list(evs)

    def emit(self, nc, block):
        for e in ENGS:
            for op in self.ops[e]:
                for d in op.deps:
                    if not d.is_dma:
                        d.signals = True
        for e in ENGS:
            c = 0
            for op in self.ops[e]:
                if not op.is_dma and op.signals:
                    c += 1
                    op.val = c
        esem = {e: nc.alloc_semaphore("sem_" + e) for e in ENGS}
        dsem = {k: nc.alloc_semaphore("dsem_%d" % i) for i, k in enumerate(self.dma_sem_vals)}

        def make(e):
            ops = self.ops[e]

            def body(h):
                seen = {}
                for op in ops:
                    for d in op.deps:
                        if d.is_dma:
                            key, sem, val = ("d", d.sem), dsem[d.sem], d.val
                        else:
                            key, sem, val = ("e", d.eng), esem[d.eng], d.val
                        if seen.get(key, 0) >= val:
                            continue
                        seen[key] = val
                        h.wait_ge(sem, val)
                    ins = op.fn(h)
                    if op.is_dma:
                        if not isinstance(ins, (list, tuple)):
                            ins = [ins]
                        assert len(ins) == op.ninc, (len(ins), op.ninc)
                        for i_ in ins:
                            i_.then_inc(dsem[op.sem], 16)
                    elif op.signals:
                        ins.then_inc(esem[e], 1)
            return body

        block.tensor(make("pe"))
        block.scalar(make("act"))
        block.vector(make("dve"))
        block.gpsimd(make("pool"))
        block.sync(make("sp"))


class Arena:
    def __init__(self, big, lo, hi):
        self.big, self.lo, self.hi, self.off = big, lo, hi, lo

    def reset(self):
        self.off = self.lo

    def take(self, name, shape, dtype, buf=None):
        esz = 4 if dtype == F32 else (1 if dtype == U8 else 2)
        n = 1
        for s in shape[1:]:
            n *= s
        nb = n * esz
        off = (self.off + 63) // 64 * 64
        assert off + nb <= self.hi, ("SBUF arena overflow", name, off + nb, self.hi)
        self.off = off + nb
        ap = self.big[:, off:off + nb].bitcast(dtype)
        if len(shape) == 3:
            ap = ap.rearrange("p (a b) -> p a b", b=shape[2])
        elif len(shape) == 4:
            ap = ap.rearrange("p (a b c) -> p a b c", b=shape[2], c=shape[3])
        return V(ap, buf if buf is not None else Buf(name))


def bc(ap, n):
    return ap.unsqueeze(len(ap.shape)).broadcast_to(list(ap.shape) + [n])


def build(ntok=NTOK_FULL, link=LINK_FULL, depth=DEPTH, debug=False):
    assert ntok % 512 == 0 and link % 512 == 0
    NT = ntok // 512
    nc = bass.Bass("TRN2", target_bir_lowering=False)
    P = Prog()

    def din(name, shape, dt=F32):
        return nc.dram_tensor(name, list(shape), dt, kind="ExternalInput").ap()

    def dscr(name, shape, dt):
        kind = "ExternalOutput" if (debug and name in debug) else "Internal"
        return nc.dram_tensor(name, list(shape), dt, kind=kind).ap()

    x_d = din("x", [ntok, D])
    flag_d = din("flag", [128, 1])
    vecs_d = din("vecs", [V_ROWS, 128])
    consts_d = din("consts", [128, 768])
    rowb_d = din("rowb", [128, DEPTH * 160])
    w_ada_d = din("w_ada", [DEPTH, D, 9 * D])
    wsrc = {
        "f1u": din("ffn1_up", [DEPTH, D, 2 * FF]),
        "f1d": din("ffn1_down", [DEPTH, FF, D]),
        "win": din("w_in", [DEPTH, D, IN_COLS]),
        "pa": din("w_proj_ssd", [DEPTH, DI, D]),
        "pb": din("w_proj_conv", [DEPTH, D, D]),
        "wo": din("w_out", [DEPTH, D, D]),
        "f2u": din("ffn2_up", [DEPTH, D, 2 * FF]),
        "f2d": din("ffn2_down", [DEPTH, FF, D]),
    }
    y_d = nc.dram_tensor("y", [ntok, D], F32, kind="ExternalOutput").ap()

    wbf = {k: dscr("wb_" + k, list(v.shape), BF16) for k, v in wsrc.items()}
    diagc_d = dscr("diagc", [DEPTH, 8, 128, KCF * 128], BF16)
    H1_d = dscr("H1", [D, ntok], F32)
    HN_d = dscr("HN", [D, ntok], F32)
    ZS_d = dscr("ZS", [DI, ntok], F32)
    XBCP_d = dscr("XBCP", [CD, ntok], BF16)
    XSBC_d = dscr("XSBC", [CD, ntok], BF16)
    DT_d = dscr("DT", [ntok, 64], F32)
    VP_d = dscr("VP", [D, ntok], BF16)
    GA_d = dscr("GA", [D, ntok], F32)
    GB_d = dscr("GB", [D, ntok], F32)
    YF_d = dscr("YF", [ntok, DI], F32)
    Y_d = dscr("Y", [DI, ntok], F32)

    def fm(ap):
        return ap.rearrange("(c p) t -> p c t", p=128)

    avail = nc.sbuf_bytes_remaining() if callable(nc.sbuf_bytes_remaining) else nc.sbuf_bytes_remaining
    SB_BYTES = (int(avail) // 64) * 64 - 256
    big = nc.alloc_sbuf_tensor("big", [128, SB_BYTES], U8)
    PS = nc.alloc_psum_tensor("ps", [128, 8, 512], F32)
    banks = [V(PS[:, i, :], Buf("bank%d" % i)) for i in range(8)]
    bank_ctr = [0]

    def nextbank():
        b = banks[bank_ctr[0] % 8]
        bank_ctr[0] += 1
        return b

    pers = Arena(big, 0, 16384)
    CONST = pers.take("const", [128, 768], F32)
    IDENT, TRIU, TRIL, NEGF32, NEGB32, ONES = [CONST.ap[:, i * 128:(i + 1) * 128] for i in range(6)]
    CBF = pers.take("constbf", [128, 768], BF16)
    IDENTB, NEGFB, NEGBB, TRIUB, TRILB, ONESB = [CBF.ap[:, i * 128:(i + 1) * 128] for i in range(6)]
    VEC = pers.take("vec", [128, V_ROWS], F32)
    ROWB = pers.take("rowb", [128, DEPTH * 160], F32)
    ANEG = pers.take("aneg", [128, DEPTH * 64], F32)
    MODS = pers.take("mods", [128, DEPTH * 72], F32)
    DER = pers.take("der", [128, DEPTH * 72], F32)
    CACT = pers.take("cact", [128, 8, 2], F32)
    FLAG = pers.take("flag", [128, 1], F32)
    arena = Arena(big, 16384, SB_BYTES)

    def vcol(r):
        return VEC.ap[:, r:r + 1]

    def dma(eng, out, in_, reads, writes, sem, nsplit=1, split_axis=1):
        if nsplit == 1:
            P.add(eng, lambda h: h.dma_start(out=out, in_=in_), reads, writes, dma_sem=sem)
            return
        n = out.shape[split_axis]
        step = (n + nsplit - 1) // nsplit
        pieces = []
        for a in range(0, n, step):
            b = min(n, a + step)
            idx = [slice(None)] * len(out.shape)
            idx[split_axis] = slice(a, b)
            pieces.append((out[tuple(idx)], in_[tuple(idx)]))

        def fn(h):
            return [h.dma_start(out=o, in_=i) for o, i in pieces]
        P.add(eng, fn, reads, writes, dma_sem=sem, ninc=len(pieces))

    def mm(outv, out_ap, lhsT, rhs, start, stop, reads):
        P.add("pe", lambda h: h.matmul(out_ap, lhsT, rhs, start=start, stop=stop),
              [r.buf for r in reads], [outv.buf])

    def tr(outv, out_ap, in_ap, ident, reads):
        P.add("pe", lambda h: h.transpose(out_ap, in_ap, ident), [r.buf for r in reads], [outv.buf])

    def act(out_ap, in_ap, func, reads, writes, bias=None, scale=None):
        kw = {}
        if bias is not None:
            kw["bias"] = bias
        if scale is not None:
            kw["scale"] = scale
        P.add("act", lambda h: h.activation(out=out_ap, in_=in_ap, func=func, **kw),
              [r.buf for r in reads], [w.buf for w in writes])

    def tt(eng, out_ap, in0, in1, op, reads, writes):
        P.add(eng, lambda h: h.tensor_tensor(out=out_ap, in0=in0, in1=in1, op=op),
              [r.buf for r in reads], [w.buf for w in writes])

    def ts(eng, out_ap, in0, s1, op0, reads, writes, s2=None, op1=None):
        if op1 is None:
            P.add(eng, lambda h: h.tensor_scalar(out=out_ap, in0=in0, scalar1=s1, scalar2=None, op0=op0),
                  [r.buf for r in reads], [w.buf for w in writes])
        else:
            P.add(eng, lambda h: h.tensor_scalar(out=out_ap, in0=in0, scalar1=s1, scalar2=s2, op0=op0, op1=op1),
                  [r.buf for r in reads], [w.buf for w in writes])

    def stt(out_ap, in0, scalar, op0, in1, op1, reads, writes):
        P.add("dve", lambda h: h.scalar_tensor_tensor(out=out_ap, in0=in0, scalar=scalar, op0=op0, in1=in1, op1=op1),
              [r.buf for r in reads], [w.buf for w in writes])

    def cp(eng, out_ap, in_ap, reads, writes):
        if eng == "act":
            act(out_ap, in_ap, AF.Copy, reads, writes)
        else:
            P.add(eng, lambda h: h.tensor_copy(out=out_ap, in_=in_ap),
                  [r.buf for r in reads], [w.buf for w in writes])

    def memset(eng, v, ap, val):
        P.add(eng, lambda h: h.memset(ap, val), [], [v.buf])

    class Ring:
        def __init__(self, ar, name, n, shape, dtype):
            self.vs = [ar.take("%s%d" % (name, i), shape, dtype) for i in range(n)]
            self.i = 0
            self.name = name

        def next(self):
            k = self.i % len(self.vs)
            self.i += 1
            return self.vs[k], (self.name, k)

    class WStream:
        def __init__(self, slots, name):
            self.slots, self.name = slots, name
            self.sched, self.loaded, self.cur = [], 0, 0

        def extend(self, items):
            self.sched.extend(items)

        def _load(self, k):
            slot = self.slots[k % len(self.slots)]
            pieces = self.sched[k](slot)

            def fn(h):
                return [h.dma_start(out=o, in_=i) for o, i in pieces]
            P.add("sp", fn, [], [slot.buf], dma_sem=(self.name, k % len(self.slots)), ninc=len(pieces))

        def next(self):
            k = self.cur
            self.cur += 1
            want = min(len(self.sched), k + len(self.slots))
            while self.loaded < want:
                self._load(self.loaded)
                self.loaded += 1
            return self.slots[k % len(self.slots)]

    arena.reset()
    dma("sp", CONST.ap, consts_d, [], [CONST.buf], "c0")
    dma("sp", ROWB.ap, rowb_d, [], [ROWB.buf], "c1")
    dma("sp", FLAG.ap, flag_d, [], [FLAG.buf], "c2")
    VST = arena.take("vst", [128, 9, 128], F32)
    dma("sp", VST.ap, vecs_d.rearrange("(b p) f -> p b f", p=128), [], [VST.buf], "c3")
    cp("dve", CBF.ap[:, 0:128], IDENT, [CONST], [CBF])
    cp("dve", CBF.ap[:, 128:256], NEGF32, [CONST], [CBF])
    cp("dve", CBF.ap[:, 256:384], NEGB32, [CONST], [CBF])
    cp("dve", CBF.ap[:, 384:512], TRIU, [CONST], [CBF])
    cp("dve", CBF.ap[:, 512:640], TRIL, [CONST], [CBF])
    cp("dve", CBF.ap[:, 640:768], ONES, [CONST], [CBF])
    for b0 in range(0, 9, 4):
        nb = min(4, 9 - b0)
        bk = nextbank()
        for b in range(nb):
            tr(bk, bk.ap[:, b * 128:(b + 1) * 128], VST.ap[:, b0 + b, :], IDENT, [VST, CONST])
        cp("dve", VEC.ap[:, b0 * 128:(b0 + nb) * 128], bk.ap[:, 0:nb * 128], [bk], [VEC])
    act(CACT.ap[:, :, 0], VEC.ap[:, V_C:V_C + 8], AF.Silu, [VEC], [CACT])
    act(CACT.ap[:, :, 1], VEC.ap[:, V_C:V_C + 8], AF.Silu, [VEC], [CACT])
    for l in range(depth):
        act(ANEG.ap[:, l * 64:(l + 1) * 64], ROWB.ap[:, l * 160 + 64:l * 160 + 128], AF.Exp, [ROWB], [ANEG])
    ts("dve", ANEG.ap, ANEG.ap, -1.0, ALU.mult, [ANEG], [ANEG])
    WA = [arena.take("wa%d" % i, [128, 8, 1024], F32) for i in range(2)]
    wai = 0
    for l in range(depth):
        bk = nextbank()
        for q in range(9):
            wa = WA[wai % 2]
            dma("sp", wa.ap, w_ada_d[l].rearrange("(k p) n -> p k n", p=128)[:, :, q * 1024:(q + 1) * 1024],
                [], [wa.buf], ("wa", wai % 2))
            wai += 1
            for mc in range(8):
                col = (q * 8 + mc) * 2
                for kc in range(8):
                    mm(bk, bk.ap[:, col:col + 2], wa.ap[:, kc, mc * 128:(mc + 1) * 128], CACT.ap[:, kc, :],
                       kc == 0, kc == 7, [wa, CACT])
        vb = V_L0 + l * V_LSZ
        tt("dve", MODS.ap[:, l * 72:(l + 1) * 72], bk.ap[:, 0:144:2], VEC.ap[:, vb + V_BADA:vb + V_BADA + 72], ALU.add,
           [bk, VEC], [MODS])
        m0 = l * 72
        for si, (nrm, half) in enumerate(((V_N1, True), (V_N2, False), (V_N3, True))):
            sh = MODS.ap[:, m0 + si * 24:m0 + si * 24 + 8]
            sc = MODS.ap[:, m0 + si * 24 + 8:m0 + si * 24 + 16]
            g = MODS.ap[:, m0 + si * 24 + 16:m0 + si * 24 + 24]
            d0 = l * 72 + si * 24
            stt(DER.ap[:, d0:d0 + 8], sc, 1.0, ALU.add, VEC.ap[:, vb + nrm:vb + nrm + 8], ALU.mult, [MODS, VEC], [DER])
            cp("dve", DER.ap[:, d0 + 8:d0 + 16], sh, [MODS], [DER])
            ts("dve", DER.ap[:, d0 + 16:d0 + 24], g, 0.5 if half else 1.0, ALU.mult, [MODS], [DER])

    def dcol(l, si, which, fc):
        c = l * 72 + si * 24 + which * 8 + fc
        return DER.ap[:, c:c + 1]

    CW = 2048
    ceng = ["dve", "act", "pool"]
    cast_cnt = [0]

    def cast_list(l, keys):
        out = []
        for k in keys:
            src, dst = wsrc[k][l], wbf[k][l]
            Kd, Nd = src.shape
            for kc in range(Kd // 128):
                for c0 in range(0, Nd, CW):
                    out.append((src, dst, kc, c0, min(CW, Nd - c0)))
        return out

    def do_cast(desc, r32, rbf, engs=("dve", "act", "pool"), steng="pool"):
        src, dst, kc, c0, w = desc
        s32, sem32 = r32.next()
        sbf, sembf = rbf.next()
        dma("sp", s32.ap[:, 0:w], src[kc * 128:(kc + 1) * 128, c0:c0 + w], [], [s32.buf], sem32)
        cp(engs[cast_cnt[0] % len(engs)], sbf.ap[:, 0:w], s32.ap[:, 0:w], [s32], [sbf])
        dma(steng, dst[kc * 128:(kc + 1) * 128, c0:c0 + w], sbf.ap[:, 0:w], [sbf.buf], [], sembf)
        cast_cnt[0] += 1

    def do_diag(l, fc, ring):
        vb_ = V_L0 + l * V_LSZ
        sv, sem = ring.next()
        for k in range(KCF):
            ts("dve", sv.ap[:, k, :], IDENT, vcol(vb_ + V_CCW + k * 8 + fc), ALU.mult, [CONST, VEC], [sv])
        dma("pool", diagc_d[l, fc].rearrange("p (k j) -> p k j", j=128), sv.ap, [sv.buf], [], sem)

    cst32 = Ring(arena, "cst32", 4, [128, CW], F32)
    cstbf = Ring(arena, "cstbf", 4, [128, CW], BF16)
    for l in range(depth):
        for desc in cast_list(l, ("f1u", "f1d", "win", "pa", "pb", "wo", "f2u", "f2d")):
            do_cast(desc, cst32, cstbf)
    dgs0 = Ring(arena, "dgs0", 2, [128, KCF, 128], BF16)
    for l in range(depth):
        for fc in range(8):
            do_diag(l, fc, dgs0)
    deferred_cast = []
    deferred_diag = []
    P.barrier()

    def slab_up(key, l, j0, nj):
        def f(slot):
            w = wbf[key][l].rearrange("(k p) n -> p k n", p=128)
            dst = slot.ap.bitcast(BF16)[:, 0:8 * 1024].rearrange("p (k t c) -> p k t c", t=2, c=512)
            return [(dst[:, :, 0, 0:nj * 128], w[:, :, j0 * 128:(j0 + nj) * 128]),
                    (dst[:, :, 1, 0:nj * 128], w[:, :, FF + j0 * 128:FF + (j0 + nj) * 128])]
        return f

    def slab_cols(key, l, c0, ncols, KC):
        def f(slot):
            w = wbf[key][l].rearrange("(k p) n -> p k n", p=128)
            dst = slot.ap.bitcast(BF16)[:, 0:KC * ncols].rearrange("p (k c) -> p k c", c=ncols)
            if KC > 8:
                h = KC // 2
                return [(dst[:, 0:h, :], w[:, 0:h, c0:c0 + ncols]), (dst[:, h:KC, :], w[:, h:KC, c0:c0 + ncols])]
            return [(dst, w[:, :, c0:c0 + ncols])]
        return f

    def slab_pair(key, l, c0a, c0b, ncols):
        def f(slot):
            w = wbf[key][l].rearrange("(k p) n -> p k n", p=128)
            dst = slot.ap.bitcast(BF16)[:, 0:8 * 2 * ncols].rearrange("p (k t c) -> p k t c", t=2, c=ncols)
            return [(dst[:, :, 0, :], w[:, :, c0a:c0a + ncols]), (dst[:, :, 1, :], w[:, :, c0b:c0b + ncols])]
        return f

    def slab_diag(l, fc):
        def f(slot):
            dst = slot.ap.bitcast(BF16)[:, 0:KCF * 128]
            return [(dst, diagc_d[l, fc])]
        return f

    def wview(slot, KC, ncols):
        return slot.ap.bitcast(BF16)[:, 0:KC * ncols].rearrange("p (k c) -> p k c", c=ncols)

    def wview2(slot, ncols):
        return slot.ap.bitcast(BF16)[:, 0:8 * 2 * ncols].rearrange("p (k t c) -> p k t c", t=2, c=ncols)

    def rstd_from(bk, RSTD, TMP, nfeat):
        act(TMP.ap, bk.ap, AF.Ln, [bk], [TMP], bias=EPS, scale=1.0 / nfeat)
        act(bk.ap, TMP.ap, AF.Exp, [TMP], [bk], scale=-0.5)

    def norm_mod(Hs, U, l, si, cm):
        bk = nextbank()
        for fc in range(8):
            sq, _ = cm["sqb"].next()
            act(sq.ap, Hs.ap[:, fc, :], AF.Square, [Hs], [sq])
            mm(bk, bk.ap, ONESB, sq.ap, fc == 0, fc == 7, [sq, CBF])
        rstd_from(bk, cm["rstd"], cm["tmp0"], D)
        for fc in range(8):
            t, _ = cm["tmpn"].next()
            stt(t.ap, Hs.ap[:, fc, :], dcol(l, si, 0, fc), ALU.mult, bk.ap, ALU.mult, [Hs, bk, DER], [t])
            act(U.ap[:, fc, :], t.ap, AF.Identity, [t, DER], [U], bias=dcol(l, si, 1, fc))

    def ffn(Hs, U, HID, l, si, ws, cm):
        j = 0
        while j < 22:
            nj = min(4, 22 - j)
            slot = ws.next()
            wv = wview2(slot, 512)
            for jj in range(nj):
                ba, bb = nextbank(), nextbank()
                for kc in range(8):
                    mm(ba, ba.ap, wv[:, kc, 0, jj * 128:(jj + 1) * 128], U.ap[:, kc, :], kc == 0, kc == 7, [slot, U])
                for kc in range(8):
                    mm(bb, bb.ap, wv[:, kc, 1, jj * 128:(jj + 1) * 128], U.ap[:, kc, :], kc == 0, kc == 7, [slot, U])
                sa, _ = cm["silu"].next()
                act(sa.ap, ba.ap, AF.Silu, [ba], [sa])
                tt("dve", HID.ap[:, j + jj, :], sa.ap, bb.ap, ALU.mult, [sa, bb], [HID])
            j += nj
        for m0 in range(0, 8, 2):
            slot = ws.next()
            wv = wview(slot, 22, 256)
            for mm_ in range(2):
                mc = m0 + mm_
                bk = nextbank()
                for kc in range(22):
                    mm(bk, bk.ap, wv[:, kc, mm_ * 128:(mm_ + 1) * 128], HID.ap[:, kc, :], kc == 0, kc == 21, [slot, HID])
                stt(Hs.ap[:, mc, :], bk.ap, dcol(l, si, 2, mc), ALU.mult, Hs.ap[:, mc, :], ALU.add, [bk, Hs, DER], [Hs])

    def ffn_sched(key_u, key_d, l):
        s = []
        j = 0
        while j < 22:
            nj = min(4, 22 - j)
            s.append(slab_up(key_u, l, j, nj))
            j += nj
        for m0 in range(0, 8, 2):
            s.append(slab_cols(key_d, l, m0 * 128, 256, 22))
        return s

    def common(ar, nh=2):
        cm = {}
        cm["H32"] = [ar.take("h32_%d" % i, [128, 8, 512], F32) for i in range(nh)]
        cm["U"] = ar.take("u", [128, 8, 512], BF16)
        cm["HID"] = ar.take("hid", [128, 22, 512], BF16)
        cm["sqb"] = Ring(ar, "sqb", 3, [128, 512], BF16)
        cm["rstd"] = None
        cm["tmp0"] = ar.take("tmp0", [128, 512], F32)
        cm["tmpn"] = Ring(ar, "tmpn", 2, [128, 512], F32)
        cm["silu"] = Ring(ar, "silu", 3, [128, 512], F32)
        cm["wslots"] = [ar.take("wslot%d" % i, [128, 16384], U8) for i in range(3)]
        return cm

    def sweep_A(l):
        arena.reset()
        cm = common(arena)
        XT = arena.take("xt", [128, 4, 1024], F32)
        st32 = Ring(arena, "st32", 2, [128, 4, 512], F32)
        stbf = Ring(arena, "stbf", 2, [128, 4, 512], BF16)
        stdt = Ring(arena, "stdt", 2, [128, 4, 64], F32)
        if l == 0 and (deferred_cast or deferred_diag):
            c32 = Ring(arena, "c32A", 2, [128, CW], F32)
            cbf = Ring(arena, "cbfA", 2, [128, CW], BF16)
            dgr = Ring(arena, "dgsA", 1, [128, KCF, 128], BF16)
            per_tile = (len(deferred_cast) + NT - 1) // NT
            per_tile_d = (len(deferred_diag) + NT - 1) // NT
        ws = WStream(cm["wslots"], "wA")
        for i in range(NT):
            s = ffn_sched("f1u", "f1d", l)
            for q in range(4):
                s.append(slab_cols("win", l, q * 512, 512, 8))
            for q in range(6):
                s.append(slab_cols("win", l, OFF_XBC + q * 512, 512, 8))
            s.append(slab_cols("win", l, OFF_DT, 64, 8))
            for q in range(2):
                s.append(slab_pair("win", l, OFF_GLU + q * 512, OFF_GLU + D + q * 512, 512))
            for q in range(4):
                s.append(slab_cols("win", l, OFF_GATE + q * 512, 512, 8))
            ws.extend(s)
        U, HID = cm["U"], cm["HID"]

        def load_h(i):
            Hs = cm["H32"][i % 2]
            t0 = i * 512
            if l == 0:
                dma("sp", XT.ap, x_d.rearrange("(n p) f -> p n f", p=128)[:, i * 4:(i + 1) * 4, :], [], [XT.buf], "xt")
                for fc in range(8):
                    bk = nextbank()
                    for q in range(4):
                        tr(bk, bk.ap[:, q * 128:(q + 1) * 128], XT.ap[:, q, fc * 128:(fc + 1) * 128], IDENT, [XT, CONST])
                    cp("act" if fc % 2 else "dve", Hs.ap[:, fc, :], bk.ap, [bk], [Hs])
            else:
                dma("sp", Hs.ap, fm(HN_d)[:, :, t0:t0 + 512], [], [Hs.buf], ("h32", i % 2), nsplit=2)

        load_h(0)
        for i in range(NT):
            Hs = cm["H32"][i % 2]
            t0 = i * 512
            norm_mod(Hs, U, l, 0, cm)
            if i + 1 < NT:
                load_h(i + 1)
            ffn(Hs, U, HID, l, 0, ws, cm)
            dma("pool", fm(H1_d)[:, :, t0:t0 + 512], Hs.ap, [Hs.buf], [], ("h1st", i % 2), nsplit=2)
            if l == 0 and (deferred_cast or deferred_diag):
                for _ in range(per_tile):
                    if deferred_cast:
                        do_cast(deferred_cast.pop(0), c32, cbf, engs=("dve", "act"), steng="sp")
                for _ in range(per_tile_d):
                    if deferred_diag:
                        do_diag(*deferred_diag.pop(0), dgr)
            norm_mod(Hs, U, l, 1, cm)
            for q in range(4):
                slot = ws.next()
                wv = wview(slot, 8, 512)
                sv, sem = st32.next()
                for mm_ in range(4):
                    bk = nextbank()
                    for kc in range(8):
                        mm(bk, bk.ap, wv[:, kc, mm_ * 128:(mm_ + 1) * 128], U.ap[:, kc, :], kc == 0, kc == 7, [slot, U])
                    act(sv.ap[:, mm_, :], bk.ap, AF.Silu, [bk], [sv])
                dma("pool", fm(ZS_d)[:, q * 4:(q + 1) * 4, t0:t0 + 512], sv.ap, [sv.buf], [], sem)
            for q in range(6):
                slot = ws.next()
                wv = wview(slot, 8, 512)
                sv, sem = stbf.next()
                for mm_ in range(4):
                    bk = nextbank()
                    for kc in range(8):
                        mm(bk, bk.ap, wv[:, kc, mm_ * 128:(mm_ + 1) * 128], U.ap[:, kc, :], kc == 0, kc == 7, [slot, U])
                    cp("dve", sv.ap[:, mm_, :], bk.ap, [bk], [sv])
                dma("pool", fm(XBCP_d)[:, q * 4:(q + 1) * 4, t0:t0 + 512], sv.ap, [sv.buf], [], sem)
            slot = ws.next()
            wv = wview(slot, 8, 64)
            bk = nextbank()
            for q in range(4):
                for kc in range(8):
                    mm(bk, bk.ap[:, q * 64:(q + 1) * 64], U.ap[:, kc, q * 128:(q + 1) * 128], wv[:, kc, :], kc == 0, kc == 7,
                       [slot, U])
            sv, sem = stdt.next()
            cp("dve", sv.ap, bk.ap[:, 0:256].rearrange("p (q c) -> p q c", c=64), [bk], [sv])
            dma("pool", DT_d.rearrange("(n p) c -> p n c", p=128)[:, i * 4:(i + 1) * 4, :], sv.ap, [sv.buf], [], sem)
            for q in range(2):
                slot = ws.next()
                wv = wview2(slot, 512)
                sv, sem = stbf.next()
                for mm_ in range(4):
                    ba, bg = nextbank(), nextbank()
                    for kc in range(8):
                        mm(ba, ba.ap, wv[:, kc, 0, mm_ * 128:(mm_ + 1) * 128], U.ap[:, kc, :], kc == 0, kc == 7, [slot, U])
                    for kc in range(8):
                        mm(bg, bg.ap, wv[:, kc, 1, mm_ * 128:(mm_ + 1) * 128], U.ap[:, kc, :], kc == 0, kc == 7, [slot, U])
                    sg, _ = cm["silu"].next()
                    act(sg.ap, bg.ap, AF.Sigmoid, [bg], [sg])
                    tt("dve", sv.ap[:, mm_, :], sg.ap, ba.ap, ALU.mult, [sg, ba], [sv])
                dma("pool", fm(VP_d)[:, q * 4:(q + 1) * 4, t0:t0 + 512], sv.ap, [sv.buf], [], sem)
            for q in range(4):
                slot = ws.next()
                wv = wview(slot, 8, 512)
                sv, sem = st32.next()
                for mm_ in range(4):
                    bk = nextbank()
                    for kc in range(8):
                        mm(bk, bk.ap, wv[:, kc, mm_ * 128:(mm_ + 1) * 128], U.ap[:, kc, :], kc == 0, kc == 7, [slot, U])
                    act(sv.ap[:, mm_, :], bk.ap, AF.Sigmoid, [bk], [sv])
                dst = GA_d if q < 2 else GB_d
                qq = q % 2
                dma("pool", fm(dst)[:, qq * 4:(qq + 1) * 4, t0:t0 + 512], sv.ap, [sv.buf], [], sem)
        P.barrier()

    def sweep_S(l, d):
        fwd = d == 0
        arena.reset()
        vb = V_L0 + l * V_LSZ
        XS2 = [arena.take("xsbc%d" % i, [128, 24, 512], BF16) for i in range(2)]
        DTT = arena.take("dtt", [128, 4, 64], F32)
        DTX = arena.take("dtx", [128, 4, 32], F32)
        DTP2 = [arena.take("dtp%d" % i, [128, 4, 32], F32) for i in range(2)]
        DA2 = [arena.take("da%d" % i, [128, 4, 32], F32) for i in range(2)]
        DAH2 = [arena.take("dah%d" % i, [128, 4, 32], BF16) for i in range(2)]
        DAL2 = [arena.take("dal%d" % i, [128, 4, 32], BF16) for i in range(2)]
        XSDT = Ring(arena, "xsdt", 2, [128, 2048], BF16)
        BTOK = Ring(arena, "btok", 2, [128, 4, 128], BF16)
        CBTS = Ring(arena, "cbts", 2, [128, 512], BF16)
        SMALL = Ring(arena, "small", 2, [128, 5, 32], F32)
        ERING = Ring(arena, "e", 5, [128, 4, 128], BF16)
        GRING = Ring(arena, "g", 6, [128, 4, 128], BF16)
        XSD = Ring(arena, "xsd", 2, [128, 2048], BF16)
        DTDEC = Ring(arena, "dtdec", 2, [128, 32], F32)
        T1R = Ring(arena, "t1", 2, [128, 512], F32)
        T2R = Ring(arena, "t2", 4 if fwd else 2, [128, 512], F32)
        HT = Ring(arena, "ht", 3, [128, 512], F32)
        H32s = arena.take("hstate", [128, 2048], F32)
        HBF = arena.take("hbf", [128, 2048], BF16)
        if fwd:
            XP = arena.take("xp", [128, 24, 516], BF16)
            DG = arena.take("dg", [128, 24, KS, 128], BF16)
        else:
            YFW = Ring(arena, "yfw", 2, [128, 2048], F32)
            YFM2 = [arena.take("yfm%d" % i, [128, 16, 512], F32) for i in range(2)]
        TRI = TRIU if fwd else TRIL
        TRIB = TRIUB if fwd else TRILB
        NEGM = NEGFB if fwd else NEGBB
        rot = [0]
        RB = [banks[0], banks[1]]

        def rb():
            b_ = RB[rot[0] % 2]
            rot[0] += 1
            return b_
        BK_A = [banks[2], banks[7], banks[5], banks[3]]
        BK_Y1 = [banks[4], banks[6]]
        HG = [Buf("hg%d" % g_) for g_ in range(4)]
        HBG = [Buf("hbg%d" % g_) for g_ in range(4)]
        dtb = ROWB.ap[:, l * 160 + d * 32:l * 160 + d * 32 + 32]
        aneg = ANEG.ap[:, l * 64 + d * 32:l * 64 + d * 32 + 32]
        dsk = ROWB.ap[:, l * 160 + 128:l * 160 + 160]

        P.add("dve", lambda h: h.memset(H32s.ap, 0.0), [], HG)
        P.add("dve", lambda h: h.memset(HBF.ap, 0.0), [], HBG)
        if fwd:
            for j in range(24):
                for k in range(KS):
                    ts("dve" if (j + k) % 2 else "pool", DG.ap[:, j, k, :], IDENT, vcol(vb + V_SCW + k * 24 + j), ALU.mult,
                       [CONST, VEC], [DG])
        tiles = list(range(NT)) if fwd else list(range(NT - 1, -1, -1))
        chunks = [0, 1, 2, 3] if fwd else [3, 2, 1, 0]
        seq = [(ti, i, q) for ti, i in enumerate(tiles) for q in chunks]
        ctxs = {}

        def tile_load(ti, i):
            t0 = i * 512
            XS = XS2[ti % 2]
            DTP, DA, DAH, DAL = DTP2[ti % 2], DA2[ti % 2], DAH2[ti % 2], DAL2[ti % 2]
            dma("sp", DTT.ap, DT_d.rearrange("(n p) c -> p n c", p=128)[:, i * 4:(i + 1) * 4, :], [], [DTT.buf], "dtt")
            if fwd:
                lo, hi = t0 - 2, t0 + 514
                clo, chi = max(lo, 0), min(hi, ntok)
                if clo > lo:
                    memset("dve", XP, XP.ap[:, :, 0:clo - lo], 0.0)
                if chi < hi:
                    memset("dve", XP, XP.ap[:, :, 516 - (hi - chi):516], 0.0)
                dma("sp", XP.ap[:, :, clo - lo:chi - lo], fm(XBCP_d)[:, :, clo:chi], [], [XP.buf], "xp", nsplit=3)
                if t0 + 512 == link:
                    ts("dve", XP.ap[:, :, 514:516], XP.ap[:, :, 514:516], FLAG.ap[:, 0:1], ALU.mult, [XP, FLAG], [XP])
            else:
                dma("sp", XS.ap, fm(XSBC_d)[:, :, t0:t0 + 512], [], [XS.buf], ("xs", ti % 2), nsplit=3)
            tt("dve", DTX.ap, DTT.ap[:, :, d * 32:d * 32 + 32], dtb.unsqueeze(1).broadcast_to([128, 4, 32]), ALU.add,
               [DTT, ROWB], [DTX])
            act(DTX.ap, DTX.ap, AF.Exp, [DTX], [DTX])
            act(DTP.ap, DTX.ap, AF.Ln, [DTX], [DTP], bias=1.0)
            tt("dve", DA.ap, DTP.ap, aneg.unsqueeze(1).broadcast_to([128, 4, 32]), ALU.mult, [DTP, ANEG], [DA])
            cp("dve", DAH.ap, DA.ap, [DA], [DAH])
            tt("dve", DAL.ap, DA.ap, DAH.ap, ALU.subtract, [DA, DAH], [DAL])
            if fwd:
                for j in range(24):
                    bk = rb()
                    for k in range(KS):
                        mm(bk, bk.ap, DG.ap[:, j, k, :], XP.ap[:, j, k:k + 512], k == 0, k == KS - 1, [DG, XP])
                    act(XS.ap[:, j, :], bk.ap, AF.Silu, [bk, VEC], [XS], bias=vcol(vb + V_SCB + j))
                dma("pool", fm(XSBC_d)[:, :, t0:t0 + 512], XS.ap, [XS.buf], [], ("xsst", ti % 2), nsplit=3)

        def prep(n):
            ti, i, q = seq[n]
            c = {}
            XS = XS2[ti % 2]
            DTP, DA = DTP2[ti % 2], DA2[ti % 2]
            tq = slice(q * 128, (q + 1) * 128)
            tok0 = i * 512 + q * 128
            c.update(XS=XS, DAH=DAH2[ti % 2], DAL=DAL2[ti % 2], q=q, tq=tq, tok0=tok0, ti=ti, i=i)
            if not fwd:
                yfw, semy = YFW.next()
                dma("sp", yfw.ap, YF_d[tok0:tok0 + 128, :], [], [yfw.buf], semy)
                c["yfw"] = yfw
            sm, _ = SMALL.next()
            c["sm"] = sm
            NACS, EACS, ETOT, DEC, TMPS = [sm.ap[:, k_, :] for k_ in range(5)]
            bs = rb()
            mm(bs, bs.ap[:, 0:32], TRI, DA.ap[:, q, :], True, True, [CONST, DA])
            mm(bs, bs.ap[:, 32:64], ONES, DA.ap[:, q, :], True, True, [CONST, DA])
            ts("dve", NACS, bs.ap[:, 0:32], -1.0, ALU.mult, [bs], [sm])
            act(EACS, bs.ap[:, 0:32], AF.Exp, [bs], [sm])
            act(ETOT, bs.ap[:, 32:64], AF.Exp, [bs], [sm])
            tt("dve", TMPS, bs.ap[:, 32:64], NACS, ALU.add, [bs, sm], [sm])
            act(DEC, TMPS, AF.Exp, [sm], [sm])
            dtdec, _ = DTDEC.next()
            tt("dve", dtdec.ap, DTP.ap[:, q, :], DEC, ALU.mult, [DTP, sm], [dtdec])
            xsdt, _ = XSDT.next()
            c["xsdt"] = xsdt
            xsd, _ = XSD.next()
            c["xsd"] = xsd
            for half in range(2):
                bk = rb()
                bkb = bk.ap.bitcast(BF16)
                for jj in range(8):
                    tr(bk, bkb[:, jj * 128:(jj + 1) * 128], XS.ap[:, half * 8 + jj, tq], IDENTB, [XS, CBF])
                src = bkb.rearrange("p (h e) -> p h e", e=64)
                tt("dve", xsdt.ap[:, half * 1024:(half + 1) * 1024].rearrange("p (h e) -> p h e", e=64), src,
                   bc(DTP.ap[:, q, half * 16:(half + 1) * 16], 64), ALU.mult, [bk, DTP], [xsdt])
                tt("dve", xsd.ap[:, half * 1024:(half + 1) * 1024].rearrange("p (h e) -> p h e", e=64), src,
                   bc(dtdec.ap[:, half * 16:(half + 1) * 16], 64), ALU.mult, [bk, dtdec], [xsd])
            btok, _ = BTOK.next()
            c["btok"] = btok
            bk = rb()
            bkb = bk.ap.bitcast(BF16)
            for g in range(4):
                tr(bk, bkb[:, g * 128:(g + 1) * 128], XS.ap[:, 16 + g, tq], IDENTB, [XS, CBF])
            cp("act", btok.ap, bkb[:, 0:512].rearrange("p (g n) -> p g n", n=128), [bk], [btok])
            bcb = rb()
            for g in range(4):
                mm(bcb, bcb.ap[:, g * 128:(g + 1) * 128], XS.ap[:, 16 + g, tq], XS.ap[:, 20 + g, tq], True, True, [XS])
            cbt, _ = CBTS.next()
            cp("act", cbt.ap, bcb.ap, [bcb], [cbt])
            c["cbt"] = cbt
            c["g"] = {}
            ctxs[n] = c

        def abc(n, s):
            c = ctxs[n]
            q, sm, xsdt = c["q"], c["sm"], c["xsdt"]
            NACS, EACS, ETOT, DEC, TMPS = [sm.ap[:, k_, :] for k_ in range(5)]
            if s % 2 == 0:
                g = s // 2
                gs = slice(g * 512, (g + 1) * 512)
                ht, _ = HT.next()
                P.add("dve", (lambda h, o=ht.ap.rearrange("p (h e) -> p h e", e=64),
                              i0=H32s.ap[:, gs].rearrange("p (h e) -> p h e", e=64),
                              i1=bc(ETOT[:, g * 8:(g + 1) * 8], 64): h.tensor_tensor(out=o, in0=i0, in1=i1, op=ALU.mult)),
                      [HG[g], sm.buf], [ht.buf])
                c["ht%d" % g] = ht
            e_, _ = ERING.next()
            g_, _ = GRING.next()
            g = s // 2
            ba = BK_A[(n * 8 + s) % 4]
            for hq in range(4):
                h_ = s * 4 + hq
                o_ = ba.ap[:, hq * 128:(hq + 1) * 128]
                P.add("pe", (lambda h, o=o_, l_=c["DAH"].ap[:, q, h_:h_ + 1].broadcast_to([128, 128]):
                             h.matmul(o, l_, TRIB, start=True, stop=False)), [c["DAH"].buf, CBF.buf], [ba.buf])
                P.add("pe", (lambda h, o=o_, l_=c["DAL"].ap[:, q, h_:h_ + 1].broadcast_to([128, 128]):
                             h.matmul(o, l_, TRIB, start=False, stop=False)), [c["DAL"].buf, CBF.buf], [ba.buf])
                P.add("pe", (lambda h, o=o_: h.matmul(o, IDENTB, NEGM, start=False, stop=True)), [CBF.buf], [ba.buf])
            for hq in range(4):
                h_ = s * 4 + hq
                P.add("act", (lambda h, o=e_.ap[:, hq, :], i_=ba.ap[:, hq * 128:(hq + 1) * 128], b_=NACS[:, h_:h_ + 1]:
                              h.activation(out=o, in_=i_, func=AF.Exp, bias=b_)), [ba.buf, sm.buf], [e_.buf])
            tt("dve", g_.ap, e_.ap, c["cbt"].ap[:, g * 128:(g + 1) * 128].unsqueeze(1).broadcast_to([128, 4, 128]), ALU.mult,
               [e_, c["cbt"]], [g_])
            c["g"][s] = g_

        def y1(n, s):
            c = ctxs[n]
            xsdt = c["xsdt"]
            g = s // 2
            by1 = BK_Y1[(n * 4 + g) % 2]
            if (not fwd) and s % 2 == 0:
                mm(by1, by1.ap, IDENT, c["yfw"].ap[:, g * 512:(g + 1) * 512], True, False, [CONST, c["yfw"]])
            for hq in range(4):
                h_ = s * 4 + hq
                hh = h_ % 8
                g_ = c["g"][s]
                mm(by1, by1.ap[:, hh * 64:(hh + 1) * 64], g_.ap[:, hq, :], xsdt.ap[:, h_ * 64:(h_ + 1) * 64], fwd,
                   fwd or (s % 2 == 1 and hq == 3), [g_, xsdt])
            del c["g"][s]

        def epiB(n, g):
            c = ctxs[n]
            sm, tok0 = c["sm"], c["tok0"]
            EACS = sm.ap[:, 1, :]
            gs = slice(g * 512, (g + 1) * 512)
            by1 = BK_Y1[(n * 4 + g) % 2]
            BK_Y2, BK_ST = rb(), rb()
            P.add("pe", (lambda h, o=BK_Y2.ap, l_=c["XS"].ap[:, 20 + g, c["tq"]], r_=HBF.ap[:, gs]:
                         h.matmul(o, l_, r_, start=True, stop=True)), [c["XS"].buf, HBG[g]], [BK_Y2.buf])
            xsd = c["xsd"]
            mm(BK_ST, BK_ST.ap, c["btok"].ap[:, g, :], xsd.ap[:, gs], True, True, [c["btok"], xsd])
            t1, _ = T1R.next()
            tt("dve", t1.ap.rearrange("p (h e) -> p h e", e=64), BK_Y2.ap.rearrange("p (h e) -> p h e", e=64),
               bc(EACS[:, g * 8:(g + 1) * 8], 64), ALU.mult, [BK_Y2, sm], [t1])
            t2, semt2 = T2R.next()
            tt("dve", t2.ap, t1.ap, by1.ap, ALU.add, [t1, by1], [t2])
            ht = c["ht%d" % g]
            P.add("dve", (lambda h, o=H32s.ap[:, gs], i0=ht.ap, i1=BK_ST.ap: h.tensor_tensor(out=o, in0=i0, in1=i1, op=ALU.add)),
                  [ht.buf, BK_ST.buf], [HG[g]])
            if (not fwd) and tok0 == link:
                P.add("dve", (lambda h, o=H32s.ap[:, gs]: h.tensor_scalar(out=o, in0=o, scalar1=FLAG.ap[:, 0:1], scalar2=None,
                                                                        op0=ALU.mult)), [HG[g], FLAG.buf], [HG[g]])
            if fwd:
                dma("sp", YF_d[tok0:tok0 + 128, gs], t2.ap, [t2.buf], [], semt2)
            else:
                c["t2_%d" % g] = t2

        def epiC(n, g):
            c = ctxs[n]
            gs = slice(g * 512, (g + 1) * 512)
            P.add("act", (lambda h, o=HBF.ap[:, gs], i_=H32s.ap[:, gs]: h.activation(out=o, in_=i_, func=AF.Copy)),
                  [HG[g]], [HBG[g]])
            if not fwd:
                YFM = YFM2[c["ti"] % 2]
                t2 = c["t2_%d" % g]
                bt = rb()
                for fb in range(4):
                    tr(bt, bt.ap[:, fb * 128:(fb + 1) * 128], t2.ap[:, fb * 128:(fb + 1) * 128], IDENT, [t2, CONST])
                cp("act", YFM.ap[:, g * 4:(g + 1) * 4, c["tq"]], bt.ap.rearrange("p (f t) -> p f t", t=128), [bt], [YFM])
            if g == 3:
                if (not fwd) and c["q"] == chunks[-1]:
                    t0 = c["i"] * 512
                    dma("sp", fm(Y_d)[:, :, t0:t0 + 512], YFM2[c["ti"] % 2].ap, [YFM2[c["ti"] % 2].buf], [],
                        ("yfmst", c["ti"] % 2), nsplit=2)

        tile_load(0, tiles[0])
        prep(0)
        flat = [(n, s) for n in range(len(seq)) for s in range(8)]
        pend = []

        def run_due(idx):
            keep = []
            for due, fn in pend:
                if due <= idx:
                    fn()
                else:
                    keep.append((due, fn))
            pend[:] = keep

        LAG = 3
        for idx, (n, s) in enumerate(flat):
            abc(n, s)
            run_due(idx)
            pend.append((idx + LAG, (lambda n=n, s=s: y1(n, s))))
            if s % 2 == 1:
                g = s // 2
                pend.append((idx + LAG, (lambda n=n, g=g: epiB(n, g))))
                pend.append((idx + LAG + 1, (lambda n=n, g=g: epiC(n, g))))
            if s == 3 and n + 1 < len(seq):
                if seq[n + 1][0] != seq[n][0]:
                    tile_load(seq[n + 1][0], seq[n + 1][1])
                prep(n + 1)
        for k in range(len(flat), len(flat) + 6):
            run_due(k)
        assert not pend
        P.barrier()

    def sweep_C(l):
        arena.reset()
        last = l == depth - 1
        vb = V_L0 + l * V_LSZ
        cm = common(arena, nh=1)
        BIGT = arena.take("bigt", [128, 16, 512], F32)
        CV = arena.take("cv", [128, 8, 512], F32)
        RSTD2 = arena.take("rstd2", [128, 512], F32)
        TMP02 = arena.take("tmp02", [128, 512], F32)
        VPT = arena.take("vpt", [128, 8, 542], BF16)
        LY = Ring(arena, "ly", 2, [128, 512], F32)
        LZ = Ring(arena, "lz", 2, [128, 512], F32)
        LX = Ring(arena, "lx", 2, [128, 512], BF16)
        LG = Ring(arena, "lg", 3, [128, 512], F32)
        U, HID = cm["U"], cm["HID"]
        hid_flat = HID.ap.rearrange("p a b -> p (a b)")
        YN = V(hid_flat[:, 0:16 * 512].rearrange("p (a b) -> p a b", b=512), HID.buf)
        MB = V(hid_flat[:, 0:8 * 512].rearrange("p (a b) -> p a b", b=512), HID.buf)
        GY = BIGT
        M1 = V(BIGT.ap[:, 8:16, :], BIGT.buf)
        ws = WStream(cm["wslots"], "wC")
        for i in range(NT):
            s = []
            for fc in range(8):
                s.append(slab_diag(l, fc))
            for m0 in range(0, 8, 2):
                s.append(slab_cols("pa", l, m0 * 128, 256, 16))
            for q in range(2):
                s.append(slab_cols("pb", l, q * 512, 512, 8))
            for q in range(2):
                s.append(slab_cols("wo", l, q * 512, 512, 8))
            s += ffn_sched("f2u", "f2d", l)
            ws.extend(s)

        for i in range(NT):
            t0 = i * 512
            Hs = cm["H32"][0]
            dma("sp", Hs.ap, fm(H1_d)[:, :, t0:t0 + 512], [], [Hs.buf], ("h32", 0), nsplit=2)
            lo, hi = t0 - 15, t0 + 527
            clo, chi = max(lo, 0), min(hi, ntok)
            if clo > lo:
                memset("dve", VPT, VPT.ap[:, :, 0:clo - lo], 0.0)
            if chi < hi:
                memset("dve", VPT, VPT.ap[:, :, 542 - (hi - chi):542], 0.0)
            dma("sp", VPT.ap[:, :, clo - lo:chi - lo], fm(VP_d)[:, :, clo:chi], [], [VPT.buf], "vpt")
            if t0 + 512 == link:
                ts("dve", VPT.ap[:, :, 527:542], VPT.ap[:, :, 527:542], FLAG.ap[:, 0:1], ALU.mult, [VPT, FLAG], [VPT])
            for f2 in range(16):
                ly, sy = LY.next()
                lz, sz = LZ.next()
                dma("sp", ly.ap, fm(Y_d)[:, f2, t0:t0 + 512], [], [ly.buf], sy)
                dma("sp", lz.ap, fm(ZS_d)[:, f2, t0:t0 + 512], [], [lz.buf], sz)
                lx, sx = LX.next()
                dma("sp", lx.ap, fm(XSBC_d)[:, f2, t0:t0 + 512], [], [lx.buf], sx)
                stt(ly.ap, lx.ap, vcol(vb + V_DCOL + f2), ALU.mult, ly.ap, ALU.add, [lx, ly, VEC], [ly])
                tt("dve", GY.ap[:, f2, :], ly.ap, lz.ap, ALU.mult, [ly, lz], [GY])
            for fc in range(8):
                slot = ws.next()
                wv = slot.ap.bitcast(BF16)[:, 0:KCF * 128].rearrange("p (k j) -> p k j", j=128)
                bk = nextbank()
                for k in range(KCF):
                    mm(bk, bk.ap, wv[:, k, :], VPT.ap[:, fc, k:k + 512], k == 0, k == KCF - 1, [slot, VPT])
                act(CV.ap[:, fc, :], bk.ap, AF.Identity, [bk, VEC], [CV], bias=vcol(vb + V_CCB + fc))
            bk = nextbank()
            for fc in range(16):
                sq, _ = cm["sqb"].next()
                act(sq.ap, GY.ap[:, fc, :], AF.Square, [GY], [sq])
                mm(bk, bk.ap, ONESB, sq.ap, fc == 0, fc == 15, [sq, CBF])
            rstd_from(bk, cm["rstd"], cm["tmp0"], DI)
            for fc in range(16):
                stt(YN.ap[:, fc, :], GY.ap[:, fc, :], vcol(vb + V_SNRM + fc), ALU.mult, bk.ap, ALU.mult,
                    [GY, bk, VEC], [YN])
            b1, b2 = nextbank(), nextbank()
            for fc in range(8):
                mm(b1, b1.ap, ONES, CV.ap[:, fc, :], fc == 0, fc == 7, [CV, CONST])
            for fc in range(8):
                sq, _ = cm["sqb"].next()
                act(sq.ap, CV.ap[:, fc, :], AF.Square, [CV], [sq])
                mm(b2, b2.ap, ONESB, sq.ap, fc == 0, fc == 7, [sq, CBF])
            MEAN, _ = cm["tmpn"].next()
            act(MEAN.ap, b1.ap, AF.Copy, [b1], [MEAN], scale=1.0 / D)
            MSQ, _ = cm["tmpn"].next()
            tt("dve", MSQ.ap, MEAN.ap, MEAN.ap, ALU.mult, [MEAN], [MSQ])
            stt(MSQ.ap, b2.ap, 1.0 / D, ALU.mult, MSQ.ap, ALU.subtract, [b2, MSQ], [MSQ])
            act(TMP02.ap, MSQ.ap, AF.Ln, [MSQ], [TMP02], bias=EPS)
            act(RSTD2.ap, TMP02.ap, AF.Exp, [TMP02], [RSTD2], scale=-0.5)
            for m0 in range(0, 8, 2):
                slot = ws.next()
                wv = wview(slot, 16, 256)
                for mm_ in range(2):
                    mc = m0 + mm_
                    bk = nextbank()
                    for kc in range(16):
                        mm(bk, bk.ap, wv[:, kc, mm_ * 128:(mm_ + 1) * 128], YN.ap[:, kc, :], kc == 0, kc == 15, [slot, YN])
                    lg, sg = LG.next()
                    dma("sp", lg.ap, fm(GA_d)[:, mc, t0:t0 + 512], [], [lg.buf], sg)
                    tt("dve", M1.ap[:, mc, :], lg.ap, bk.ap, ALU.mult, [lg, bk], [M1])
            for fc in range(8):
                t, _ = cm["silu"].next()
                tt("dve", t.ap, CV.ap[:, fc, :], MEAN.ap, ALU.subtract, [CV, MEAN], [t])
                stt(t.ap, t.ap, vcol(vb + V_LNG + fc), ALU.mult, RSTD2.ap, ALU.mult, [t, RSTD2, VEC], [t])
                act(U.ap[:, fc, :], t.ap, AF.Silu, [t, VEC], [U], bias=vcol(vb + V_LNB + fc))
            for q in range(2):
                slot = ws.next()
                wv = wview(slot, 8, 512)
                for mm_ in range(4):
                    mc = q * 4 + mm_
                    bk = nextbank()
                    for kc in range(8):
                        mm(bk, bk.ap, wv[:, kc, mm_ * 128:(mm_ + 1) * 128], U.ap[:, kc, :], kc == 0, kc == 7, [slot, U])
                    lg, sg = LG.next()
                    dma("sp", lg.ap, fm(GB_d)[:, mc, t0:t0 + 512], [], [lg.buf], sg)
                    t, _ = cm["silu"].next()
                    tt("dve", t.ap, lg.ap, bk.ap, ALU.mult, [lg, bk], [t])
                    tt("pool", MB.ap[:, mc, :], t.ap, M1.ap[:, mc, :], ALU.add, [t, M1], [MB])
            for q in range(2):
                slot = ws.next()
                wv = wview(slot, 8, 512)
                for mm_ in range(4):
                    mc = q * 4 + mm_
                    bk = nextbank()
                    for kc in range(8):
                        mm(bk, bk.ap, wv[:, kc, mm_ * 128:(mm_ + 1) * 128], MB.ap[:, kc, :], kc == 0, kc == 7, [slot, MB])
                    stt(Hs.ap[:, mc, :], bk.ap, dcol(l, 1, 2, mc), ALU.mult, Hs.ap[:, mc, :], ALU.add, [bk, Hs, DER], [Hs])
            norm_mod(Hs, U, l, 2, cm)
            ffn(Hs, U, HID, l, 2, ws, cm)
            if not last:
                dma("pool", fm(HN_d)[:, :, t0:t0 + 512], Hs.ap, [Hs.buf], [], ("hnst", i % 2), nsplit=2)
            else:
                bk = nextbank()
                for fc in range(8):
                    sq, _ = cm["sqb"].next()
                    act(sq.ap, Hs.ap[:, fc, :], AF.Square, [Hs], [sq])
                    mm(bk, bk.ap, ONESB, sq.ap, fc == 0, fc == 7, [sq, CBF])
                rstd_from(bk, cm["rstd"], cm["tmp0"], D)
                OUTF = CV
                for fc in range(8):
                    stt(OUTF.ap[:, fc, :], Hs.ap[:, fc, :], vcol(V_FN + fc), ALU.mult, bk.ap, ALU.mult,
                        [Hs, bk, VEC], [OUTF])
                OT = V(BIGT.ap[:, 8:16, :].rearrange("p a b -> p (a b)").rearrange("p (q f) -> p q f", f=1024), BIGT.buf)
                for q in range(4):
                    for half in range(2):
                        bk = nextbank()
                        for ff in range(4):
                            fc = half * 4 + ff
                            tr(bk, bk.ap[:, ff * 128:(ff + 1) * 128], OUTF.ap[:, fc, q * 128:(q + 1) * 128], IDENT,
                               [OUTF, CONST])
                        cp("act" if half else "dve", OT.ap[:, q, half * 512:(half + 1) * 512], bk.ap, [bk], [OT])
                dma("pool", y_d.rearrange("(n p) f -> p n f", p=128)[:, i * 4:(i + 1) * 4, :], OT.ap, [OT.buf], [], "yout")
        P.barrier()

    for l in range(depth):
        sweep_A(l)
        sweep_S(l, 0)
        sweep_S(l, 1)
        sweep_C(l)
    P.barrier()
    for e in ENGS:
        P.add(e, lambda h: h.nop())
    with nc.Block() as block:
        P.emit(nc, block)
    return nc, P


def make_consts():
    c = np.zeros((128, 768), np.float32)
    c[:, 0:128] = np.eye(128, dtype=np.float32)
    c[:, 128:256] = np.triu(np.ones((128, 128), np.float32))
    c[:, 256:384] = np.tril(np.ones((128, 128), np.float32))
    c[:, 384:512] = -30000.0 * np.tril(np.ones((128, 128), np.float32), -1)
    c[:, 512:640] = -30000.0 * np.triu(np.ones((128, 128), np.float32), 1)
    c[:, 640:768] = 1.0
    return c


def make_vecs(c, inp):
    rows = np.zeros((V_ROWS, 128), np.float32)
    rows[V_C:V_C + 8] = np.asarray(c, np.float32).reshape(8, 128)
    rows[V_FN:V_FN + 8] = np.asarray(inp["final_norm"], np.float32).reshape(8, 128)
    for l in range(DEPTH):
        b = V_L0 + l * V_LSZ
        rows[b + V_BADA:b + V_BADA + 72] = inp["b_ada"][l].reshape(72, 128)
        rows[b + V_N1:b + V_N1 + 8] = inp["ffn1_norm"][l].reshape(8, 128)
        rows[b + V_N2:b + V_N2 + 8] = inp["mix_norm"][l].reshape(8, 128)
        rows[b + V_N3:b + V_N3 + 8] = inp["ffn2_norm"][l].reshape(8, 128)
        rows[b + V_SCW:b + V_SCW + 120] = inp["ssm_conv_w"][l].reshape(KS * 24, 128)
        rows[b + V_SCB:b + V_SCB + 24] = inp["ssm_conv_b"][l].reshape(24, 128)
        rows[b + V_SNRM:b + V_SNRM + 16] = inp["ssm_norm"][l].reshape(16, 128)
        rows[b + V_CCW:b + V_CCW + 248] = inp["conf_conv_w"][l].reshape(KCF * 8, 128)
        rows[b + V_CCB:b + V_CCB + 8] = inp["conf_conv_b"][l].reshape(8, 128)
        rows[b + V_LNG:b + V_LNG + 8] = inp["conf_ln_g"][l].reshape(8, 128)
        rows[b + V_LNB:b + V_LNB + 8] = inp["conf_ln_b"][l].reshape(8, 128)
        rows[b + V_DCOL:b + V_DCOL + 16] = np.repeat(np.asarray(inp["d_skip"][l], np.float32), HP).reshape(16, 128)
    return rows


def make_rowb(inp):
    r = np.zeros((DEPTH, 160), np.float32)
    for l in range(DEPTH):
        r[l, 0:64] = inp["dt_bias"][l].reshape(64)
        r[l, 64:128] = inp["a_log"][l].reshape(64)
        r[l, 128:160] = inp["d_skip"][l].reshape(32)
    return np.ascontiguousarray(np.broadcast_to(r.reshape(1, DEPTH * 160), (128, DEPTH * 160)))


_CACHE = {}


def kernel(**inputs):
    inp = {k: np.asarray(v) for k, v in inputs.items()}
    if "nc" not in _CACHE:
        _CACHE["nc"] = build()[0]
    nc = _CACHE["nc"]
    xp, xs = inp["x_prompt"], inp["x_sample"]
    consts = make_consts()
    rowb = make_rowb(inp)
    shared = {"consts": consts, "rowb": rowb, "w_ada": inp["w_ada"],
              "ffn1_up": inp["ffn1_up"], "ffn1_down": inp["ffn1_down"], "w_in": inp["w_in"],
              "w_proj_ssd": inp["w_proj_ssd"], "w_proj_conv": inp["w_proj_conv"], "w_out": inp["w_out"],
              "ffn2_up": inp["ffn2_up"], "ffn2_down": inp["ffn2_down"]}
    in_maps = []
    for core in range(NCORES):
        m = dict(shared)
        if core < 4:
            m["x"] = np.ascontiguousarray(xp[core])
            m["vecs"] = make_vecs(inp["c_prompt"][core], inp)
            m["flag"] = np.ones((128, 1), np.float32)
        else:
            b = core - 4
            xx = np.zeros((NTOK_FULL, D), np.float32)
            xx[:LINK_FULL] = xs[b]
            m["x"] = xx
            m["vecs"] = make_vecs(inp["c_sample"][b], inp)
            m["flag"] = np.zeros((128, 1), np.float32)
        in_maps.append(m)
    res = run_bass_kernel_spmd(nc, in_maps, core_ids=list(range(NCORES)))
    y_prompt = np.stack([np.asarray(res.results[c]["y"], np.float32) for c in range(4)], axis=0)
    y_sample = np.stack([np.asarray(res.results[4 + b]["y"], np.float32)[:LINK_FULL] for b in range(4)], axis=0)
    return (y_prompt, y_sample)
```

```python
import numpy as np
import concourse.bass as bass
import concourse.mybir as mybir
from concourse.bass_utils import run_bass_kernel_spmd

F32 = mybir.dt.float32
BF16 = mybir.dt.bfloat16
U8 = mybir.dt.uint8
AF = mybir.ActivationFunctionType
ALU = mybir.AluOpType

D = 1024
DI = 2048
NH = 32
HP = 64
NGRP = 4
NS = 128
KS = 5
CD = 3072
KCF = 31
FF = 2816
IN_COLS = 9280
OFF_XBC = 2048
OFF_DT = 5120
OFF_GLU = 5184
OFF_GATE = 7232
DEPTH = 2
EPS = 1e-6
NCORES = 8
NTOK_FULL = 8192
LINK_FULL = 4096

V_C = 0
V_FN = 8
V_L0 = 16
V_LSZ = 528
V_BADA, V_N1, V_N2, V_N3, V_SCW, V_SCB, V_SNRM, V_CCW, V_CCB, V_LNG, V_LNB = 0, 72, 80, 88, 96, 216, 240, 256, 504, 512, 520
V_ROWS = 1152

ENGS = ("pe", "act", "dve", "pool", "sp")


class Buf:
    __slots__ = ("name", "last_w", "readers")

    def __init__(self, name):
        self.name = name
        self.last_w = None
        self.readers = {}


class V:
    __slots__ = ("ap", "buf")

    def __init__(self, ap, buf):
        self.ap = ap
        self.buf = buf


class Op:
    __slots__ = ("eng", "fn", "deps", "signals", "idx", "is_dma", "sem", "val", "ninc")

    def __init__(self, eng, fn, is_dma, ninc):
        self.eng = eng
        self.fn = fn
        self.deps = None
        self.signals = False
        self.idx = 0
        self.is_dma = is_dma
        self.sem = None
        self.val = 0
        self.ninc = ninc


class Prog:
    def __init__(self):
        self.ops = {e: [] for e in ENGS}
        self.pending = {e: [] for e in ENGS}
        self.dma_sem_vals = {}
        self.last_dma = {}
        self.nops = 0

    def add(self, eng, fn, reads=(), writes=(), dma_sem=None, ninc=1):
        is_dma = dma_sem is not None
        op = Op(eng, fn, is_dma, ninc)
        deps = {}

        def need(d, raw=False):
            if d is None:
                return
            if d.is_dma:
                k = ("d", d.sem)
                if k not in deps or deps[k].val < d.val:
                    deps[k] = d
            else:
                if d.eng == eng and (not is_dma) and (eng == "pe" or not raw):
                    return
                k = ("e", d.eng)
                if k not in deps or deps[k].idx < d.idx:
                    deps[k] = d

        for b in reads:
            need(b.last_w, True)
        for b in writes:
            need(b.last_w)
            for r in b.readers.values():
                need(r)
        for d in self.pending[eng]:
            need(d)
        self.pending[eng] = []
        if is_dma:
            need(self.last_dma.get(dma_sem))
            v = self.dma_sem_vals.get(dma_sem, 0) + 16 * ninc
            self.dma_sem_vals[dma_sem] = v
            op.sem = dma_sem
            op.val = v
            self.last_dma[dma_sem] = op
        op.deps = list(deps.values())
        op.idx = len(self.ops[eng])
        self.ops[eng].append(op)
        for b in reads:
            b.readers[id(op) if is_dma else eng] = op
        for b in writes:
            b.last_w = op
            b.readers = {}
        self.nops += 1
        return op

    def barrier(self):
        evs = []
        for e in ENGS:
            for o in reversed(self.ops[e]):
                if not o.is_dma:
                    evs.append(o)
                    break
        for o in self.last_dma.values():
            evs.append(o)
        for e in ENGS:
            self.pending[e] = list(evs)

    def emit(self, nc, block):
        for e in ENGS:
            for op in self.ops[e]:
                for d in op.deps:
                    if not d.is_dma:
                        d.signals = True
        for e in ENGS:
            c = 0
            for op in self.ops[e]:
                if not op.is_dma and op.signals:
                    c += 1
                    op.val = c
        esem = {e: nc.alloc_semaphore("sem_" + e) for e in ENGS}
        dsem = {k: nc.alloc_semaphore("dsem_%d" % i) for i, k in enumerate(self.dma_sem_vals)}

        def make(e):
            ops = self.ops[e]

            def body(h):
                seen = {}
                for op in ops:
                    for d in op.deps:
                        if d.is_dma:
                            key, sem, val = ("d", d.sem), dsem[d.sem], d.val
                        else:
                            key, sem, val = ("e", d.eng), esem[d.eng], d.val
                        if seen.get(key, 0) >= val:
                            continue
                        seen[key] = val
                        h.wait_ge(sem, val)
                    ins = op.fn(h)
                    if op.is_dma:
                        if not isinstance(ins, (list, tuple)):
                            ins = [ins]
                        assert len(ins) == op.ninc, (len(ins), op.ninc)
                        for i_ in ins:
                            i_.then_inc(dsem[op.sem], 16)
                    elif op.signals:
                        ins.then_inc(esem[e], 1)
            return body

        block.tensor(make("pe"))
        block.scalar(make("act"))
        block.vector(make("dve"))
        block.gpsimd(make("pool"))
        block.sync(make("sp"))


class Arena:
    def __init__(self, big, lo, hi):
        self.big, self.lo, self.hi, self.off = big, lo, hi, lo

    def reset(self):
        self.off = self.lo

    def take(self, name, shape, dtype, buf=None):
        esz = 4 if dtype == F32 else (1 if dtype == U8 else 2)
        n = 1
        for s in shape[1:]:
            n *= s
        nb = n * esz
        off = (self.off + 63) // 64 * 64
        assert off + nb <= self.hi, ("SBUF arena overflow", name, off + nb, self.hi)
        self.off = off + nb
        ap = self.big[:, off:off + nb].bitcast(dtype)
        if len(shape) == 3:
            ap = ap.rearrange("p (a b) -> p a b", b=shape[2])
        elif len(shape) == 4:
            ap = ap.rearrange("p (a b c) -> p a b c", b=shape[2], c=shape[3])
        return V(ap, buf if buf is not None else Buf(name))


def bc(ap, n):
    return ap.unsqueeze(len(ap.shape)).broadcast_to(list(ap.shape) + [n])


def build(ntok=NTOK_FULL, link=LINK_FULL, depth=DEPTH, debug=False):
    assert ntok % 512 == 0 and link % 512 == 0
    NT = ntok // 512
    nc = bass.Bass("TRN2", target_bir_lowering=False)
    P = Prog()

    def din(name, shape, dt=F32):
        return nc.dram_tensor(name, list(shape), dt, kind="ExternalInput").ap()

    def dscr(name, shape, dt):
        kind = "ExternalOutput" if (debug and name in debug) else "Internal"
        return nc.dram_tensor(name, list(shape), dt, kind=kind).ap()

    x_d = din("x", [ntok, D])
    flag_d = din("flag", [128, 1])
    vecs_d = din("vecs", [V_ROWS, 128])
    consts_d = din("consts", [128, 768])
    rowb_d = din("rowb", [128, DEPTH * 160])
    w_ada_d = din("w_ada", [DEPTH, D, 9 * D])
    wsrc = {
        "f1u": din("ffn1_up", [DEPTH, D, 2 * FF]),
        "f1d": din("ffn1_down", [DEPTH, FF, D]),
        "win": din("w_in", [DEPTH, D, IN_COLS]),
        "pa": din("w_proj_ssd", [DEPTH, DI, D]),
        "pb": din("w_proj_conv", [DEPTH, D, D]),
        "wo": din("w_out", [DEPTH, D, D]),
        "f2u": din("ffn2_up", [DEPTH, D, 2 * FF]),
        "f2d": din("ffn2_down", [DEPTH, FF, D]),
    }
    y_d = nc.dram_tensor("y", [ntok, D], F32, kind="ExternalOutput").ap()

    wbf = {k: dscr("wb_" + k, list(v.shape), BF16) for k, v in wsrc.items()}
    diagc_d = dscr("diagc", [DEPTH, 8, 128, KCF * 128], BF16)
    H1_d = dscr("H1", [D, ntok], F32)
    HN_d = dscr("HN", [D, ntok], F32)
    ZS_d = dscr("ZS", [DI, ntok], F32)
    XBCP_d = dscr("XBCP", [CD, ntok], BF16)
    XSBC_d = dscr("XSBC", [CD, ntok], BF16)
    DT_d = dscr("DT", [ntok, 64], F32)
    VP_d = dscr("VP", [D, ntok], BF16)
    GA_d = dscr("GA", [D, ntok], F32)
    GB_d = dscr("GB", [D, ntok], F32)
    YF_d = dscr("YF", [ntok, DI], F32)
    Y_d = dscr("Y", [DI, ntok], F32)

    def fm(ap):
        return ap.rearrange("(c p) t -> p c t", p=128)

    avail = nc.sbuf_bytes_remaining() if callable(nc.sbuf_bytes_remaining) else nc.sbuf_bytes_remaining
    SB_BYTES = (int(avail) // 64) * 64 - 256
    big = nc.alloc_sbuf_tensor("big", [128, SB_BYTES], U8)
    PS = nc.alloc_psum_tensor("ps", [128, 8, 512], F32)
    banks = [V(PS[:, i, :], Buf("bank%d" % i)) for i in range(8)]
    bank_ctr = [0]

    def nextbank():
        b = banks[bank_ctr[0] % 8]
        bank_ctr[0] += 1
        return b

    pers = Arena(big, 0, 16384)
    CONST = pers.take("const", [128, 768], F32)
    IDENT, TRIU, TRIL, NEGF32, NEGB32, ONES = [CONST.ap[:, i * 128:(i + 1) * 128] for i in range(6)]
    CBF = pers.take("constbf", [128, 768], BF16)
    IDENTB, NEGFB, NEGBB, TRIUB, TRILB, ONESB = [CBF.ap[:, i * 128:(i + 1) * 128] for i in range(6)]
    VEC = pers.take("vec", [128, V_ROWS], F32)
    ROWB = pers.take("rowb", [128, DEPTH * 160], F32)
    ANEG = pers.take("aneg", [128, DEPTH * 64], F32)
    MODS = pers.take("mods", [128, DEPTH * 72], F32)
    DER = pers.take("der", [128, DEPTH * 72], F32)
    CACT = pers.take("cact", [128, 8, 2], F32)
    FLAG = pers.take("flag", [128, 1], F32)
    arena = Arena(big, 16384, SB_BYTES)

    def vcol(r):
        return VEC.ap[:, r:r + 1]

    def dma(eng, out, in_, reads, writes, sem, nsplit=1, split_axis=1):
        if nsplit == 1:
            P.add(eng, lambda h: h.dma_start(out=out, in_=in_), reads, writes, dma_sem=sem)
            return
        n = out.shape[split_axis]
        step = (n + nsplit - 1) // nsplit
        pieces = []
        for a in range(0, n, step):
            b = min(n, a + step)
            idx = [slice(None)] * len(out.shape)
            idx[split_axis] = slice(a, b)
            pieces.append((out[tuple(idx)], in_[tuple(idx)]))

        def fn(h):
            return [h.dma_start(out=o, in_=i) for o, i in pieces]
        P.add(eng, fn, reads, writes, dma_sem=sem, ninc=len(pieces))

    def mm(outv, out_ap, lhsT, rhs, start, stop, reads):
        P.add("pe", lambda h: h.matmul(out_ap, lhsT, rhs, start=start, stop=stop),
              [r.buf for r in reads], [outv.buf])

    def tr(outv, out_ap, in_ap, ident, reads):
        P.add("pe", lambda h: h.transpose(out_ap, in_ap, ident), [r.buf for r in reads], [outv.buf])

    def act(out_ap, in_ap, func, reads, writes, bias=None, scale=None):
        kw = {}
        if bias is not None:
            kw["bias"] = bias
        if scale is not None:
            kw["scale"] = scale
        P.add("act", lambda h: h.activation(out=out_ap, in_=in_ap, func=func, **kw),
              [r.buf for r in reads], [w.buf for w in writes])

    def tt(eng, out_ap, in0, in1, op, reads, writes):
        P.add(eng, lambda h: h.tensor_tensor(out=out_ap, in0=in0, in1=in1, op=op),
              [r.buf for r in reads], [w.buf for w in writes])

    def ts(eng, out_ap, in0, s1, op0, reads, writes, s2=None, op1=None):
        if op1 is None:
            P.add(eng, lambda h: h.tensor_scalar(out=out_ap, in0=in0, scalar1=s1, scalar2=None, op0=op0),
                  [r.buf for r in reads], [w.buf for w in writes])
        else:
            P.add(eng, lambda h: h.tensor_scalar(out=out_ap, in0=in0, scalar1=s1, scalar2=s2, op0=op0, op1=op1),
                  [r.buf for r in reads], [w.buf for w in writes])

    def stt(out_ap, in0, scalar, op0, in1, op1, reads, writes):
        P.add("dve", lambda h: h.scalar_tensor_tensor(out=out_ap, in0=in0, scalar=scalar, op0=op0, in1=in1, op1=op1),
              [r.buf for r in reads], [w.buf for w in writes])

    def cp(eng, out_ap, in_ap, reads, writes):
        if eng == "act":
            act(out_ap, in_ap, AF.Copy, reads, writes)
        else:
            P.add(eng, lambda h: h.tensor_copy(out=out_ap, in_=in_ap),
                  [r.buf for r in reads], [w.buf for w in writes])

    def memset(eng, v, ap, val):
        P.add(eng, lambda h: h.memset(ap, val), [], [v.buf])

    class Ring:
        def __init__(self, ar, name, n, shape, dtype):
            self.vs = [ar.take("%s%d" % (name, i), shape, dtype) for i in range(n)]
            self.i = 0
            self.name = name

        def next(self):
            k = self.i % len(self.vs)
            self.i += 1
            return self.vs[k], (self.name, k)

    class WStream:
        def __init__(self, slots, name):
            self.slots, self.name = slots, name
            self.sched, self.loaded, self.cur = [], 0, 0

        def extend(self, items):
            self.sched.extend(items)

        def _load(self, k):
            slot = self.slots[k % len(self.slots)]
            pieces = self.sched[k](slot)

            def fn(h):
                return [h.dma_start(out=o, in_=i) for o, i in pieces]
            P.add("sp", fn, [], [slot.buf], dma_sem=(self.name, k % len(self.slots)), ninc=len(pieces))

        def next(self):
            k = self.cur
            self.cur += 1
            want = min(len(self.sched), k + len(self.slots))
            while self.loaded < want:
                self._load(self.loaded)
                self.loaded += 1
            return self.slots[k % len(self.slots)]

    arena.reset()
    dma("sp", CONST.ap, consts_d, [], [CONST.buf], "c0")
    dma("sp", ROWB.ap, rowb_d, [], [ROWB.buf], "c1")
    dma("sp", FLAG.ap, flag_d, [], [FLAG.buf], "c2")
    VST = arena.take("vst", [128, 9, 128], F32)
    dma("sp", VST.ap, vecs_d.rearrange("(b p) f -> p b f", p=128), [], [VST.buf], "c3")
    cp("dve", CBF.ap[:, 0:128], IDENT, [CONST], [CBF])
    cp("dve", CBF.ap[:, 128:256], NEGF32, [CONST], [CBF])
    cp("dve", CBF.ap[:, 256:384], NEGB32, [CONST], [CBF])
    cp("dve", CBF.ap[:, 384:512], TRIU, [CONST], [CBF])
    cp("dve", CBF.ap[:, 512:640], TRIL, [CONST], [CBF])
    cp("dve", CBF.ap[:, 640:768], ONES, [CONST], [CBF])
    for b0 in range(0, 9, 4):
        nb = min(4, 9 - b0)
        bk = nextbank()
        for b in range(nb):
            tr(bk, bk.ap[:, b * 128:(b + 1) * 128], VST.ap[:, b0 + b, :], IDENT, [VST, CONST])
        cp("dve", VEC.ap[:, b0 * 128:(b0 + nb) * 128], bk.ap[:, 0:nb * 128], [bk], [VEC])
    act(CACT.ap[:, :, 0], VEC.ap[:, V_C:V_C + 8], AF.Silu, [VEC], [CACT])
    act(CACT.ap[:, :, 1], VEC.ap[:, V_C:V_C + 8], AF.Silu, [VEC], [CACT])
    for l in range(depth):
        act(ANEG.ap[:, l * 64:(l + 1) * 64], ROWB.ap[:, l * 160 + 64:l * 160 + 128], AF.Exp, [ROWB], [ANEG])
    ts("dve", ANEG.ap, ANEG.ap, -1.0, ALU.mult, [ANEG], [ANEG])
    WA = [arena.take("wa%d" % i, [128, 8, 1024], F32) for i in range(2)]
    wai = 0
    for l in range(depth):
        bk = nextbank()
        for q in range(9):
            wa = WA[wai % 2]
            dma("sp", wa.ap, w_ada_d[l].rearrange("(k p) n -> p k n", p=128)[:, :, q * 1024:(q + 1) * 1024],
                [], [wa.buf], ("wa", wai % 2))
            wai += 1
            for mc in range(8):
                col = (q * 8 + mc) * 2
                for kc in range(8):
                    mm(bk, bk.ap[:, col:col + 2], wa.ap[:, kc, mc * 128:(mc + 1) * 128], CACT.ap[:, kc, :],
                       kc == 0, kc == 7, [wa, CACT])
        vb = V_L0 + l * V_LSZ
        tt("dve", MODS.ap[:, l * 72:(l + 1) * 72], bk.ap[:, 0:144:2], VEC.ap[:, vb + V_BADA:vb + V_BADA + 72], ALU.add,
           [bk, VEC], [MODS])
        m0 = l * 72
        for si, (nrm, half) in enumerate(((V_N1, True), (V_N2, False), (V_N3, True))):
            sh = MODS.ap[:, m0 + si * 24:m0 + si * 24 + 8]
            sc = MODS.ap[:, m0 + si * 24 + 8:m0 + si * 24 + 16]
            g = MODS.ap[:, m0 + si * 24 + 16:m0 + si * 24 + 24]
            d0 = l * 72 + si * 24
            stt(DER.ap[:, d0:d0 + 8], sc, 1.0, ALU.add, VEC.ap[:, vb + nrm:vb + nrm + 8], ALU.mult, [MODS, VEC], [DER])
            cp("dve", DER.ap[:, d0 + 8:d0 + 16], sh, [MODS], [DER])
            ts("dve", DER.ap[:, d0 + 16:d0 + 24], g, 0.5 if half else 1.0, ALU.mult, [MODS], [DER])

    def dcol(l, si, which, fc):
        c = l * 72 + si * 24 + which * 8 + fc
        return DER.ap[:, c:c + 1]

    CW = 2048
    ceng = ["dve", "act", "pool"]
    cast_cnt = [0]

    def cast_list(l, keys):
        out = []
        for k in keys:
            src, dst = wsrc[k][l], wbf[k][l]
            Kd, Nd = src.shape
            for kc in range(Kd // 128):
                for c0 in range(0, Nd, CW):
                    out.append((src, dst, kc, c0, min(CW, Nd - c0)))
        return out

    def do_cast(desc, r32, rbf, engs=("dve", "act", "pool"), steng="pool"):
        src, dst, kc, c0, w = desc
        s32, sem32 = r32.next()
        sbf, sembf = rbf.next()
        dma("sp", s32.ap[:, 0:w], src[kc * 128:(kc + 1) * 128, c0:c0 + w], [], [s32.buf], sem32)
        cp(engs[cast_cnt[0] % len(engs)], sbf.ap[:, 0:w], s32.ap[:, 0:w], [s32], [sbf])
        dma(steng, dst[kc * 128:(kc + 1) * 128, c0:c0 + w], sbf.ap[:, 0:w], [sbf.buf], [], sembf)
        cast_cnt[0] += 1

    def do_diag(l, fc, ring):
        vb_ = V_L0 + l * V_LSZ
        sv, sem = ring.next()
        for k in range(KCF):
            ts("dve", sv.ap[:, k, :], IDENT, vcol(vb_ + V_CCW + k * 8 + fc), ALU.mult, [CONST, VEC], [sv])
        dma("pool", diagc_d[l, fc].rearrange("p (k j) -> p k j", j=128), sv.ap, [sv.buf], [], sem)

    cst32 = Ring(arena, "cst32", 4, [128, CW], F32)
    cstbf = Ring(arena, "cstbf", 4, [128, CW], BF16)
    for l in range(depth):
        for desc in cast_list(l, ("f1u", "f1d", "win", "pa", "pb", "wo", "f2u", "f2d")):
            do_cast(desc, cst32, cstbf)
    dgs0 = Ring(arena, "dgs0", 2, [128, KCF, 128], BF16)
    for l in range(depth):
        for fc in range(8):
            do_diag(l, fc, dgs0)
    deferred_cast = []
    deferred_diag = []
    P.barrier()

    def slab_up(key, l, j0, nj):
        def f(slot):
            w = wbf[key][l].rearrange("(k p) n -> p k n", p=128)
            dst = slot.ap.bitcast(BF16)[:, 0:8 * 1024].rearrange("p (k t c) -> p k t c", t=2, c=512)
            return [(dst[:, :, 0, 0:nj * 128], w[:, :, j0 * 128:(j0 + nj) * 128]),
                    (dst[:, :, 1, 0:nj * 128], w[:, :, FF + j0 * 128:FF + (j0 + nj) * 128])]
        return f

    def slab_cols(key, l, c0, ncols, KC):
        def f(slot):
            w = wbf[key][l].rearrange("(k p) n -> p k n", p=128)
            dst = slot.ap.bitcast(BF16)[:, 0:KC * ncols].rearrange("p (k c) -> p k c", c=ncols)
            if KC > 8:
                h = KC // 2
                return [(dst[:, 0:h, :], w[:, 0:h, c0:c0 + ncols]), (dst[:, h:KC, :], w[:, h:KC, c0:c0 + ncols])]
            return [(dst, w[:, :, c0:c0 + ncols])]
        return f

    def slab_pair(key, l, c0a, c0b, ncols):
        def f(slot):
            w = wbf[key][l].rearrange("(k p) n -> p k n", p=128)
            dst = slot.ap.bitcast(BF16)[:, 0:8 * 2 * ncols].rearrange("p (k t c) -> p k t c", t=2, c=ncols)
            return [(dst[:, :, 0, :], w[:, :, c0a:c0a + ncols]), (dst[:, :, 1, :], w[:, :, c0b:c0b + ncols])]
        return f

    def slab_diag(l, fc):
        def f(slot):
            dst = slot.ap.bitcast(BF16)[:, 0:KCF * 128]
            return [(dst, diagc_d[l, fc])]
        return f

    def wview(slot, KC, ncols):
        return slot.ap.bitcast(BF16)[:, 0:KC * ncols].rearrange("p (k c) -> p k c", c=ncols)

    def wview2(slot, ncols):
        return slot.ap.bitcast(BF16)[:, 0:8 * 2 * ncols].rearrange("p (k t c) -> p k t c", t=2, c=ncols)

    def rstd_from(bk, RSTD, TMP, nfeat):
        act(TMP.ap, bk.ap, AF.Ln, [bk], [TMP], bias=EPS, scale=1.0 / nfeat)
        act(bk.ap, TMP.ap, AF.Exp, [TMP], [bk], scale=-0.5)

    def norm_mod(Hs, U, l, si, cm):
        bk = nextbank()
        for fc in range(8):
            sq, _ = cm["sqb"].next()
            act(sq.ap, Hs.ap[:, fc, :], AF.Square, [Hs], [sq])
            mm(bk, bk.ap, ONESB, sq.ap, fc == 0, fc == 7, [sq, CBF])
        rstd_from(bk, cm["rstd"], cm["tmp0"], D)
        for fc in range(8):
            t, _ = cm["tmpn"].next()
            stt(t.ap, Hs.ap[:, fc, :], dcol(l, si, 0, fc), ALU.mult, bk.ap, ALU.mult, [Hs, bk, DER], [t])
            act(U.ap[:, fc, :], t.ap, AF.Identity, [t, DER], [U], bias=dcol(l, si, 1, fc))

    def ffn(Hs, U, HID, l, si, ws, cm):
        j = 0
        while j < 22:
            nj = min(4, 22 - j)
            slot = ws.next()
            wv = wview2(slot, 512)
            for jj in range(nj):
                ba, bb = nextbank(), nextbank()
                for kc in range(8):
                    mm(ba, ba.ap, wv[:, kc, 0, jj * 128:(jj + 1) * 128], U.ap[:, kc, :], kc == 0, kc == 7, [slot, U])
                for kc in range(8):
                    mm(bb, bb.ap, wv[:, kc, 1, jj * 128:(jj + 1) * 128], U.ap[:, kc, :], kc == 0, kc == 7, [slot, U])
                sa, _ = cm["silu"].next()
                act(sa.ap, ba.ap, AF.Silu, [ba], [sa])
                tt("dve", HID.ap[:, j + jj, :], sa.ap, bb.ap, ALU.mult, [sa, bb], [HID])
            j += nj
        for m0 in range(0, 8, 2):
            slot = ws.next()
            wv = wview(slot, 22, 256)
            for mm_ in range(2):
                mc = m0 + mm_
                bk = nextbank()
                for kc in range(22):
                    mm(bk, bk.ap, wv[:, kc, mm_ * 128:(mm_ + 1) * 128], HID.ap[:, kc, :], kc == 0, kc == 21, [slot, HID])
                stt(Hs.ap[:, mc, :], bk.ap, dcol(l, si, 2, mc), ALU.mult, Hs.ap[:, mc, :], ALU.add, [bk, Hs, DER], [Hs])

    def ffn_sched(key_u, key_d, l):
        s = []
        j = 0
        while j < 22:
            nj = min(4, 22 - j)
            s.append(slab_up(key_u, l, j, nj))
            j += nj
        for m0 in range(0, 8, 2):
            s.append(slab_cols(key_d, l, m0 * 128, 256, 22))
        return s

    def common(ar, nh=2):
        cm = {}
        cm["H32"] = [ar.take("h32_%d" % i, [128, 8, 512], F32) for i in range(nh)]
        cm["U"] = ar.take("u", [128, 8, 512], BF16)
        cm["HID"] = ar.take("hid", [128, 22, 512], BF16)
        cm["sqb"] = Ring(ar, "sqb", 3, [128, 512], BF16)
        cm["rstd"] = None
        cm["tmp0"] = ar.take("tmp0", [128, 512], F32)
        cm["tmpn"] = Ring(ar, "tmpn", 2, [128, 512], F32)
        cm["silu"] = Ring(ar, "silu", 3, [128, 512], F32)
        cm["wslots"] = [ar.take("wslot%d" % i, [128, 16384], U8) for i in range(3)]
        return cm

    def sweep_A(l):
        arena.reset()
        cm = common(arena)
        XT = arena.take("xt", [128, 4, 1024], F32)
        st32 = Ring(arena, "st32", 2, [128, 4, 512], F32)
        stbf = Ring(arena, "stbf", 2, [128, 4, 512], BF16)
        stdt = Ring(arena, "stdt", 2, [128, 4, 64], F32)
        if l == 0 and (deferred_cast or deferred_diag):
            c32 = Ring(arena, "c32A", 2, [128, CW], F32)
            cbf = Ring(arena, "cbfA", 2, [128, CW], BF16)
            dgr = Ring(arena, "dgsA", 1, [128, KCF, 128], BF16)
            per_tile = (len(deferred_cast) + NT - 1) // NT
            per_tile_d = (len(deferred_diag) + NT - 1) // NT
        ws = WStream(cm["wslots"], "wA")
        for i in range(NT):
            s = ffn_sched("f1u", "f1d", l)
            for q in range(4):
                s.append(slab_cols("win", l, q * 512, 512, 8))
            for q in range(6):
                s.append(slab_cols("win", l, OFF_XBC + q * 512, 512, 8))
            s.append(slab_cols("win", l, OFF_DT, 64, 8))
            for q in range(2):
                s.append(slab_pair("win", l, OFF_GLU + q * 512, OFF_GLU + D + q * 512, 512))
            for q in range(4):
                s.append(slab_cols("win", l, OFF_GATE + q * 512, 512, 8))
            ws.extend(s)
        U, HID = cm["U"], cm["HID"]

        def load_h(i):
            Hs = cm["H32"][i % 2]
            t0 = i * 512
            if l == 0:
                dma("sp", XT.ap, x_d.rearrange("(n p) f -> p n f", p=128)[:, i * 4:(i + 1) * 4, :], [], [XT.buf], "xt")
                for fc in range(8):
                    bk = nextbank()
                    for q in range(4):
                        tr(bk, bk.ap[:, q * 128:(q + 1) * 128], XT.ap[:, q, fc * 128:(fc + 1) * 128], IDENT, [XT, CONST])
                    cp("act" if fc % 2 else "dve", Hs.ap[:, fc, :], bk.ap, [bk], [Hs])
            else:
                dma("sp", Hs.ap, fm(HN_d)[:, :, t0:t0 + 512], [], [Hs.buf], ("h32", i % 2), nsplit=2)

        load_h(0)
        for i in range(NT):
            Hs = cm["H32"][i % 2]
            t0 = i * 512
            norm_mod(Hs, U, l, 0, cm)
            if i + 1 < NT:
                load_h(i + 1)
            ffn(Hs, U, HID, l, 0, ws, cm)
            dma("pool", fm(H1_d)[:, :, t0:t0 + 512], Hs.ap, [Hs.buf], [], ("h1st", i % 2), nsplit=2)
            if l == 0 and (deferred_cast or deferred_diag):
                for _ in range(per_tile):
                    if deferred_cast:
                        do_cast(deferred_cast.pop(0), c32, cbf, engs=("dve", "act"), steng="sp")
                for _ in range(per_tile_d):
                    if deferred_diag:
                        do_diag(*deferred_diag.pop(0), dgr)
            norm_mod(Hs, U, l, 1, cm)
            for q in range(4):
                slot = ws.next()
                wv = wview(slot, 8, 512)
                sv, sem = st32.next()
                for mm_ in range(4):
                    bk = nextbank()
                    for kc in range(8):
                        mm(bk, bk.ap, wv[:, kc, mm_ * 128:(mm_ + 1) * 128], U.ap[:, kc, :], kc == 0, kc == 7, [slot, U])
                    act(sv.ap[:, mm_, :], bk.ap, AF.Silu, [bk], [sv])
                dma("pool", fm(ZS_d)[:, q * 4:(q + 1) * 4, t0:t0 + 512], sv.ap, [sv.buf], [], sem)
            for q in range(6):
                slot = ws.next()
                wv = wview(slot, 8, 512)
                sv, sem = stbf.next()
                for mm_ in range(4):
                    bk = nextbank()
                    for kc in range(8):
                        mm(bk, bk.ap, wv[:, kc, mm_ * 128:(mm_ + 1) * 128], U.ap[:, kc, :], kc == 0, kc == 7, [slot, U])
                    cp("dve", sv.ap[:, mm_, :], bk.ap, [bk], [sv])
                dma("pool", fm(XBCP_d)[:, q * 4:(q + 1) * 4, t0:t0 + 512], sv.ap, [sv.buf], [], sem)
            slot = ws.next()
            wv = wview(slot, 8, 64)
            bk = nextbank()
            for q in range(4):
                for kc in range(8):
                    mm(bk, bk.ap[:, q * 64:(q + 1) * 64], U.ap[:, kc, q * 128:(q + 1) * 128], wv[:, kc, :], kc == 0, kc == 7,
                       [slot, U])
            sv, sem = stdt.next()
            cp("dve", sv.ap, bk.ap[:, 0:256].rearrange("p (q c) -> p q c", c=64), [bk], [sv])
            dma("pool", DT_d.rearrange("(n p) c -> p n c", p=128)[:, i * 4:(i + 1) * 4, :], sv.ap, [sv.buf], [], sem)
            for q in range(2):
                slot = ws.next()
                wv = wview2(slot, 512)
                sv, sem = stbf.next()
                for mm_ in range(4):
                    ba, bg = nextbank(), nextbank()
                    for kc in range(8):
                        mm(ba, ba.ap, wv[:, kc, 0, mm_ * 128:(mm_ + 1) * 128], U.ap[:, kc, :], kc == 0, kc == 7, [slot, U])
                    for kc in range(8):
                        mm(bg, bg.ap, wv[:, kc, 1, mm_ * 128:(mm_ + 1) * 128], U.ap[:, kc, :], kc == 0, kc == 7, [slot, U])
                    sg, _ = cm["silu"].next()
                    act(sg.ap, bg.ap, AF.Sigmoid, [bg], [sg])
                    tt("dve", sv.ap[:, mm_, :], sg.ap, ba.ap, ALU.mult, [sg, ba], [sv])
                dma("pool", fm(VP_d)[:, q * 4:(q + 1) * 4, t0:t0 + 512], sv.ap, [sv.buf], [], sem)
            for q in range(4):
                slot = ws.next()
                wv = wview(slot, 8, 512)
                sv, sem = st32.next()
                for mm_ in range(4):
                    bk = nextbank()
                    for kc in range(8):
                        mm(bk, bk.ap, wv[:, kc, mm_ * 128:(mm_ + 1) * 128], U.ap[:, kc, :], kc == 0, kc == 7, [slot, U])
                    act(sv.ap[:, mm_, :], bk.ap, AF.Sigmoid, [bk], [sv])
                dst = GA_d if q < 2 else GB_d
                qq = q % 2
                dma("pool", fm(dst)[:, qq * 4:(qq + 1) * 4, t0:t0 + 512], sv.ap, [sv.buf], [], sem)
        P.barrier()

    def sweep_S(l, d):
        fwd = d == 0
        arena.reset()
        vb = V_L0 + l * V_LSZ
        XS2 = [arena.take("xsbc%d" % i, [128, 24, 512], BF16) for i in range(2)]
        DTT = arena.take("dtt", [128, 4, 64], F32)
        DTX = arena.take("dtx", [128, 4, 32], F32)
        DTP2 = [arena.take("dtp%d" % i, [128, 4, 32], F32) for i in range(2)]
        DA2 = [arena.take("da%d" % i, [128, 4, 32], F32) for i in range(2)]
        DAH2 = [arena.take("dah%d" % i, [128, 4, 32], BF16) for i in range(2)]
        DAL2 = [arena.take("dal%d" % i, [128, 4, 32], BF16) for i in range(2)]
        XSDT = Ring(arena, "xsdt", 2, [128, 2048], BF16)
        BTOK = Ring(arena, "btok", 2, [128, 4, 128], BF16)
        CBTS = Ring(arena, "cbts", 2, [128, 512], BF16)
        SMALL = Ring(arena, "small", 3, [128, 5, 32], F32)
        ERING = Ring(arena, "e", 5, [128, 4, 128], BF16)
        GRING = Ring(arena, "g", 6, [128, 4, 128], BF16)
        XSD = Ring(arena, "xsd", 4, [128, 512], BF16)
        T1R = Ring(arena, "t1", 2, [128, 512], F32)
        T2R = Ring(arena, "t2", 2, [128, 512], F32)
        HT = Ring(arena, "ht", 4, [128, 512], F32)
        H32s = arena.take("hstate", [128, 2048], F32)
        HBF = arena.take("hbf", [128, 2048], BF16)
        if fwd:
            XP = arena.take("xp", [128, 24, 516], BF16)
            DG = arena.take("dg", [128, 24, KS, 128], BF16)
            DX = Ring(arena, "dx", 2, [128, 2048], F32)
            YST = Ring(arena, "yst", 3, [128, 512], F32)
        else:
            YFW = Ring(arena, "yfw", 2, [128, 2048], F32)
            YFM2 = [arena.take("yfm%d" % i, [128, 16, 512], F32) for i in range(2)]
        TRI = TRIU if fwd else TRIL
        TRIB = TRIUB if fwd else TRILB
        NEGM = NEGFB if fwd else NEGBB
        rot = [0]
        RB = [banks[0], banks[1]]

        def rb():
            b_ = RB[rot[0] % 2]
            rot[0] += 1
            return b_
        BK_A = [banks[2], banks[7], banks[5], banks[3]]
        BK_Y1 = [banks[4], banks[6]]
        HG = [Buf("hg%d" % g_) for g_ in range(4)]
        HBG = [Buf("hbg%d" % g_) for g_ in range(4)]
        dtb = ROWB.ap[:, l * 160 + d * 32:l * 160 + d * 32 + 32]
        aneg = ANEG.ap[:, l * 64 + d * 32:l * 64 + d * 32 + 32]
        dsk = ROWB.ap[:, l * 160 + 128:l * 160 + 160]

        P.add("dve", lambda h: h.memset(H32s.ap, 0.0), [], HG)
        P.add("dve", lambda h: h.memset(HBF.ap, 0.0), [], HBG)
        if fwd:
            for j in range(24):
                for k in range(KS):
                    ts("dve" if (j + k) % 2 else "pool", DG.ap[:, j, k, :], IDENT, vcol(vb + V_SCW + k * 24 + j), ALU.mult,
                       [CONST, VEC], [DG])
        tiles = list(range(NT)) if fwd else list(range(NT - 1, -1, -1))
        chunks = [0, 1, 2, 3] if fwd else [3, 2, 1, 0]
        seq = [(ti, i, q) for ti, i in enumerate(tiles) for q in chunks]
        ctxs = {}

        def tile_load(ti, i):
            t0 = i * 512
            XS = XS2[ti % 2]
            DTP, DA, DAH, DAL = DTP2[ti % 2], DA2[ti % 2], DAH2[ti % 2], DAL2[ti % 2]
            dma("sp", DTT.ap, DT_d.rearrange("(n p) c -> p n c", p=128)[:, i * 4:(i + 1) * 4, :], [], [DTT.buf], "dtt")
            if fwd:
                lo, hi = t0 - 2, t0 + 514
                clo, chi = max(lo, 0), min(hi, ntok)
                if clo > lo:
                    memset("dve", XP, XP.ap[:, :, 0:clo - lo], 0.0)
                if chi < hi:
                    memset("dve", XP, XP.ap[:, :, 516 - (hi - chi):516], 0.0)
                dma("sp", XP.ap[:, :, clo - lo:chi - lo], fm(XBCP_d)[:, :, clo:chi], [], [XP.buf], "xp", nsplit=3)
                if t0 + 512 == link:
                    ts("dve", XP.ap[:, :, 514:516], XP.ap[:, :, 514:516], FLAG.ap[:, 0:1], ALU.mult, [XP, FLAG], [XP])
            else:
                dma("sp", XS.ap, fm(XSBC_d)[:, :, t0:t0 + 512], [], [XS.buf], ("xs", ti % 2), nsplit=3)
            tt("dve", DTX.ap, DTT.ap[:, :, d * 32:d * 32 + 32], dtb.unsqueeze(1).broadcast_to([128, 4, 32]), ALU.add,
               [DTT, ROWB], [DTX])
            act(DTX.ap, DTX.ap, AF.Exp, [DTX], [DTX])
            act(DTP.ap, DTX.ap, AF.Ln, [DTX], [DTP], bias=1.0)
            tt("dve", DA.ap, DTP.ap, aneg.unsqueeze(1).broadcast_to([128, 4, 32]), ALU.mult, [DTP, ANEG], [DA])
            cp("dve", DAH.ap, DA.ap, [DA], [DAH])
            tt("dve", DAL.ap, DA.ap, DAH.ap, ALU.subtract, [DA, DAH], [DAL])
            if fwd:
                for j in range(24):
                    bk = rb()
                    for k in range(KS):
                        mm(bk, bk.ap, DG.ap[:, j, k, :], XP.ap[:, j, k:k + 512], k == 0, k == KS - 1, [DG, XP])
                    act(XS.ap[:, j, :], bk.ap, AF.Silu, [bk, VEC], [XS], bias=vcol(vb + V_SCB + j))
                dma("pool", fm(XSBC_d)[:, :, t0:t0 + 512], XS.ap, [XS.buf], [], ("xsst", ti % 2), nsplit=3)

        def prep(n, piece):
            ti, i, q = seq[n]
            XS = XS2[ti % 2]
            DTP, DA = DTP2[ti % 2], DA2[ti % 2]
            tq = slice(q * 128, (q + 1) * 128)
            tok0 = i * 512 + q * 128
            if piece == 0:
                c = {}
                c.update(XS=XS, DAH=DAH2[ti % 2], DAL=DAL2[ti % 2], q=q, tq=tq, tok0=tok0, ti=ti, i=i)
                c["g"] = {}
                ctxs[n] = c
                sm, _ = SMALL.next()
                c["sm"] = sm
                NACS, EACS, ETOT, DEC, TMPS = [sm.ap[:, k_, :] for k_ in range(5)]
                bs = rb()
                mm(bs, bs.ap[:, 0:32], TRI, DA.ap[:, q, :], True, True, [CONST, DA])
                mm(bs, bs.ap[:, 32:64], ONES, DA.ap[:, q, :], True, True, [CONST, DA])
                ts("dve", NACS, bs.ap[:, 0:32], -1.0, ALU.mult, [bs], [sm])
                act(EACS, bs.ap[:, 0:32], AF.Exp, [bs], [sm])
                act(ETOT, bs.ap[:, 32:64], AF.Exp, [bs], [sm])
                tt("dve", TMPS, bs.ap[:, 32:64], NACS, ALU.add, [bs, sm], [sm])
                act(DEC, TMPS, AF.Exp, [sm], [sm])
                xsdt, _ = XSDT.next()
                c["xsdt"] = xsdt
                if fwd:
                    dx, _ = DX.next()
                    c["dx"] = dx
                return
            c = ctxs[n]
            if piece in (1, 2):
                half = piece - 1
                if (not fwd) and half == 0:
                    yfw, semy = YFW.next()
                    dma("sp", yfw.ap, YF_d[tok0:tok0 + 128, :], [], [yfw.buf], semy)
                    c["yfw"] = yfw
                xsdt = c["xsdt"]
                bk = rb()
                bkb = bk.ap.bitcast(BF16)
                for jj in range(8):
                    tr(bk, bkb[:, jj * 128:(jj + 1) * 128], XS.ap[:, half * 8 + jj, tq], IDENTB, [XS, CBF])
                src = bkb.rearrange("p (h e) -> p h e", e=64)
                tt("dve", xsdt.ap[:, half * 1024:(half + 1) * 1024].rearrange("p (h e) -> p h e", e=64), src,
                   bc(DTP.ap[:, q, half * 16:(half + 1) * 16], 64), ALU.mult, [bk, DTP], [xsdt])
                if fwd:
                    dx = c["dx"]
                    tt("dve", dx.ap[:, half * 1024:(half + 1) * 1024].rearrange("p (h e) -> p h e", e=64), src,
                       bc(dsk[:, half * 16:(half + 1) * 16], 64), ALU.mult, [bk, ROWB], [dx])
                return
            btok, _ = BTOK.next()
            c["btok"] = btok
            bk = rb()
            bkb = bk.ap.bitcast(BF16)
            for g in range(4):
                tr(bk, bkb[:, g * 128:(g + 1) * 128], XS.ap[:, 16 + g, tq], IDENTB, [XS, CBF])
            cp("act", btok.ap, bkb[:, 0:512].rearrange("p (g n) -> p g n", n=128), [bk], [btok])
            bcb = rb()
            for g in range(4):
                mm(bcb, bcb.ap[:, g * 128:(g + 1) * 128], XS.ap[:, 16 + g, tq], XS.ap[:, 20 + g, tq], True, True, [XS])
            cbt, _ = CBTS.next()
            cp("act", cbt.ap, bcb.ap, [bcb], [cbt])
            c["cbt"] = cbt

        def abc(n, s):
            c = ctxs[n]
            q, sm, xsdt = c["q"], c["sm"], c["xsdt"]
            NACS, EACS, ETOT, DEC, TMPS = [sm.ap[:, k_, :] for k_ in range(5)]
            if s % 2 == 0:
                g = s // 2
                gs = slice(g * 512, (g + 1) * 512)
                xsd, _ = XSD.next()
                tt("pool", xsd.ap.rearrange("p (h e) -> p h e", e=64), xsdt.ap[:, gs].rearrange("p (h e) -> p h e", e=64),
                   bc(DEC[:, g * 8:(g + 1) * 8], 64), ALU.mult, [xsdt, sm], [xsd])
                ht, _ = HT.next()
                P.add("pool", (lambda h, o=ht.ap.rearrange("p (h e) -> p h e", e=64),
                               i0=H32s.ap[:, gs].rearrange("p (h e) -> p h e", e=64),
                               i1=bc(ETOT[:, g * 8:(g + 1) * 8], 64): h.tensor_tensor(out=o, in0=i0, in1=i1, op=ALU.mult)),
                      [HG[g], sm.buf], [ht.buf])
                c["xsd%d" % g] = xsd
                c["ht%d" % g] = ht
            e_, _ = ERING.next()
            g_, _ = GRING.next()
            g = s // 2
            ba = BK_A[(n * 8 + s) % 4]
            for hq in range(4):
                h_ = s * 4 + hq
                o_ = ba.ap[:, hq * 128:(hq + 1) * 128]
                P.add("pe", (lambda h, o=o_, l_=c["DAH"].ap[:, q, h_:h_ + 1].broadcast_to([128, 128]):
                             h.matmul(o, l_, TRIB, start=True, stop=False)), [c["DAH"].buf, CBF.buf], [ba.buf])
                P.add("pe", (lambda h, o=o_, l_=c["DAL"].ap[:, q, h_:h_ + 1].broadcast_to([128, 128]):
                             h.matmul(o, l_, TRIB, start=False, stop=False)), [c["DAL"].buf, CBF.buf], [ba.buf])
                P.add("pe", (lambda h, o=o_: h.matmul(o, IDENTB, NEGM, start=False, stop=True)), [CBF.buf], [ba.buf])
            for hq in range(4):
                h_ = s * 4 + hq
                P.add("act", (lambda h, o=e_.ap[:, hq, :], i_=ba.ap[:, hq * 128:(hq + 1) * 128], b_=NACS[:, h_:h_ + 1]:
                              h.activation(out=o, in_=i_, func=AF.Exp, bias=b_)), [ba.buf, sm.buf], [e_.buf])
            tt("dve", g_.ap, e_.ap, c["cbt"].ap[:, g * 128:(g + 1) * 128].unsqueeze(1).broadcast_to([128, 4, 128]), ALU.mult,
               [e_, c["cbt"]], [g_])
            c["g"][s] = g_

        def y1(n, s):
            c = ctxs[n]
            xsdt = c["xsdt"]
            g = s // 2
            by1 = BK_Y1[(n * 4 + g) % 2]
            for hq in range(4):
                h_ = s * 4 + hq
                hh = h_ % 8
                g_ = c["g"][s]
                mm(by1, by1.ap[:, hh * 64:(hh + 1) * 64], g_.ap[:, hq, :], xsdt.ap[:, h_ * 64:(h_ + 1) * 64], True, True,
                   [g_, xsdt])
            del c["g"][s]

        def epiB(n, g):
            c = ctxs[n]
            sm, tok0 = c["sm"], c["tok0"]
            EACS = sm.ap[:, 1, :]
            gs = slice(g * 512, (g + 1) * 512)
            by1 = BK_Y1[(n * 4 + g) % 2]
            BK_Y2, BK_ST = rb(), rb()
            P.add("pe", (lambda h, o=BK_Y2.ap, l_=c["XS"].ap[:, 20 + g, c["tq"]], r_=HBF.ap[:, gs]:
                         h.matmul(o, l_, r_, start=True, stop=True)), [c["XS"].buf, HBG[g]], [BK_Y2.buf])
            xsd = c["xsd%d" % g]
            mm(BK_ST, BK_ST.ap, c["btok"].ap[:, g, :], xsd.ap, True, True, [c["btok"], xsd])
            t1, _ = T1R.next()
            tt("dve", t1.ap.rearrange("p (h e) -> p h e", e=64), BK_Y2.ap.rearrange("p (h e) -> p h e", e=64),
               bc(EACS[:, g * 8:(g + 1) * 8], 64), ALU.mult, [BK_Y2, sm], [t1])
            t2, _ = T2R.next()
            tt("dve", t2.ap, t1.ap, by1.ap, ALU.add, [t1, by1], [t2])
            ht = c["ht%d" % g]
            P.add("dve", (lambda h, o=H32s.ap[:, gs], i0=ht.ap, i1=BK_ST.ap: h.tensor_tensor(out=o, in0=i0, in1=i1, op=ALU.add)),
                  [ht.buf, BK_ST.buf], [HG[g]])
            if (not fwd) and tok0 == link:
                P.add("dve", (lambda h, o=H32s.ap[:, gs]: h.tensor_scalar(out=o, in0=o, scalar1=FLAG.ap[:, 0:1], scalar2=None,
                                                                        op0=ALU.mult)), [HG[g], FLAG.buf], [HG[g]])
            if fwd:
                ys, semy = YST.next()
                tt("pool", ys.ap, t2.ap, c["dx"].ap[:, gs], ALU.add, [t2, c["dx"]], [ys])
                dma("sp", YF_d[tok0:tok0 + 128, gs], ys.ap, [ys.buf], [], semy)
            else:
                tt("pool", t2.ap, t2.ap, c["yfw"].ap[:, gs], ALU.add, [t2, c["yfw"]], [t2])
                c["t2_%d" % g] = t2

        def epiC(n, g):
            c = ctxs[n]
            gs = slice(g * 512, (g + 1) * 512)
            P.add("act", (lambda h, o=HBF.ap[:, gs], i_=H32s.ap[:, gs]: h.activation(out=o, in_=i_, func=AF.Copy)),
                  [HG[g]], [HBG[g]])
            if not fwd:
                YFM = YFM2[c["ti"] % 2]
                t2 = c["t2_%d" % g]
                bt = rb()
                for fb in range(4):
                    tr(bt, bt.ap[:, fb * 128:(fb + 1) * 128], t2.ap[:, fb * 128:(fb + 1) * 128], IDENT, [t2, CONST])
                cp("act", YFM.ap[:, g * 4:(g + 1) * 4, c["tq"]], bt.ap.rearrange("p (f t) -> p f t", t=128), [bt], [YFM])
            if g == 3:
                if (not fwd) and c["q"] == chunks[-1]:
                    t0 = c["i"] * 512
                    dma("sp", fm(Y_d)[:, :, t0:t0 + 512], YFM2[c["ti"] % 2].ap, [YFM2[c["ti"] % 2].buf], [],
                        ("yfmst", c["ti"] % 2), nsplit=2)

        tile_load(0, tiles[0])
        for pc in range(4):
            prep(0, pc)
        flat = [(n, s) for n in range(len(seq)) for s in range(8)]
        pend = []

        def run_due(idx):
            keep = []
            for due, fn in pend:
                if due <= idx:
                    fn()
                else:
                    keep.append((due, fn))
            pend[:] = keep

        LAG = 3
        for idx, (n, s) in enumerate(flat):
            abc(n, s)
            run_due(idx)
            pend.append((idx + LAG, (lambda n=n, s=s: y1(n, s))))
            if s % 2 == 1:
                g = s // 2
                pend.append((idx + LAG, (lambda n=n, g=g: epiB(n, g))))
                pend.append((idx + LAG + 1, (lambda n=n, g=g: epiC(n, g))))
            if n + 1 < len(seq) and s in (0, 2, 4, 6):
                if s == 0 and seq[n + 1][0] != seq[n][0]:
                    tile_load(seq[n + 1][0], seq[n + 1][1])
                prep(n + 1, s // 2)
        for k in range(len(flat), len(flat) + 6):
            run_due(k)
        assert not pend
        P.barrier()

    def sweep_C(l):
        arena.reset()
        last = l == depth - 1
        vb = V_L0 + l * V_LSZ
        cm = common(arena, nh=1)
        BIGT = arena.take("bigt", [128, 16, 512], F32)
        CV = arena.take("cv", [128, 8, 512], F32)
        RSTD2 = arena.take("rstd2", [128, 512], F32)
        TMP02 = arena.take("tmp02", [128, 512], F32)
        VPT = arena.take("vpt", [128, 8, 542], BF16)
        LY = Ring(arena, "ly", 2, [128, 512], F32)
        LZ = Ring(arena, "lz", 2, [128, 512], F32)
        LG = Ring(arena, "lg", 3, [128, 512], F32)
        U, HID = cm["U"], cm["HID"]
        hid_flat = HID.ap.rearrange("p a b -> p (a b)")
        YN = V(hid_flat[:, 0:16 * 512].rearrange("p (a b) -> p a b", b=512), HID.buf)
        MB = V(hid_flat[:, 0:8 * 512].rearrange("p (a b) -> p a b", b=512), HID.buf)
        GY = BIGT
        M1 = V(BIGT.ap[:, 8:16, :], BIGT.buf)
        ws = WStream(cm["wslots"], "wC")
        for i in range(NT):
            s = []
            for fc in range(8):
                s.append(slab_diag(l, fc))
            for m0 in range(0, 8, 2):
                s.append(slab_cols("pa", l, m0 * 128, 256, 16))
            for q in range(2):
                s.append(slab_cols("pb", l, q * 512, 512, 8))
            for q in range(2):
                s.append(slab_cols("wo", l, q * 512, 512, 8))
            s += ffn_sched("f2u", "f2d", l)
            ws.extend(s)

        for i in range(NT):
            t0 = i * 512
            Hs = cm["H32"][0]
            dma("sp", Hs.ap, fm(H1_d)[:, :, t0:t0 + 512], [], [Hs.buf], ("h32", 0), nsplit=2)
            lo, hi = t0 - 15, t0 + 527
            clo, chi = max(lo, 0), min(hi, ntok)
            if clo > lo:
                memset("dve", VPT, VPT.ap[:, :, 0:clo - lo], 0.0)
            if chi < hi:
                memset("dve", VPT, VPT.ap[:, :, 542 - (hi - chi):542], 0.0)
            dma("sp", VPT.ap[:, :, clo - lo:chi - lo], fm(VP_d)[:, :, clo:chi], [], [VPT.buf], "vpt")
            if t0 + 512 == link:
                ts("dve", VPT.ap[:, :, 527:542], VPT.ap[:, :, 527:542], FLAG.ap[:, 0:1], ALU.mult, [VPT, FLAG], [VPT])
            for f2 in range(16):
                ly, sy = LY.next()
                lz, sz = LZ.next()
                dma("sp", ly.ap, fm(Y_d)[:, f2, t0:t0 + 512], [], [ly.buf], sy)
                dma("sp", lz.ap, fm(ZS_d)[:, f2, t0:t0 + 512], [], [lz.buf], sz)
                tt("pool", GY.ap[:, f2, :], ly.ap, lz.ap, ALU.mult, [ly, lz], [GY])
            for fc in range(8):
                slot = ws.next()
                wv = slot.ap.bitcast(BF16)[:, 0:KCF * 128].rearrange("p (k j) -> p k j", j=128)
                bk = nextbank()
                for k in range(KCF):
                    mm(bk, bk.ap, wv[:, k, :], VPT.ap[:, fc, k:k + 512], k == 0, k == KCF - 1, [slot, VPT])
                act(CV.ap[:, fc, :], bk.ap, AF.Identity, [bk, VEC], [CV], bias=vcol(vb + V_CCB + fc))
            bk = nextbank()
            for fc in range(16):
                sq, _ = cm["sqb"].next()
                act(sq.ap, GY.ap[:, fc, :], AF.Square, [GY], [sq])
                mm(bk, bk.ap, ONESB, sq.ap, fc == 0, fc == 15, [sq, CBF])
            rstd_from(bk, cm["rstd"], cm["tmp0"], DI)
            for fc in range(16):
                stt(YN.ap[:, fc, :], GY.ap[:, fc, :], vcol(vb + V_SNRM + fc), ALU.mult, bk.ap, ALU.mult,
                    [GY, bk, VEC], [YN])
            b1, b2 = nextbank(), nextbank()
            for fc in range(8):
                mm(b1, b1.ap, ONES, CV.ap[:, fc, :], fc == 0, fc == 7, [CV, CONST])
            for fc in range(8):
                sq, _ = cm["sqb"].next()
                act(sq.ap, CV.ap[:, fc, :], AF.Square, [CV], [sq])
                mm(b2, b2.ap, ONESB, sq.ap, fc == 0, fc == 7, [sq, CBF])
            MEAN, _ = cm["tmpn"].next()
            act(MEAN.ap, b1.ap, AF.Copy, [b1], [MEAN], scale=1.0 / D)
            MSQ, _ = cm["tmpn"].next()
            tt("dve", MSQ.ap, MEAN.ap, MEAN.ap, ALU.mult, [MEAN], [MSQ])
            stt(MSQ.ap, b2.ap, 1.0 / D, ALU.mult, MSQ.ap, ALU.subtract, [b2, MSQ], [MSQ])
            act(TMP02.ap, MSQ.ap, AF.Ln, [MSQ], [TMP02], bias=EPS)
            act(RSTD2.ap, TMP02.ap, AF.Exp, [TMP02], [RSTD2], scale=-0.5)
            for m0 in range(0, 8, 2):
                slot = ws.next()
                wv = wview(slot, 16, 256)
                for mm_ in range(2):
                    mc = m0 + mm_
                    bk = nextbank()
                    for kc in range(16):
                        mm(bk, bk.ap, wv[:, kc, mm_ * 128:(mm_ + 1) * 128], YN.ap[:, kc, :], kc == 0, kc == 15, [slot, YN])
                    lg, sg = LG.next()
                    dma("sp", lg.ap, fm(GA_d)[:, mc, t0:t0 + 512], [], [lg.buf], sg)
                    tt("dve", M1.ap[:, mc, :], lg.ap, bk.ap, ALU.mult, [lg, bk], [M1])
            for fc in range(8):
                t, _ = cm["silu"].next()
                tt("dve", t.ap, CV.ap[:, fc, :], MEAN.ap, ALU.subtract, [CV, MEAN], [t])
                stt(t.ap, t.ap, vcol(vb + V_LNG + fc), ALU.mult, RSTD2.ap, ALU.mult, [t, RSTD2, VEC], [t])
                act(U.ap[:, fc, :], t.ap, AF.Silu, [t, VEC], [U], bias=vcol(vb + V_LNB + fc))
            for q in range(2):
                slot = ws.next()
                wv = wview(slot, 8, 512)
                for mm_ in range(4):
                    mc = q * 4 + mm_
                    bk = nextbank()
                    for kc in range(8):
                        mm(bk, bk.ap, wv[:, kc, mm_ * 128:(mm_ + 1) * 128], U.ap[:, kc, :], kc == 0, kc == 7, [slot, U])
                    lg, sg = LG.next()
                    dma("sp", lg.ap, fm(GB_d)[:, mc, t0:t0 + 512], [], [lg.buf], sg)
                    t, _ = cm["silu"].next()
                    tt("dve", t.ap, lg.ap, bk.ap, ALU.mult, [lg, bk], [t])
                    tt("pool", MB.ap[:, mc, :], t.ap, M1.ap[:, mc, :], ALU.add, [t, M1], [MB])
            for q in range(2):
                slot = ws.next()
                wv = wview(slot, 8, 512)
                for mm_ in range(4):
                    mc = q * 4 + mm_
                    bk = nextbank()
                    for kc in range(8):
                        mm(bk, bk.ap, wv[:, kc, mm_ * 128:(mm_ + 1) * 128], MB.ap[:, kc, :], kc == 0, kc == 7, [slot, MB])
                    stt(Hs.ap[:, mc, :], bk.ap, dcol(l, 1, 2, mc), ALU.mult, Hs.ap[:, mc, :], ALU.add, [bk, Hs, DER], [Hs])
            norm_mod(Hs, U, l, 2, cm)
            ffn(Hs, U, HID, l, 2, ws, cm)
            if not last:
                dma("pool", fm(HN_d)[:, :, t0:t0 + 512], Hs.ap, [Hs.buf], [], ("hnst", i % 2), nsplit=2)
            else:
                bk = nextbank()
                for fc in range(8):
                    sq, _ = cm["sqb"].next()
                    act(sq.ap, Hs.ap[:, fc, :], AF.Square, [Hs], [sq])
                    mm(bk, bk.ap, ONESB, sq.ap, fc == 0, fc == 7, [sq, CBF])
                rstd_from(bk, cm["rstd"], cm["tmp0"], D)
                OUTF = CV
                for fc in range(8):
                    stt(OUTF.ap[:, fc, :], Hs.ap[:, fc, :], vcol(V_FN + fc), ALU.mult, bk.ap, ALU.mult,
                        [Hs, bk, VEC], [OUTF])
                OT = V(BIGT.ap[:, 8:16, :].rearrange("p a b -> p (a b)").rearrange("p (q f) -> p q f", f=1024), BIGT.buf)
                for q in range(4):
                    for half in range(2):
                        bk = nextbank()
                        for ff in range(4):
                            fc = half * 4 + ff
                            tr(bk, bk.ap[:, ff * 128:(ff + 1) * 128], OUTF.ap[:, fc, q * 128:(q + 1) * 128], IDENT,
                               [OUTF, CONST])
                        cp("act" if half else "dve", OT.ap[:, q, half * 512:(half + 1) * 512], bk.ap, [bk], [OT])
                dma("pool", y_d.rearrange("(n p) f -> p n f", p=128)[:, i * 4:(i + 1) * 4, :], OT.ap, [OT.buf], [], "yout")
        P.barrier()

    for l in range(depth):
        sweep_A(l)
        sweep_S(l, 0)
        sweep_S(l, 1)
        sweep_C(l)
    P.barrier()
    for e in ENGS:
        P.add(e, lambda h: h.nop())
    with nc.Block() as block:
        P.emit(nc, block)
    return nc, P


def make_consts():
    c = np.zeros((128, 768), np.float32)
    c[:, 0:128] = np.eye(128, dtype=np.float32)
    c[:, 128:256] = np.triu(np.ones((128, 128), np.float32))
    c[:, 256:384] = np.tril(np.ones((128, 128), np.float32))
    c[:, 384:512] = -30000.0 * np.tril(np.ones((128, 128), np.float32), -1)
    c[:, 512:640] = -30000.0 * np.triu(np.ones((128, 128), np.float32), 1)
    c[:, 640:768] = 1.0
    return c


def make_vecs(c, inp):
    rows = np.zeros((V_ROWS, 128), np.float32)
    rows[V_C:V_C + 8] = np.asarray(c, np.float32).reshape(8, 128)
    rows[V_FN:V_FN + 8] = np.asarray(inp["final_norm"], np.float32).reshape(8, 128)
    for l in range(DEPTH):
        b = V_L0 + l * V_LSZ
        rows[b + V_BADA:b + V_BADA + 72] = inp["b_ada"][l].reshape(72, 128)
        rows[b + V_N1:b + V_N1 + 8] = inp["ffn1_norm"][l].reshape(8, 128)
        rows[b + V_N2:b + V_N2 + 8] = inp["mix_norm"][l].reshape(8, 128)
        rows[b + V_N3:b + V_N3 + 8] = inp["ffn2_norm"][l].reshape(8, 128)
        rows[b + V_SCW:b + V_SCW + 120] = inp["ssm_conv_w"][l].reshape(KS * 24, 128)
        rows[b + V_SCB:b + V_SCB + 24] = inp["ssm_conv_b"][l].reshape(24, 128)
        rows[b + V_SNRM:b + V_SNRM + 16] = inp["ssm_norm"][l].reshape(16, 128)
        rows[b + V_CCW:b + V_CCW + 248] = inp["conf_conv_w"][l].reshape(KCF * 8, 128)
        rows[b + V_CCB:b + V_CCB + 8] = inp["conf_conv_b"][l].reshape(8, 128)
        rows[b + V_LNG:b + V_LNG + 8] = inp["conf_ln_g"][l].reshape(8, 128)
        rows[b + V_LNB:b + V_LNB + 8] = inp["conf_ln_b"][l].reshape(8, 128)
    return rows


def make_rowb(inp):
    r = np.zeros((DEPTH, 160), np.float32)
    for l in range(DEPTH):
        r[l, 0:64] = inp["dt_bias"][l].reshape(64)
        r[l, 64:128] = inp["a_log"][l].reshape(64)
        r[l, 128:160] = inp["d_skip"][l].reshape(32)
    return np.ascontiguousarray(np.broadcast_to(r.reshape(1, DEPTH * 160), (128, DEPTH * 160)))


_CACHE = {}


def kernel(**inputs):
    inp = {k: np.asarray(v) for k, v in inputs.items()}
    if "nc" not in _CACHE:
        _CACHE["nc"] = build()[0]
    nc = _CACHE["nc"]
    xp, xs = inp["x_prompt"], inp["x_sample"]
    consts = make_consts()
    rowb = make_rowb(inp)
    shared = {"consts": consts, "rowb": rowb, "w_ada": inp["w_ada"],
              "ffn1_up": inp["ffn1_up"], "ffn1_down": inp["ffn1_down"], "w_in": inp["w_in"],
              "w_proj_ssd": inp["w_proj_ssd"], "w_proj_conv": inp["w_proj_conv"], "w_out": inp["w_out"],
              "ffn2_up": inp["ffn2_up"], "ffn2_down": inp["ffn2_down"]}
    in_maps = []
    for core in range(NCORES):
        m = dict(shared)
        if core < 4:
            m["x"] = np.ascontiguousarray(xp[core])
            m["vecs"] = make_vecs(inp["c_prompt"][core], inp)
            m["flag"] = np.ones((128, 1), np.float32)
        else:
            b = core - 4
            xx = np.zeros((NTOK_FULL, D), np.float32)
            xx[:LINK_FULL] = xs[b]
            m["x"] = xx
            m["vecs"] = make_vecs(inp["c_sample"][b], inp)
            m["flag"] = np.zeros((128, 1), np.float32)
        in_maps.append(m)
    res = run_bass_kernel_spmd(nc, in_maps, core_ids=list(range(NCORES)))
    y_prompt = np.stack([np.asarray(res.results[c]["y"], np.float32) for c in range(4)], axis=0)
    y_sample = np.stack([np.asarray(res.results[4 + b]["y"], np.float32)[:LINK_FULL] for b in range(4)], axis=0)
    return (y_prompt, y_sample)
```

```python
import numpy as np
import concourse.bass as bass
import concourse.mybir as mybir
from concourse.bass_utils import run_bass_kernel_spmd

F32 = mybir.dt.float32
BF16 = mybir.dt.bfloat16
U8 = mybir.dt.uint8
AF = mybir.ActivationFunctionType
ALU = mybir.AluOpType

D = 1024
DI = 2048
NH = 32
HP = 64
NGRP = 4
NS = 128
KS = 5
CD = 3072
KCF = 31
FF = 2816
IN_COLS = 9280
OFF_XBC = 2048
OFF_DT = 5120
OFF_GLU = 5184
OFF_GATE = 7232
DEPTH = 2
EPS = 1e-6
NCORES = 8
NTOK_FULL = 8192
LINK_FULL = 4096

V_C = 0
V_FN = 8
V_L0 = 16
V_LSZ = 528
V_BADA, V_N1, V_N2, V_N3, V_SCW, V_SCB, V_SNRM, V_CCW, V_CCB, V_LNG, V_LNB = 0, 72, 80, 88, 96, 216, 240, 256, 504, 512, 520
V_ROWS = 1152

ENGS = ("pe", "act", "dve", "pool", "sp")


class Buf:
    __slots__ = ("name", "last_w", "readers")

    def __init__(self, name):
        self.name = name
        self.last_w = None
        self.readers = {}


class V:
    __slots__ = ("ap", "buf")

    def __init__(self, ap, buf):
        self.ap = ap
        self.buf = buf


class Op:
    __slots__ = ("eng", "fn", "deps", "signals", "idx", "is_dma", "sem", "val", "ninc")

    def __init__(self, eng, fn, is_dma, ninc):
        self.eng = eng
        self.fn = fn
        self.deps = None
        self.signals = False
        self.idx = 0
        self.is_dma = is_dma
        self.sem = None
        self.val = 0
        self.ninc = ninc


class Prog:
    def __init__(self):
        self.ops = {e: [] for e in ENGS}
        self.pending = {e: [] for e in ENGS}
        self.dma_sem_vals = {}
        self.last_dma = {}
        self.nops = 0

    def add(self, eng, fn, reads=(), writes=(), dma_sem=None, ninc=1):
        is_dma = dma_sem is not None
        op = Op(eng, fn, is_dma, ninc)
        deps = {}

        def need(d, raw=False):
            if d is None:
                return
            if d.is_dma:
                k = ("d", d.sem)
                if k not in deps or deps[k].val < d.val:
                    deps[k] = d
            else:
                if d.eng == eng and (not is_dma) and (eng == "pe" or not raw):
                    return
                k = ("e", d.eng)
                if k not in deps or deps[k].idx < d.idx:
                    deps[k] = d

        for b in reads:
            need(b.last_w, True)
        for b in writes:
            need(b.last_w)
            for r in b.readers.values():
                need(r)
        for d in self.pending[eng]:
            need(d)
        self.pending[eng] = []
        if is_dma:
            need(self.last_dma.get(dma_sem))
            v = self.dma_sem_vals.get(dma_sem, 0) + 16 * ninc
            self.dma_sem_vals[dma_sem] = v
            op.sem = dma_sem
            op.val = v
            self.last_dma[dma_sem] = op
        op.deps = list(deps.values())
        op.idx = len(self.ops[eng])
        self.ops[eng].append(op)
        for b in reads:
            b.readers[id(op) if is_dma else eng] = op
        for b in writes:
            b.last_w = op
            b.readers = {}
        self.nops += 1
        return op

    def barrier(self):
        evs = []
        for e in ENGS:
            for o in reversed(self.ops[e]):
                if not o.is_dma:
                    evs.append(o)
                    break
        for o in self.last_dma.values():
            evs.append(o)
        for e in ENGS:
            self.pending[e] = list(evs)

    def emit(self, nc, block):
        for e in ENGS:
            for op in self.ops[e]:
                for d in op.deps:
                    if not d.is_dma:
                        d.signals = True
        for e in ENGS:
            c = 0
            for op in self.ops[e]:
                if not op.is_dma and op.signals:
                    c += 1
                    op.val = c
        esem = {e: nc.alloc_semaphore("sem_" + e) for e in ENGS}
        dsem = {k: nc.alloc_semaphore("dsem_%d" % i) for i, k in enumerate(self.dma_sem_vals)}

        def make(e):
            ops = self.ops[e]

            def body(h):
                seen = {}
                for op in ops:
                    for d in op.deps:
                        if d.is_dma:
                            key, sem, val = ("d", d.sem), dsem[d.sem], d.val
                        else:
                            key, sem, val = ("e", d.eng), esem[d.eng], d.val
                        if seen.get(key, 0) >= val:
                            continue
                        seen[key] = val
                        h.wait_ge(sem, val)
                    ins = op.fn(h)
                    if op.is_dma:
                        if not isinstance(ins, (list, tuple)):
                            ins = [ins]
                        assert len(ins) == op.ninc, (len(ins), op.ninc)
                        for i_ in ins:
                            i_.then_inc(dsem[op.sem], 16)
                    elif op.signals:
                        ins.then_inc(esem[e], 1)
            return body

        block.tensor(make("pe"))
        block.scalar(make("act"))
        block.vector(make("dve"))
        block.gpsimd(make("pool"))
        block.sync(make("sp"))


class Arena:
    def __init__(self, big, lo, hi):
        self.big, self.lo, self.hi, self.off = big, lo, hi, lo

    def reset(self):
        self.off = self.lo

    def take(self, name, shape, dtype, buf=None):
        esz = 4 if dtype == F32 else (1 if dtype == U8 else 2)
        n = 1
        for s in shape[1:]:
            n *= s
        nb = n * esz
        off = (self.off + 63) // 64 * 64
        assert off + nb <= self.hi, ("SBUF arena overflow", name, off + nb, self.hi)
        self.off = off + nb
        ap = self.big[:, off:off + nb].bitcast(dtype)
        if len(shape) == 3:
            ap = ap.rearrange("p (a b) -> p a b", b=shape[2])
        elif len(shape) == 4:
            ap = ap.rearrange("p (a b c) -> p a b c", b=shape[2], c=shape[3])
        return V(ap, buf if buf is not None else Buf(name))


def bc(ap, n):
    return ap.unsqueeze(len(ap.shape)).broadcast_to(list(ap.shape) + [n])


def build(ntok=NTOK_FULL, link=LINK_FULL, depth=DEPTH, debug=False):
    assert ntok % 512 == 0 and link % 512 == 0
    NT = ntok // 512
    nc = bass.Bass("TRN2", target_bir_lowering=False)
    P = Prog()

    def din(name, shape, dt=F32):
        return nc.dram_tensor(name, list(shape), dt, kind="ExternalInput").ap()

    def dscr(name, shape, dt):
        kind = "ExternalOutput" if (debug and name in debug) else "Internal"
        return nc.dram_tensor(name, list(shape), dt, kind=kind).ap()

    x_d = din("x", [ntok, D])
    flag_d = din("flag", [128, 1])
    vecs_d = din("vecs", [V_ROWS, 128])
    consts_d = din("consts", [128, 768])
    rowb_d = din("rowb", [128, DEPTH * 160])
    w_ada_d = din("w_ada", [DEPTH, D, 9 * D])
    wsrc = {
        "f1u": din("ffn1_up", [DEPTH, D, 2 * FF]),
        "f1d": din("ffn1_down", [DEPTH, FF, D]),
        "win": din("w_in", [DEPTH, D, IN_COLS]),
        "pa": din("w_proj_ssd", [DEPTH, DI, D]),
        "pb": din("w_proj_conv", [DEPTH, D, D]),
        "wo": din("w_out", [DEPTH, D, D]),
        "f2u": din("ffn2_up", [DEPTH, D, 2 * FF]),
        "f2d": din("ffn2_down", [DEPTH, FF, D]),
    }
    y_d = nc.dram_tensor("y", [ntok, D], F32, kind="ExternalOutput").ap()

    wbf = {k: dscr("wb_" + k, list(v.shape), BF16) for k, v in wsrc.items()}
    diagc_d = dscr("diagc", [DEPTH, 8, 128, KCF * 128], BF16)
    H1_d = dscr("H1", [D, ntok], F32)
    HN_d = dscr("HN", [D, ntok], F32)
    ZS_d = dscr("ZS", [DI, ntok], F32)
    XBCP_d = dscr("XBCP", [CD, ntok], BF16)
    XSBC_d = dscr("XSBC", [CD, ntok], BF16)
    DT_d = dscr("DT", [ntok, 64], F32)
    VP_d = dscr("VP", [D, ntok], BF16)
    GA_d = dscr("GA", [D, ntok], F32)
    GB_d = dscr("GB", [D, ntok], F32)
    YF_d = dscr("YF", [ntok, DI], F32)
    Y_d = dscr("Y", [DI, ntok], F32)

    def fm(ap):
        return ap.rearrange("(c p) t -> p c t", p=128)

    avail = nc.sbuf_bytes_remaining() if callable(nc.sbuf_bytes_remaining) else nc.sbuf_bytes_remaining
    SB_BYTES = (int(avail) // 64) * 64 - 256
    big = nc.alloc_sbuf_tensor("big", [128, SB_BYTES], U8)
    PS = nc.alloc_psum_tensor("ps", [128, 8, 512], F32)
    banks = [V(PS[:, i, :], Buf("bank%d" % i)) for i in range(8)]
    bank_ctr = [0]

    def nextbank():
        b = banks[bank_ctr[0] % 8]
        bank_ctr[0] += 1
        return b

    pers = Arena(big, 0, 16384)
    CONST = pers.take("const", [128, 768], F32)
    IDENT, TRIU, TRIL, NEGF32, NEGB32, ONES = [CONST.ap[:, i * 128:(i + 1) * 128] for i in range(6)]
    CBF = pers.take("constbf", [128, 768], BF16)
    IDENTB, NEGFB, NEGBB, TRIUB, TRILB, ONESB = [CBF.ap[:, i * 128:(i + 1) * 128] for i in range(6)]
    VEC = pers.take("vec", [128, V_ROWS], F32)
    ROWB = pers.take("rowb", [128, DEPTH * 160], F32)
    ANEG = pers.take("aneg", [128, DEPTH * 64], F32)
    MODS = pers.take("mods", [128, DEPTH * 72], F32)
    DER = pers.take("der", [128, DEPTH * 72], F32)
    CACT = pers.take("cact", [128, 8, 2], F32)
    FLAG = pers.take("flag", [128, 1], F32)
    arena = Arena(big, 16384, SB_BYTES)

    def vcol(r):
        return VEC.ap[:, r:r + 1]

    def dma(eng, out, in_, reads, writes, sem, nsplit=1, split_axis=1):
        if nsplit == 1:
            P.add(eng, lambda h: h.dma_start(out=out, in_=in_), reads, writes, dma_sem=sem)
            return
        n = out.shape[split_axis]
        step = (n + nsplit - 1) // nsplit
        pieces = []
        for a in range(0, n, step):
            b = min(n, a + step)
            idx = [slice(None)] * len(out.shape)
            idx[split_axis] = slice(a, b)
            pieces.append((out[tuple(idx)], in_[tuple(idx)]))

        def fn(h):
            return [h.dma_start(out=o, in_=i) for o, i in pieces]
        P.add(eng, fn, reads, writes, dma_sem=sem, ninc=len(pieces))

    def mm(outv, out_ap, lhsT, rhs, start, stop, reads):
        P.add("pe", lambda h: h.matmul(out_ap, lhsT, rhs, start=start, stop=stop),
              [r.buf for r in reads], [outv.buf])

    def tr(outv, out_ap, in_ap, ident, reads):
        P.add("pe", lambda h: h.transpose(out_ap, in_ap, ident), [r.buf for r in reads], [outv.buf])

    def act(out_ap, in_ap, func, reads, writes, bias=None, scale=None):
        kw = {}
        if bias is not None:
            kw["bias"] = bias
        if scale is not None:
            kw["scale"] = scale
        P.add("act", lambda h: h.activation(out=out_ap, in_=in_ap, func=func, **kw),
              [r.buf for r in reads], [w.buf for w in writes])

    def tt(eng, out_ap, in0, in1, op, reads, writes):
        P.add(eng, lambda h: h.tensor_tensor(out=out_ap, in0=in0, in1=in1, op=op),
              [r.buf for r in reads], [w.buf for w in writes])

    def ts(eng, out_ap, in0, s1, op0, reads, writes, s2=None, op1=None):
        if op1 is None:
            P.add(eng, lambda h: h.tensor_scalar(out=out_ap, in0=in0, scalar1=s1, scalar2=None, op0=op0),
                  [r.buf for r in reads], [w.buf for w in writes])
        else:
            P.add(eng, lambda h: h.tensor_scalar(out=out_ap, in0=in0, scalar1=s1, scalar2=s2, op0=op0, op1=op1),
                  [r.buf for r in reads], [w.buf for w in writes])

    def stt(out_ap, in0, scalar, op0, in1, op1, reads, writes):
        P.add("dve", lambda h: h.scalar_tensor_tensor(out=out_ap, in0=in0, scalar=scalar, op0=op0, in1=in1, op1=op1),
              [r.buf for r in reads], [w.buf for w in writes])

    def cp(eng, out_ap, in_ap, reads, writes):
        if eng == "act":
            act(out_ap, in_ap, AF.Copy, reads, writes)
        else:
            P.add(eng, lambda h: h.tensor_copy(out=out_ap, in_=in_ap),
                  [r.buf for r in reads], [w.buf for w in writes])

    def memset(eng, v, ap, val):
        P.add(eng, lambda h: h.memset(ap, val), [], [v.buf])

    class Ring:
        def __init__(self, ar, name, n, shape, dtype):
            self.vs = [ar.take("%s%d" % (name, i), shape, dtype) for i in range(n)]
            self.i = 0
            self.name = name

        def next(self):
            k = self.i % len(self.vs)
            self.i += 1
            return self.vs[k], (self.name, k)

    class WStream:
        def __init__(self, slots, name):
            self.slots, self.name = slots, name
            self.sched, self.loaded, self.cur = [], 0, 0

        def extend(self, items):
            self.sched.extend(items)

        def _load(self, k):
            slot = self.slots[k % len(self.slots)]
            if hasattr(self.sched[k], "gen"):
                self.sched[k].gen(slot)
                return
            pieces = self.sched[k](slot)

            def fn(h):
                return [h.dma_start(out=o, in_=i) for o, i in pieces]
            P.add("sp", fn, [], [slot.buf], dma_sem=(self.name, k % len(self.slots)), ninc=len(pieces))

        def next(self):
            k = self.cur
            self.cur += 1
            want = min(len(self.sched), k + len(self.slots))
            while self.loaded < want:
                self._load(self.loaded)
                self.loaded += 1
            return self.slots[k % len(self.slots)]

    arena.reset()
    dma("sp", CONST.ap, consts_d, [], [CONST.buf], "c0")
    dma("sp", ROWB.ap, rowb_d, [], [ROWB.buf], "c1")
    dma("sp", FLAG.ap, flag_d, [], [FLAG.buf], "c2")
    VST = arena.take("vst", [128, 9, 128], F32)
    dma("sp", VST.ap, vecs_d.rearrange("(b p) f -> p b f", p=128), [], [VST.buf], "c3")
    cp("dve", CBF.ap[:, 0:128], IDENT, [CONST], [CBF])
    cp("dve", CBF.ap[:, 128:256], NEGF32, [CONST], [CBF])
    cp("dve", CBF.ap[:, 256:384], NEGB32, [CONST], [CBF])
    cp("dve", CBF.ap[:, 384:512], TRIU, [CONST], [CBF])
    cp("dve", CBF.ap[:, 512:640], TRIL, [CONST], [CBF])
    cp("dve", CBF.ap[:, 640:768], ONES, [CONST], [CBF])
    for b0 in range(0, 9, 4):
        nb = min(4, 9 - b0)
        bk = nextbank()
        for b in range(nb):
            tr(bk, bk.ap[:, b * 128:(b + 1) * 128], VST.ap[:, b0 + b, :], IDENT, [VST, CONST])
        cp("dve", VEC.ap[:, b0 * 128:(b0 + nb) * 128], bk.ap[:, 0:nb * 128], [bk], [VEC])
    act(CACT.ap[:, :, 0], VEC.ap[:, V_C:V_C + 8], AF.Silu, [VEC], [CACT])
    act(CACT.ap[:, :, 1], VEC.ap[:, V_C:V_C + 8], AF.Silu, [VEC], [CACT])
    for l in range(depth):
        act(ANEG.ap[:, l * 64:(l + 1) * 64], ROWB.ap[:, l * 160 + 64:l * 160 + 128], AF.Exp, [ROWB], [ANEG])
    ts("dve", ANEG.ap, ANEG.ap, -1.0, ALU.mult, [ANEG], [ANEG])
    WA = [arena.take("wa%d" % i, [128, 8, 1024], F32) for i in range(2)]
    wai = 0
    for l in range(depth):
        bk = nextbank()
        for q in range(9):
            wa = WA[wai % 2]
            dma("sp", wa.ap, w_ada_d[l].rearrange("(k p) n -> p k n", p=128)[:, :, q * 1024:(q + 1) * 1024],
                [], [wa.buf], ("wa", wai % 2))
            wai += 1
            for mc in range(8):
                col = (q * 8 + mc) * 2
                for kc in range(8):
                    mm(bk, bk.ap[:, col:col + 2], wa.ap[:, kc, mc * 128:(mc + 1) * 128], CACT.ap[:, kc, :],
                       kc == 0, kc == 7, [wa, CACT])
        vb = V_L0 + l * V_LSZ
        tt("dve", MODS.ap[:, l * 72:(l + 1) * 72], bk.ap[:, 0:144:2], VEC.ap[:, vb + V_BADA:vb + V_BADA + 72], ALU.add,
           [bk, VEC], [MODS])
        m0 = l * 72
        for si, (nrm, half) in enumerate(((V_N1, True), (V_N2, False), (V_N3, True))):
            sh = MODS.ap[:, m0 + si * 24:m0 + si * 24 + 8]
            sc = MODS.ap[:, m0 + si * 24 + 8:m0 + si * 24 + 16]
            g = MODS.ap[:, m0 + si * 24 + 16:m0 + si * 24 + 24]
            d0 = l * 72 + si * 24
            stt(DER.ap[:, d0:d0 + 8], sc, 1.0, ALU.add, VEC.ap[:, vb + nrm:vb + nrm + 8], ALU.mult, [MODS, VEC], [DER])
            cp("dve", DER.ap[:, d0 + 8:d0 + 16], sh, [MODS], [DER])
            ts("dve", DER.ap[:, d0 + 16:d0 + 24], g, 0.5 if half else 1.0, ALU.mult, [MODS], [DER])

    def dcol(l, si, which, fc):
        c = l * 72 + si * 24 + which * 8 + fc
        return DER.ap[:, c:c + 1]

    CW = 2048
    ceng = ["dve", "act", "pool"]
    cast_cnt = [0]

    def cast_list(l, keys):
        out = []
        for k in keys:
            src, dst = wsrc[k][l], wbf[k][l]
            Kd, Nd = src.shape
            for kc in range(Kd // 128):
                for c0 in range(0, Nd, CW):
                    out.append((src, dst, kc, c0, min(CW, Nd - c0)))
        return out

    def do_cast(desc, r32, rbf, engs=("dve", "act", "pool"), steng="pool"):
        src, dst, kc, c0, w = desc
        s32, sem32 = r32.next()
        sbf, sembf = rbf.next()
        dma("sp", s32.ap[:, 0:w], src[kc * 128:(kc + 1) * 128, c0:c0 + w], [], [s32.buf], sem32)
        cp(engs[cast_cnt[0] % len(engs)], sbf.ap[:, 0:w], s32.ap[:, 0:w], [s32], [sbf])
        dma(steng, dst[kc * 128:(kc + 1) * 128, c0:c0 + w], sbf.ap[:, 0:w], [sbf.buf], [], sembf)
        cast_cnt[0] += 1

    def do_diag(l, fc, ring):
        vb_ = V_L0 + l * V_LSZ
        sv, sem = ring.next()
        for k in range(KCF):
            ts("dve", sv.ap[:, k, :], IDENT, vcol(vb_ + V_CCW + k * 8 + fc), ALU.mult, [CONST, VEC], [sv])
        dma("pool", diagc_d[l, fc].rearrange("p (k j) -> p k j", j=128), sv.ap, [sv.buf], [], sem)

    cst32 = Ring(arena, "cst32", 4, [128, CW], F32)
    cstbf = Ring(arena, "cstbf", 4, [128, CW], BF16)
    for l in range(depth):
        for desc in cast_list(l, ("f1u", "f1d", "win", "pa", "pb", "wo", "f2u", "f2d")):
            do_cast(desc, cst32, cstbf)
    deferred_cast = []
    deferred_diag = []
    P.barrier()

    def slab_up(key, l, j0, nj):
        def f(slot):
            w = wbf[key][l].rearrange("(k p) n -> p k n", p=128)
            dst = slot.ap.bitcast(BF16)[:, 0:8 * 1024].rearrange("p (k t c) -> p k t c", t=2, c=512)
            return [(dst[:, :, 0, 0:nj * 128], w[:, :, j0 * 128:(j0 + nj) * 128]),
                    (dst[:, :, 1, 0:nj * 128], w[:, :, FF + j0 * 128:FF + (j0 + nj) * 128])]
        return f

    def slab_cols(key, l, c0, ncols, KC):
        def f(slot):
            w = wbf[key][l].rearrange("(k p) n -> p k n", p=128)
            dst = slot.ap.bitcast(BF16)[:, 0:KC * ncols].rearrange("p (k c) -> p k c", c=ncols)
            if KC > 8:
                h = KC // 2
                return [(dst[:, 0:h, :], w[:, 0:h, c0:c0 + ncols]), (dst[:, h:KC, :], w[:, h:KC, c0:c0 + ncols])]
            return [(dst, w[:, :, c0:c0 + ncols])]
        return f

    def slab_pair(key, l, c0a, c0b, ncols):
        def f(slot):
            w = wbf[key][l].rearrange("(k p) n -> p k n", p=128)
            dst = slot.ap.bitcast(BF16)[:, 0:8 * 2 * ncols].rearrange("p (k t c) -> p k t c", t=2, c=ncols)
            return [(dst[:, :, 0, :], w[:, :, c0a:c0a + ncols]), (dst[:, :, 1, :], w[:, :, c0b:c0b + ncols])]
        return f

    def slab_diag(l, fc):
        def f(slot):
            raise RuntimeError("generator slab")

        def gen(slot):
            vb_ = V_L0 + l * V_LSZ
            dst = slot.ap.bitcast(BF16)[:, 0:KCF * 128].rearrange("p (k j) -> p k j", j=128)
            for k in range(KCF):
                ts("dve", dst[:, k, :], IDENTB, vcol(vb_ + V_CCW + k * 8 + fc), ALU.mult, [CBF, VEC], [slot])
        f.gen = gen
        return f

    def wview(slot, KC, ncols):
        return slot.ap.bitcast(BF16)[:, 0:KC * ncols].rearrange("p (k c) -> p k c", c=ncols)

    def wview2(slot, ncols):
        return slot.ap.bitcast(BF16)[:, 0:8 * 2 * ncols].rearrange("p (k t c) -> p k t c", t=2, c=ncols)

    def rstd_from(bk, RSTD, TMP, nfeat):
        act(TMP.ap, bk.ap, AF.Ln, [bk], [TMP], bias=EPS, scale=1.0 / nfeat)
        act(bk.ap, TMP.ap, AF.Exp, [TMP], [bk], scale=-0.5)

    def norm_mod(Hs, U, l, si, cm):
        bk = nextbank()
        for fc in range(8):
            sq, _ = cm["sqb"].next()
            act(sq.ap, Hs.ap[:, fc, :], AF.Square, [Hs], [sq])
            mm(bk, bk.ap, ONESB, sq.ap, fc == 0, fc == 7, [sq, CBF])
        rstd_from(bk, cm["rstd"], cm["tmp0"], D)
        for fc in range(8):
            t, _ = cm["tmpn"].next()
            stt(t.ap, Hs.ap[:, fc, :], dcol(l, si, 0, fc), ALU.mult, bk.ap, ALU.mult, [Hs, bk, DER], [t])
            act(U.ap[:, fc, :], t.ap, AF.Identity, [t, DER], [U], bias=dcol(l, si, 1, fc))

    def ffn(Hs, U, HID, l, si, ws, cm):
        j = 0
        while j < 22:
            nj = min(4, 22 - j)
            slot = ws.next()
            wv = wview2(slot, 512)
            for jj in range(nj):
                ba, bb = nextbank(), nextbank()
                for kc in range(8):
                    mm(ba, ba.ap, wv[:, kc, 0, jj * 128:(jj + 1) * 128], U.ap[:, kc, :], kc == 0, kc == 7, [slot, U])
                for kc in range(8):
                    mm(bb, bb.ap, wv[:, kc, 1, jj * 128:(jj + 1) * 128], U.ap[:, kc, :], kc == 0, kc == 7, [slot, U])
                sa, _ = cm["silu"].next()
                act(sa.ap, ba.ap, AF.Silu, [ba], [sa])
                tt("dve", HID.ap[:, j + jj, :], sa.ap, bb.ap, ALU.mult, [sa, bb], [HID])
            j += nj
        for m0 in range(0, 8, 2):
            slot = ws.next()
            wv = wview(slot, 22, 256)
            for mm_ in range(2):
                mc = m0 + mm_
                bk = nextbank()
                for kc in range(22):
                    mm(bk, bk.ap, wv[:, kc, mm_ * 128:(mm_ + 1) * 128], HID.ap[:, kc, :], kc == 0, kc == 21, [slot, HID])
                stt(Hs.ap[:, mc, :], bk.ap, dcol(l, si, 2, mc), ALU.mult, Hs.ap[:, mc, :], ALU.add, [bk, Hs, DER], [Hs])

    def ffn_sched(key_u, key_d, l):
        s = []
        j = 0
        while j < 22:
            nj = min(4, 22 - j)
            s.append(slab_up(key_u, l, j, nj))
            j += nj
        for m0 in range(0, 8, 2):
            s.append(slab_cols(key_d, l, m0 * 128, 256, 22))
        return s

    def common(ar, nh=2):
        cm = {}
        cm["H32"] = [ar.take("h32_%d" % i, [128, 8, 512], F32) for i in range(nh)]
        cm["U"] = ar.take("u", [128, 8, 512], BF16)
        cm["HID"] = ar.take("hid", [128, 22, 512], BF16)
        cm["sqb"] = Ring(ar, "sqb", 3, [128, 512], BF16)
        cm["rstd"] = None
        cm["tmp0"] = ar.take("tmp0", [128, 512], F32)
        cm["tmpn"] = Ring(ar, "tmpn", 2, [128, 512], F32)
        cm["silu"] = Ring(ar, "silu", 3, [128, 512], F32)
        cm["wslots"] = [ar.take("wslot%d" % i, [128, 16384], U8) for i in range(3)]
        return cm

    def sweep_A(l):
        arena.reset()
        cm = common(arena)
        XT = arena.take("xt", [128, 4, 1024], F32)
        st32 = Ring(arena, "st32", 2, [128, 4, 512], F32)
        stbf = Ring(arena, "stbf", 2, [128, 4, 512], BF16)
        stdt = Ring(arena, "stdt", 2, [128, 4, 64], F32)
        if l == 0 and (deferred_cast or deferred_diag):
            c32 = Ring(arena, "c32A", 2, [128, CW], F32)
            cbf = Ring(arena, "cbfA", 2, [128, CW], BF16)
            dgr = Ring(arena, "dgsA", 1, [128, KCF, 128], BF16)
            per_tile = (len(deferred_cast) + NT - 1) // NT
            per_tile_d = (len(deferred_diag) + NT - 1) // NT
        ws = WStream(cm["wslots"], "wA")
        for i in range(NT):
            s = ffn_sched("f1u", "f1d", l)
            for q in range(4):
                s.append(slab_cols("win", l, q * 512, 512, 8))
            for q in range(6):
                s.append(slab_cols("win", l, OFF_XBC + q * 512, 512, 8))
            s.append(slab_cols("win", l, OFF_DT, 64, 8))
            for q in range(2):
                s.append(slab_pair("win", l, OFF_GLU + q * 512, OFF_GLU + D + q * 512, 512))
            for q in range(4):
                s.append(slab_cols("win", l, OFF_GATE + q * 512, 512, 8))
            ws.extend(s)
        U, HID = cm["U"], cm["HID"]

        def load_h(i):
            Hs = cm["H32"][i % 2]
            t0 = i * 512
            if l == 0:
                dma("sp", XT.ap, x_d.rearrange("(n p) f -> p n f", p=128)[:, i * 4:(i + 1) * 4, :], [], [XT.buf], "xt")
                for fc in range(8):
                    bk = nextbank()
                    for q in range(4):
                        tr(bk, bk.ap[:, q * 128:(q + 1) * 128], XT.ap[:, q, fc * 128:(fc + 1) * 128], IDENT, [XT, CONST])
                    cp("act" if fc % 2 else "dve", Hs.ap[:, fc, :], bk.ap, [bk], [Hs])
            else:
                dma("sp", Hs.ap, fm(HN_d)[:, :, t0:t0 + 512], [], [Hs.buf], ("h32", i % 2), nsplit=2)

        load_h(0)
        for i in range(NT):
            Hs = cm["H32"][i % 2]
            t0 = i * 512
            norm_mod(Hs, U, l, 0, cm)
            if i + 1 < NT:
                load_h(i + 1)
            ffn(Hs, U, HID, l, 0, ws, cm)
            dma("pool", fm(H1_d)[:, :, t0:t0 + 512], Hs.ap, [Hs.buf], [], ("h1st", i % 2), nsplit=2)
            if l == 0 and (deferred_cast or deferred_diag):
                for _ in range(per_tile):
                    if deferred_cast:
                        do_cast(deferred_cast.pop(0), c32, cbf, engs=("dve", "act"), steng="sp")
                for _ in range(per_tile_d):
                    if deferred_diag:
                        do_diag(*deferred_diag.pop(0), dgr)
            norm_mod(Hs, U, l, 1, cm)
            for q in range(4):
                slot = ws.next()
                wv = wview(slot, 8, 512)
                sv, sem = st32.next()
                for mm_ in range(4):
                    bk = nextbank()
                    for kc in range(8):
                        mm(bk, bk.ap, wv[:, kc, mm_ * 128:(mm_ + 1) * 128], U.ap[:, kc, :], kc == 0, kc == 7, [slot, U])
                    act(sv.ap[:, mm_, :], bk.ap, AF.Silu, [bk], [sv])
                dma("pool", fm(ZS_d)[:, q * 4:(q + 1) * 4, t0:t0 + 512], sv.ap, [sv.buf], [], sem)
            for q in range(6):
                slot = ws.next()
                wv = wview(slot, 8, 512)
                sv, sem = stbf.next()
                for mm_ in range(4):
                    bk = nextbank()
                    for kc in range(8):
                        mm(bk, bk.ap, wv[:, kc, mm_ * 128:(mm_ + 1) * 128], U.ap[:, kc, :], kc == 0, kc == 7, [slot, U])
                    cp("dve", sv.ap[:, mm_, :], bk.ap, [bk], [sv])
                dma("pool", fm(XBCP_d)[:, q * 4:(q + 1) * 4, t0:t0 + 512], sv.ap, [sv.buf], [], sem)
            slot = ws.next()
            wv = wview(slot, 8, 64)
            bk = nextbank()
            for q in range(4):
                for kc in range(8):
                    mm(bk, bk.ap[:, q * 64:(q + 1) * 64], U.ap[:, kc, q * 128:(q + 1) * 128], wv[:, kc, :], kc == 0, kc == 7,
                       [slot, U])
            sv, sem = stdt.next()
            cp("dve", sv.ap, bk.ap[:, 0:256].rearrange("p (q c) -> p q c", c=64), [bk], [sv])
            dma("pool", DT_d.rearrange("(n p) c -> p n c", p=128)[:, i * 4:(i + 1) * 4, :], sv.ap, [sv.buf], [], sem)
            for q in range(2):
                slot = ws.next()
                wv = wview2(slot, 512)
                sv, sem = stbf.next()
                for mm_ in range(4):
                    ba, bg = nextbank(), nextbank()
                    for kc in range(8):
                        mm(ba, ba.ap, wv[:, kc, 0, mm_ * 128:(mm_ + 1) * 128], U.ap[:, kc, :], kc == 0, kc == 7, [slot, U])
                    for kc in range(8):
                        mm(bg, bg.ap, wv[:, kc, 1, mm_ * 128:(mm_ + 1) * 128], U.ap[:, kc, :], kc == 0, kc == 7, [slot, U])
                    sg, _ = cm["silu"].next()
                    act(sg.ap, bg.ap, AF.Sigmoid, [bg], [sg])
                    tt("dve", sv.ap[:, mm_, :], sg.ap, ba.ap, ALU.mult, [sg, ba], [sv])
                dma("pool", fm(VP_d)[:, q * 4:(q + 1) * 4, t0:t0 + 512], sv.ap, [sv.buf], [], sem)
            for q in range(4):
                slot = ws.next()
                wv = wview(slot, 8, 512)
                sv, sem = st32.next()
                for mm_ in range(4):
                    bk = nextbank()
                    for kc in range(8):
                        mm(bk, bk.ap, wv[:, kc, mm_ * 128:(mm_ + 1) * 128], U.ap[:, kc, :], kc == 0, kc == 7, [slot, U])
                    act(sv.ap[:, mm_, :], bk.ap, AF.Sigmoid, [bk], [sv])
                dst = GA_d if q < 2 else GB_d
                qq = q % 2
                dma("pool", fm(dst)[:, qq * 4:(qq + 1) * 4, t0:t0 + 512], sv.ap, [sv.buf], [], sem)
        P.barrier()

    def sweep_S(l, d):
        fwd = d == 0
        arena.reset()
        vb = V_L0 + l * V_LSZ
        XS2 = [arena.take("xsbc%d" % i, [128, 24, 512], BF16) for i in range(2)]
        DTT = arena.take("dtt", [128, 4, 64], F32)
        DTX = arena.take("dtx", [128, 4, 32], F32)
        DTP2 = [arena.take("dtp%d" % i, [128, 4, 32], F32) for i in range(2)]
        DA2 = [arena.take("da%d" % i, [128, 4, 32], F32) for i in range(2)]
        DAH2 = [arena.take("dah%d" % i, [128, 4, 32], BF16) for i in range(2)]
        DAL2 = [arena.take("dal%d" % i, [128, 4, 32], BF16) for i in range(2)]
        XSDT = Ring(arena, "xsdt", 2, [128, 2048], BF16)
        BTOK = Ring(arena, "btok", 2, [128, 4, 128], BF16)
        CBTS = Ring(arena, "cbts", 2, [128, 512], BF16)
        SMALL = Ring(arena, "small", 3, [128, 5, 32], F32)
        ERING = Ring(arena, "e", 5, [128, 4, 128], BF16)
        GRING = Ring(arena, "g", 6, [128, 4, 128], BF16)
        XSD = Ring(arena, "xsd", 4, [128, 512], BF16)
        T1R = Ring(arena, "t1", 2, [128, 512], F32)
        T2R = Ring(arena, "t2", 2, [128, 512], F32)
        HT = Ring(arena, "ht", 4, [128, 512], F32)
        H32s = arena.take("hstate", [128, 2048], F32)
        HBF = arena.take("hbf", [128, 2048], BF16)
        if fwd:
            XP = arena.take("xp", [128, 24, 516], BF16)
            DG = arena.take("dg", [128, 24, KS, 128], BF16)
            DX = Ring(arena, "dx", 2, [128, 2048], F32)
            YST = Ring(arena, "yst", 3, [128, 512], F32)
        else:
            YFW = Ring(arena, "yfw", 2, [128, 2048], F32)
            YFM2 = [arena.take("yfm%d" % i, [128, 16, 512], F32) for i in range(2)]
        TRI = TRIU if fwd else TRIL
        TRIB = TRIUB if fwd else TRILB
        NEGM = NEGFB if fwd else NEGBB
        rot = [0]
        RB = [banks[0], banks[1]]

        def rb():
            b_ = RB[rot[0] % 2]
            rot[0] += 1
            return b_
        BK_A = [banks[2], banks[7], banks[5], banks[3]]
        BK_Y1 = [banks[4], banks[6]]
        HG = [Buf("hg%d" % g_) for g_ in range(4)]
        HBG = [Buf("hbg%d" % g_) for g_ in range(4)]
        dtb = ROWB.ap[:, l * 160 + d * 32:l * 160 + d * 32 + 32]
        aneg = ANEG.ap[:, l * 64 + d * 32:l * 64 + d * 32 + 32]
        dsk = ROWB.ap[:, l * 160 + 128:l * 160 + 160]

        P.add("dve", lambda h: h.memset(H32s.ap, 0.0), [], HG)
        P.add("dve", lambda h: h.memset(HBF.ap, 0.0), [], HBG)
        if fwd:
            for j in range(24):
                for k in range(KS):
                    ts("dve" if (j + k) % 2 else "pool", DG.ap[:, j, k, :], IDENT, vcol(vb + V_SCW + k * 24 + j), ALU.mult,
                       [CONST, VEC], [DG])
        tiles = list(range(NT)) if fwd else list(range(NT - 1, -1, -1))
        chunks = [0, 1, 2, 3] if fwd else [3, 2, 1, 0]
        seq = [(ti, i, q) for ti, i in enumerate(tiles) for q in chunks]
        ctxs = {}

        def tile_load(ti, i):
            t0 = i * 512
            XS = XS2[ti % 2]
            DTP, DA, DAH, DAL = DTP2[ti % 2], DA2[ti % 2], DAH2[ti % 2], DAL2[ti % 2]
            dma("sp", DTT.ap, DT_d.rearrange("(n p) c -> p n c", p=128)[:, i * 4:(i + 1) * 4, :], [], [DTT.buf], "dtt")
            if fwd:
                lo, hi = t0 - 2, t0 + 514
                clo, chi = max(lo, 0), min(hi, ntok)
                if clo > lo:
                    memset("dve", XP, XP.ap[:, :, 0:clo - lo], 0.0)
                if chi < hi:
                    memset("dve", XP, XP.ap[:, :, 516 - (hi - chi):516], 0.0)
                dma("sp", XP.ap[:, :, clo - lo:chi - lo], fm(XBCP_d)[:, :, clo:chi], [], [XP.buf], "xp", nsplit=3)
                if t0 + 512 == link:
                    ts("dve", XP.ap[:, :, 514:516], XP.ap[:, :, 514:516], FLAG.ap[:, 0:1], ALU.mult, [XP, FLAG], [XP])
            else:
                dma("sp", XS.ap, fm(XSBC_d)[:, :, t0:t0 + 512], [], [XS.buf], ("xs", ti % 2), nsplit=3)
            tt("dve", DTX.ap, DTT.ap[:, :, d * 32:d * 32 + 32], dtb.unsqueeze(1).broadcast_to([128, 4, 32]), ALU.add,
               [DTT, ROWB], [DTX])
            act(DTX.ap, DTX.ap, AF.Exp, [DTX], [DTX])
            act(DTP.ap, DTX.ap, AF.Ln, [DTX], [DTP], bias=1.0)
            tt("dve", DA.ap, DTP.ap, aneg.unsqueeze(1).broadcast_to([128, 4, 32]), ALU.mult, [DTP, ANEG], [DA])
            cp("dve", DAH.ap, DA.ap, [DA], [DAH])
            tt("dve", DAL.ap, DA.ap, DAH.ap, ALU.subtract, [DA, DAH], [DAL])
            if fwd:
                for j in range(24):
                    bk = rb()
                    for k in range(KS):
                        mm(bk, bk.ap, DG.ap[:, j, k, :], XP.ap[:, j, k:k + 512], k == 0, k == KS - 1, [DG, XP])
                    act(XS.ap[:, j, :], bk.ap, AF.Silu, [bk, VEC], [XS], bias=vcol(vb + V_SCB + j))
                dma("pool", fm(XSBC_d)[:, :, t0:t0 + 512], XS.ap, [XS.buf], [], ("xsst", ti % 2), nsplit=3)

        def prep(n, piece):
            ti, i, q = seq[n]
            XS = XS2[ti % 2]
            DTP, DA = DTP2[ti % 2], DA2[ti % 2]
            tq = slice(q * 128, (q + 1) * 128)
            tok0 = i * 512 + q * 128
            if piece == 0:
                c = {}
                c.update(XS=XS, DAH=DAH2[ti % 2], DAL=DAL2[ti % 2], q=q, tq=tq, tok0=tok0, ti=ti, i=i)
                c["g"] = {}
                ctxs[n] = c
                sm, _ = SMALL.next()
                c["sm"] = sm
                NACS, EACS, ETOT, DEC, TMPS = [sm.ap[:, k_, :] for k_ in range(5)]
                bs = rb()
                mm(bs, bs.ap[:, 0:32], TRI, DA.ap[:, q, :], True, True, [CONST, DA])
                mm(bs, bs.ap[:, 32:64], ONES, DA.ap[:, q, :], True, True, [CONST, DA])
                ts("dve", NACS, bs.ap[:, 0:32], -1.0, ALU.mult, [bs], [sm])
                act(EACS, bs.ap[:, 0:32], AF.Exp, [bs], [sm])
                act(ETOT, bs.ap[:, 32:64], AF.Exp, [bs], [sm])
                tt("dve", TMPS, bs.ap[:, 32:64], NACS, ALU.add, [bs, sm], [sm])
                act(DEC, TMPS, AF.Exp, [sm], [sm])
                xsdt, _ = XSDT.next()
                c["xsdt"] = xsdt
                if fwd:
                    dx, _ = DX.next()
                    c["dx"] = dx
                return
            c = ctxs[n]
            if piece in (1, 2):
                half = piece - 1
                if (not fwd) and half == 0:
                    yfw, semy = YFW.next()
                    dma("sp", yfw.ap, YF_d[tok0:tok0 + 128, :], [], [yfw.buf], semy)
                    c["yfw"] = yfw
                xsdt = c["xsdt"]
                bk = rb()
                bkb = bk.ap.bitcast(BF16)
                for jj in range(8):
                    tr(bk, bkb[:, jj * 128:(jj + 1) * 128], XS.ap[:, half * 8 + jj, tq], IDENTB, [XS, CBF])
                src = bkb.rearrange("p (h e) -> p h e", e=64)
                tt("dve", xsdt.ap[:, half * 1024:(half + 1) * 1024].rearrange("p (h e) -> p h e", e=64), src,
                   bc(DTP.ap[:, q, half * 16:(half + 1) * 16], 64), ALU.mult, [bk, DTP], [xsdt])
                if fwd:
                    dx = c["dx"]
                    tt("dve", dx.ap[:, half * 1024:(half + 1) * 1024].rearrange("p (h e) -> p h e", e=64), src,
                       bc(dsk[:, half * 16:(half + 1) * 16], 64), ALU.mult, [bk, ROWB], [dx])
                return
            btok, _ = BTOK.next()
            c["btok"] = btok
            bk = rb()
            bkb = bk.ap.bitcast(BF16)
            for g in range(4):
                tr(bk, bkb[:, g * 128:(g + 1) * 128], XS.ap[:, 16 + g, tq], IDENTB, [XS, CBF])
            cp("act", btok.ap, bkb[:, 0:512].rearrange("p (g n) -> p g n", n=128), [bk], [btok])
            bcb = rb()
            for g in range(4):
                mm(bcb, bcb.ap[:, g * 128:(g + 1) * 128], XS.ap[:, 16 + g, tq], XS.ap[:, 20 + g, tq], True, True, [XS])
            cbt, _ = CBTS.next()
            cp("act", cbt.ap, bcb.ap, [bcb], [cbt])
            c["cbt"] = cbt

        def abc(n, s):
            c = ctxs[n]
            q, sm, xsdt = c["q"], c["sm"], c["xsdt"]
            NACS, EACS, ETOT, DEC, TMPS = [sm.ap[:, k_, :] for k_ in range(5)]
            if s % 2 == 0:
                g = s // 2
                gs = slice(g * 512, (g + 1) * 512)
                xsd, _ = XSD.next()
                tt("pool", xsd.ap.rearrange("p (h e) -> p h e", e=64), xsdt.ap[:, gs].rearrange("p (h e) -> p h e", e=64),
                   bc(DEC[:, g * 8:(g + 1) * 8], 64), ALU.mult, [xsdt, sm], [xsd])
                ht, _ = HT.next()
                P.add("pool", (lambda h, o=ht.ap.rearrange("p (h e) -> p h e", e=64),
                               i0=H32s.ap[:, gs].rearrange("p (h e) -> p h e", e=64),
                               i1=bc(ETOT[:, g * 8:(g + 1) * 8], 64): h.tensor_tensor(out=o, in0=i0, in1=i1, op=ALU.mult)),
                      [HG[g], sm.buf], [ht.buf])
                c["xsd%d" % g] = xsd
                c["ht%d" % g] = ht
            e_, _ = ERING.next()
            g_, _ = GRING.next()
            g = s // 2
            ba = BK_A[(n * 8 + s) % 4]
            for hq in range(4):
                h_ = s * 4 + hq
                o_ = ba.ap[:, hq * 128:(hq + 1) * 128]
                P.add("pe", (lambda h, o=o_, l_=c["DAH"].ap[:, q, h_:h_ + 1].broadcast_to([128, 128]):
                             h.matmul(o, l_, TRIB, start=True, stop=False)), [c["DAH"].buf, CBF.buf], [ba.buf])
                P.add("pe", (lambda h, o=o_, l_=c["DAL"].ap[:, q, h_:h_ + 1].broadcast_to([128, 128]):
                             h.matmul(o, l_, TRIB, start=False, stop=False)), [c["DAL"].buf, CBF.buf], [ba.buf])
                P.add("pe", (lambda h, o=o_: h.matmul(o, IDENTB, NEGM, start=False, stop=True)), [CBF.buf], [ba.buf])
            for hq in range(4):
                h_ = s * 4 + hq
                P.add("act", (lambda h, o=e_.ap[:, hq, :], i_=ba.ap[:, hq * 128:(hq + 1) * 128], b_=NACS[:, h_:h_ + 1]:
                              h.activation(out=o, in_=i_, func=AF.Exp, bias=b_)), [ba.buf, sm.buf], [e_.buf])
            tt("dve", g_.ap, e_.ap, c["cbt"].ap[:, g * 128:(g + 1) * 128].unsqueeze(1).broadcast_to([128, 4, 128]), ALU.mult,
               [e_, c["cbt"]], [g_])
            c["g"][s] = g_

        def y1(n, s):
            c = ctxs[n]
            xsdt = c["xsdt"]
            g = s // 2
            by1 = BK_Y1[(n * 4 + g) % 2]
            for hq in range(4):
                h_ = s * 4 + hq
                hh = h_ % 8
                g_ = c["g"][s]
                mm(by1, by1.ap[:, hh * 64:(hh + 1) * 64], g_.ap[:, hq, :], xsdt.ap[:, h_ * 64:(h_ + 1) * 64], True, True,
                   [g_, xsdt])
            del c["g"][s]

        def epiB(n, g):
            c = ctxs[n]
            sm, tok0 = c["sm"], c["tok0"]
            EACS = sm.ap[:, 1, :]
            gs = slice(g * 512, (g + 1) * 512)
            by1 = BK_Y1[(n * 4 + g) % 2]
            BK_Y2, BK_ST = rb(), rb()
            P.add("pe", (lambda h, o=BK_Y2.ap, l_=c["XS"].ap[:, 20 + g, c["tq"]], r_=HBF.ap[:, gs]:
                         h.matmul(o, l_, r_, start=True, stop=True)), [c["XS"].buf, HBG[g]], [BK_Y2.buf])
            xsd = c["xsd%d" % g]
            mm(BK_ST, BK_ST.ap, c["btok"].ap[:, g, :], xsd.ap, True, True, [c["btok"], xsd])
            t1, _ = T1R.next()
            tt("dve", t1.ap.rearrange("p (h e) -> p h e", e=64), BK_Y2.ap.rearrange("p (h e) -> p h e", e=64),
               bc(EACS[:, g * 8:(g + 1) * 8], 64), ALU.mult, [BK_Y2, sm], [t1])
            t2, _ = T2R.next()
            tt("dve", t2.ap, t1.ap, by1.ap, ALU.add, [t1, by1], [t2])
            ht = c["ht%d" % g]
            P.add("dve", (lambda h, o=H32s.ap[:, gs], i0=ht.ap, i1=BK_ST.ap: h.tensor_tensor(out=o, in0=i0, in1=i1, op=ALU.add)),
                  [ht.buf, BK_ST.buf], [HG[g]])
            if (not fwd) and tok0 == link:
                P.add("dve", (lambda h, o=H32s.ap[:, gs]: h.tensor_scalar(out=o, in0=o, scalar1=FLAG.ap[:, 0:1], scalar2=None,
                                                                        op0=ALU.mult)), [HG[g], FLAG.buf], [HG[g]])
            if fwd:
                ys, semy = YST.next()
                tt("pool", ys.ap, t2.ap, c["dx"].ap[:, gs], ALU.add, [t2, c["dx"]], [ys])
                dma("sp", YF_d[tok0:tok0 + 128, gs], ys.ap, [ys.buf], [], semy)
            else:
                tt("pool", t2.ap, t2.ap, c["yfw"].ap[:, gs], ALU.add, [t2, c["yfw"]], [t2])
                c["t2_%d" % g] = t2

        def epiC(n, g):
            c = ctxs[n]
            gs = slice(g * 512, (g + 1) * 512)
            P.add("act", (lambda h, o=HBF.ap[:, gs], i_=H32s.ap[:, gs]: h.activation(out=o, in_=i_, func=AF.Copy)),
                  [HG[g]], [HBG[g]])
            if not fwd:
                YFM = YFM2[c["ti"] % 2]
                t2 = c["t2_%d" % g]
                bt = rb()
                for fb in range(4):
                    tr(bt, bt.ap[:, fb * 128:(fb + 1) * 128], t2.ap[:, fb * 128:(fb + 1) * 128], IDENT, [t2, CONST])
                cp("act", YFM.ap[:, g * 4:(g + 1) * 4, c["tq"]], bt.ap.rearrange("p (f t) -> p f t", t=128), [bt], [YFM])
            if g == 3:
                if (not fwd) and c["q"] == chunks[-1]:
                    t0 = c["i"] * 512
                    dma("sp", fm(Y_d)[:, :, t0:t0 + 512], YFM2[c["ti"] % 2].ap, [YFM2[c["ti"] % 2].buf], [],
                        ("yfmst", c["ti"] % 2), nsplit=2)

        tile_load(0, tiles[0])
        for pc in range(4):
            prep(0, pc)
        flat = [(n, s) for n in range(len(seq)) for s in range(8)]
        pend = []

        def run_due(idx):
            keep = []
            for due, fn in pend:
                if due <= idx:
                    fn()
                else:
                    keep.append((due, fn))
            pend[:] = keep

        LAG = 3
        for idx, (n, s) in enumerate(flat):
            abc(n, s)
            run_due(idx)
            pend.append((idx + LAG, (lambda n=n, s=s: y1(n, s))))
            if s % 2 == 1:
                g = s // 2
                pend.append((idx + LAG, (lambda n=n, g=g: epiB(n, g))))
                pend.append((idx + LAG + 1, (lambda n=n, g=g: epiC(n, g))))
            if n + 1 < len(seq) and s in (0, 2, 4, 6):
                if s == 0 and seq[n + 1][0] != seq[n][0]:
                    tile_load(seq[n + 1][0], seq[n + 1][1])
                prep(n + 1, s // 2)
        for k in range(len(flat), len(flat) + 6):
            run_due(k)
        assert not pend
        P.barrier()

    def sweep_C(l):
        arena.reset()
        last = l == depth - 1
        vb = V_L0 + l * V_LSZ
        cm = common(arena, nh=1)
        BIGT = arena.take("bigt", [128, 16, 512], F32)
        CV = arena.take("cv", [128, 8, 512], F32)
        RSTD2 = arena.take("rstd2", [128, 512], F32)
        TMP02 = arena.take("tmp02", [128, 512], F32)
        VPT = arena.take("vpt", [128, 8, 542], BF16)
        LY = Ring(arena, "ly", 2, [128, 512], F32)
        LZ = Ring(arena, "lz", 2, [128, 512], F32)
        LG = Ring(arena, "lg", 3, [128, 512], F32)
        U, HID = cm["U"], cm["HID"]
        hid_flat = HID.ap.rearrange("p a b -> p (a b)")
        YN = V(hid_flat[:, 0:16 * 512].rearrange("p (a b) -> p a b", b=512), HID.buf)
        MB = V(hid_flat[:, 0:8 * 512].rearrange("p (a b) -> p a b", b=512), HID.buf)
        GY = BIGT
        M1 = V(BIGT.ap[:, 8:16, :], BIGT.buf)
        ws = WStream(cm["wslots"], "wC")
        for i in range(NT):
            s = []
            for fc in range(8):
                s.append(slab_diag(l, fc))
            for m0 in range(0, 8, 2):
                s.append(slab_cols("pa", l, m0 * 128, 256, 16))
            for q in range(2):
                s.append(slab_cols("pb", l, q * 512, 512, 8))
            for q in range(2):
                s.append(slab_cols("wo", l, q * 512, 512, 8))
            s += ffn_sched("f2u", "f2d", l)
            ws.extend(s)

        for i in range(NT):
            t0 = i * 512
            Hs = cm["H32"][0]
            dma("sp", Hs.ap, fm(H1_d)[:, :, t0:t0 + 512], [], [Hs.buf], ("h32", 0), nsplit=2)
            lo, hi = t0 - 15, t0 + 527
            clo, chi = max(lo, 0), min(hi, ntok)
            if clo > lo:
                memset("dve", VPT, VPT.ap[:, :, 0:clo - lo], 0.0)
            if chi < hi:
                memset("dve", VPT, VPT.ap[:, :, 542 - (hi - chi):542], 0.0)
            dma("sp", VPT.ap[:, :, clo - lo:chi - lo], fm(VP_d)[:, :, clo:chi], [], [VPT.buf], "vpt")
            if t0 + 512 == link:
                ts("dve", VPT.ap[:, :, 527:542], VPT.ap[:, :, 527:542], FLAG.ap[:, 0:1], ALU.mult, [VPT, FLAG], [VPT])
            for f2 in range(16):
                ly, sy = LY.next()
                lz, sz = LZ.next()
                dma("sp", ly.ap, fm(Y_d)[:, f2, t0:t0 + 512], [], [ly.buf], sy)
                dma("sp", lz.ap, fm(ZS_d)[:, f2, t0:t0 + 512], [], [lz.buf], sz)
                tt("pool", GY.ap[:, f2, :], ly.ap, lz.ap, ALU.mult, [ly, lz], [GY])
            for fc in range(8):
                slot = ws.next()
                wv = slot.ap.bitcast(BF16)[:, 0:KCF * 128].rearrange("p (k j) -> p k j", j=128)
                bk = nextbank()
                for k in range(KCF):
                    mm(bk, bk.ap, wv[:, k, :], VPT.ap[:, fc, k:k + 512], k == 0, k == KCF - 1, [slot, VPT])
                act(CV.ap[:, fc, :], bk.ap, AF.Identity, [bk, VEC], [CV], bias=vcol(vb + V_CCB + fc))
            bk = nextbank()
            for fc in range(16):
                sq, _ = cm["sqb"].next()
                act(sq.ap, GY.ap[:, fc, :], AF.Square, [GY], [sq])
                mm(bk, bk.ap, ONESB, sq.ap, fc == 0, fc == 15, [sq, CBF])
            rstd_from(bk, cm["rstd"], cm["tmp0"], DI)
            for fc in range(16):
                stt(YN.ap[:, fc, :], GY.ap[:, fc, :], vcol(vb + V_SNRM + fc), ALU.mult, bk.ap, ALU.mult,
                    [GY, bk, VEC], [YN])
            b1, b2 = nextbank(), nextbank()
            for fc in range(8):
                mm(b1, b1.ap, ONES, CV.ap[:, fc, :], fc == 0, fc == 7, [CV, CONST])
            for fc in range(8):
                sq, _ = cm["sqb"].next()
                act(sq.ap, CV.ap[:, fc, :], AF.Square, [CV], [sq])
                mm(b2, b2.ap, ONESB, sq.ap, fc == 0, fc == 7, [sq, CBF])
            MEAN, _ = cm["tmpn"].next()
            act(MEAN.ap, b1.ap, AF.Copy, [b1], [MEAN], scale=1.0 / D)
            MSQ, _ = cm["tmpn"].next()
            tt("dve", MSQ.ap, MEAN.ap, MEAN.ap, ALU.mult, [MEAN], [MSQ])
            stt(MSQ.ap, b2.ap, 1.0 / D, ALU.mult, MSQ.ap, ALU.subtract, [b2, MSQ], [MSQ])
            act(TMP02.ap, MSQ.ap, AF.Ln, [MSQ], [TMP02], bias=EPS)
            act(RSTD2.ap, TMP02.ap, AF.Exp, [TMP02], [RSTD2], scale=-0.5)
            for m0 in range(0, 8, 2):
                slot = ws.next()
                wv = wview(slot, 16, 256)
                for mm_ in range(2):
                    mc = m0 + mm_
                    bk = nextbank()
                    for kc in range(16):
                        mm(bk, bk.ap, wv[:, kc, mm_ * 128:(mm_ + 1) * 128], YN.ap[:, kc, :], kc == 0, kc == 15, [slot, YN])
                    lg, sg = LG.next()
                    dma("sp", lg.ap, fm(GA_d)[:, mc, t0:t0 + 512], [], [lg.buf], sg)
                    tt("dve", M1.ap[:, mc, :], lg.ap, bk.ap, ALU.mult, [lg, bk], [M1])
            for fc in range(8):
                t, _ = cm["silu"].next()
                tt("dve", t.ap, CV.ap[:, fc, :], MEAN.ap, ALU.subtract, [CV, MEAN], [t])
                stt(t.ap, t.ap, vcol(vb + V_LNG + fc), ALU.mult, RSTD2.ap, ALU.mult, [t, RSTD2, VEC], [t])
                act(U.ap[:, fc, :], t.ap, AF.Silu, [t, VEC], [U], bias=vcol(vb + V_LNB + fc))
            for q in range(2):
                slot = ws.next()
                wv = wview(slot, 8, 512)
                for mm_ in range(4):
                    mc = q * 4 + mm_
                    bk = nextbank()
                    for kc in range(8):
                        mm(bk, bk.ap, wv[:, kc, mm_ * 128:(mm_ + 1) * 128], U.ap[:, kc, :], kc == 0, kc == 7, [slot, U])
                    lg, sg = LG.next()
                    dma("sp", lg.ap, fm(GB_d)[:, mc, t0:t0 + 512], [], [lg.buf], sg)
                    t, _ = cm["silu"].next()
                    tt("dve", t.ap, lg.ap, bk.ap, ALU.mult, [lg, bk], [t])
                    tt("pool", MB.ap[:, mc, :], t.ap, M1.ap[:, mc, :], ALU.add, [t, M1], [MB])
            for q in range(2):
                slot = ws.next()
                wv = wview(slot, 8, 512)
                for mm_ in range(4):
                    mc = q * 4 + mm_
                    bk = nextbank()
                    for kc in range(8):
                        mm(bk, bk.ap, wv[:, kc, mm_ * 128:(mm_ + 1) * 128], MB.ap[:, kc, :], kc == 0, kc == 7, [slot, MB])
                    stt(Hs.ap[:, mc, :], bk.ap, dcol(l, 1, 2, mc), ALU.mult, Hs.ap[:, mc, :], ALU.add, [bk, Hs, DER], [Hs])
            norm_mod(Hs, U, l, 2, cm)
            ffn(Hs, U, HID, l, 2, ws, cm)
            if not last:
                dma("pool", fm(HN_d)[:, :, t0:t0 + 512], Hs.ap, [Hs.buf], [], ("hnst", i % 2), nsplit=2)
            else:
                bk = nextbank()
                for fc in range(8):
                    sq, _ = cm["sqb"].next()
                    act(sq.ap, Hs.ap[:, fc, :], AF.Square, [Hs], [sq])
                    mm(bk, bk.ap, ONESB, sq.ap, fc == 0, fc == 7, [sq, CBF])
                rstd_from(bk, cm["rstd"], cm["tmp0"], D)
                OUTF = CV
                for fc in range(8):
                    stt(OUTF.ap[:, fc, :], Hs.ap[:, fc, :], vcol(V_FN + fc), ALU.mult, bk.ap, ALU.mult,
                        [Hs, bk, VEC], [OUTF])
                OT = V(BIGT.ap[:, 8:16, :].rearrange("p a b -> p (a b)").rearrange("p (q f) -> p q f", f=1024), BIGT.buf)
                for q in range(4):
                    for half in range(2):
                        bk = nextbank()
                        for ff in range(4):
                            fc = half * 4 + ff
                            tr(bk, bk.ap[:, ff * 128:(ff + 1) * 128], OUTF.ap[:, fc, q * 128:(q + 1) * 128], IDENT,
                               [OUTF, CONST])
                        cp("act" if half else "dve", OT.ap[:, q, half * 512:(half + 1) * 512], bk.ap, [bk], [OT])
                dma("pool", y_d.rearrange("(n p) f -> p n f", p=128)[:, i * 4:(i + 1) * 4, :], OT.ap, [OT.buf], [], "yout")
        P.barrier()

    for l in range(depth):
        sweep_A(l)
        sweep_S(l, 0)
        sweep_S(l, 1)
        sweep_C(l)
    P.barrier()
    for e in ENGS:
        P.add(e, lambda h: h.nop())
    with nc.Block() as block:
        P.emit(nc, block)
    return nc, P


def make_consts():
    c = np.zeros((128, 768), np.float32)
    c[:, 0:128] = np.eye(128, dtype=np.float32)
    c[:, 128:256] = np.triu(np.ones((128, 128), np.float32))
    c[:, 256:384] = np.tril(np.ones((128, 128), np.float32))
    c[:, 384:512] = -30000.0 * np.tril(np.ones((128, 128), np.float32), -1)
    c[:, 512:640] = -30000.0 * np.triu(np.ones((128, 128), np.float32), 1)
    c[:, 640:768] = 1.0
    return c


def make_vecs(c, inp):
    rows = np.zeros((V_ROWS, 128), np.float32)
    rows[V_C:V_C + 8] = np.asarray(c, np.float32).reshape(8, 128)
    rows[V_FN:V_FN + 8] = np.asarray(inp["final_norm"], np.float32).reshape(8, 128)
    for l in range(DEPTH):
        b = V_L0 + l * V_LSZ
        rows[b + V_BADA:b + V_BADA + 72] = inp["b_ada"][l].reshape(72, 128)
        rows[b + V_N1:b + V_N1 + 8] = inp["ffn1_norm"][l].reshape(8, 128)
        rows[b + V_N2:b + V_N2 + 8] = inp["mix_norm"][l].reshape(8, 128)
        rows[b + V_N3:b + V_N3 + 8] = inp["ffn2_norm"][l].reshape(8, 128)
        rows[b + V_SCW:b + V_SCW + 120] = inp["ssm_conv_w"][l].reshape(KS * 24, 128)
        rows[b + V_SCB:b + V_SCB + 24] = inp["ssm_conv_b"][l].reshape(24, 128)
        rows[b + V_SNRM:b + V_SNRM + 16] = inp["ssm_norm"][l].reshape(16, 128)
        rows[b + V_CCW:b + V_CCW + 248] = inp["conf_conv_w"][l].reshape(KCF * 8, 128)
        rows[b + V_CCB:b + V_CCB + 8] = inp["conf_conv_b"][l].reshape(8, 128)
        rows[b + V_LNG:b + V_LNG + 8] = inp["conf_ln_g"][l].reshape(8, 128)
        rows[b + V_LNB:b + V_LNB + 8] = inp["conf_ln_b"][l].reshape(8, 128)
    return rows


def make_rowb(inp):
    r = np.zeros((DEPTH, 160), np.float32)
    for l in range(DEPTH):
        r[l, 0:64] = inp["dt_bias"][l].reshape(64)
        r[l, 64:128] = inp["a_log"][l].reshape(64)
        r[l, 128:160] = inp["d_skip"][l].reshape(32)
    return np.ascontiguousarray(np.broadcast_to(r.reshape(1, DEPTH * 160), (128, DEPTH * 160)))


_CACHE = {}


def kernel(**inputs):
    inp = {k: np.asarray(v) for k, v in inputs.items()}
    if "nc" not in _CACHE:
        _CACHE["nc"] = build()[0]
    nc = _CACHE["nc"]
    xp, xs = inp["x_prompt"], inp["x_sample"]
    consts = make_consts()
    rowb = make_rowb(inp)
    shared = {"consts": consts, "rowb": rowb, "w_ada": inp["w_ada"],
              "ffn1_up": inp["ffn1_up"], "ffn1_down": inp["ffn1_down"], "w_in": inp["w_in"],
              "w_proj_ssd": inp["w_proj_ssd"], "w_proj_conv": inp["w_proj_conv"], "w_out": inp["w_out"],
              "ffn2_up": inp["ffn2_up"], "ffn2_down": inp["ffn2_down"]}
    in_maps = []
    for core in range(NCORES):
        m = dict(shared)
        if core < 4:
            m["x"] = np.ascontiguousarray(xp[core])
            m["vecs"] = make_vecs(inp["c_prompt"][core], inp)
            m["flag"] = np.ones((128, 1), np.float32)
        else:
            b = core - 4
            xx = np.zeros((NTOK_FULL, D), np.float32)
            xx[:LINK_FULL] = xs[b]
            m["x"] = xx
            m["vecs"] = make_vecs(inp["c_sample"][b], inp)
            m["flag"] = np.zeros((128, 1), np.float32)
        in_maps.append(m)
    res = run_bass_kernel_spmd(nc, in_maps, core_ids=list(range(NCORES)))
    y_prompt = np.stack([np.asarray(res.results[c]["y"], np.float32) for c in range(4)], axis=0)
    y_sample = np.stack([np.asarray(res.results[4 + b]["y"], np.float32)[:LINK_FULL] for b in range(4)], axis=0)
    return (y_prompt, y_sample)
```

```python
import numpy as np
import concourse.bass as bass
import concourse.mybir as mybir
from concourse.bass_utils import run_bass_kernel_spmd

F32 = mybir.dt.float32
BF16 = mybir.dt.bfloat16
U8 = mybir.dt.uint8
AF = mybir.ActivationFunctionType
ALU = mybir.AluOpType

D = 1024
DI = 2048
NH = 32
HP = 64
NGRP = 4
NS = 128
KS = 5
CD = 3072
KCF = 31
FF = 2816
IN_COLS = 9280
OFF_XBC = 2048
OFF_DT = 5120
OFF_GLU = 5184
OFF_GATE = 7232
DEPTH = 2
EPS = 1e-6
NCORES = 8
NTOK_FULL = 8192
LINK_FULL = 4096

V_C = 0
V_FN = 8
V_L0 = 16
V_LSZ = 528
V_BADA, V_N1, V_N2, V_N3, V_SCW, V_SCB, V_SNRM, V_CCW, V_CCB, V_LNG, V_LNB = 0, 72, 80, 88, 96, 216, 240, 256, 504, 512, 520
V_ROWS = 1152

ENGS = ("pe", "act", "dve", "pool", "sp")


class Buf:
    __slots__ = ("name", "last_w", "readers")

    def __init__(self, name):
        self.name = name
        self.last_w = None
        self.readers = {}


class V:
    __slots__ = ("ap", "buf")

    def __init__(self, ap, buf):
        self.ap = ap
        self.buf = buf


class Op:
    __slots__ = ("eng", "fn", "deps", "signals", "idx", "is_dma", "sem", "val", "ninc")

    def __init__(self, eng, fn, is_dma, ninc):
        self.eng = eng
        self.fn = fn
        self.deps = None
        self.signals = False
        self.idx = 0
        self.is_dma = is_dma
        self.sem = None
        self.val = 0
        self.ninc = ninc


class Prog:
    def __init__(self):
        self.ops = {e: [] for e in ENGS}
        self.pending = {e: [] for e in ENGS}
        self.dma_sem_vals = {}
        self.last_dma = {}
        self.nops = 0

    def add(self, eng, fn, reads=(), writes=(), dma_sem=None, ninc=1):
        is_dma = dma_sem is not None
        op = Op(eng, fn, is_dma, ninc)
        deps = {}

        def need(d, raw=False):
            if d is None:
                return
            if d.is_dma:
                k = ("d", d.sem)
                if k not in deps or deps[k].val < d.val:
                    deps[k] = d
            else:
                if d.eng == eng and (not is_dma) and (eng == "pe" or not raw):
                    return
                k = ("e", d.eng)
                if k not in deps or deps[k].idx < d.idx:
                    deps[k] = d

        for b in reads:
            need(b.last_w, True)
        for b in writes:
            need(b.last_w)
            for r in b.readers.values():
                need(r)
        for d in self.pending[eng]:
            need(d)
        self.pending[eng] = []
        if is_dma:
            need(self.last_dma.get(dma_sem))
            v = self.dma_sem_vals.get(dma_sem, 0) + 16 * ninc
            self.dma_sem_vals[dma_sem] = v
            op.sem = dma_sem
            op.val = v
            self.last_dma[dma_sem] = op
        op.deps = list(deps.values())
        op.idx = len(self.ops[eng])
        self.ops[eng].append(op)
        for b in reads:
            b.readers[id(op) if is_dma else eng] = op
        for b in writes:
            b.last_w = op
            b.readers = {}
        self.nops += 1
        return op

    def barrier(self):
        evs = []
        for e in ENGS:
            for o in reversed(self.ops[e]):
                if not o.is_dma:
                    evs.append(o)
                    break
        for o in self.last_dma.values():
            evs.append(o)
        for e in ENGS:
            self.pending[e] = list(evs)

    def emit(self, nc, block):
        for e in ENGS:
            for op in self.ops[e]:
                for d in op.deps:
                    if not d.is_dma:
                        d.signals = True
        for e in ENGS:
            c = 0
            for op in self.ops[e]:
                if not op.is_dma and op.signals:
                    c += 1
                    op.val = c
        esem = {e: nc.alloc_semaphore("sem_" + e) for e in ENGS}
        dsem = {k: nc.alloc_semaphore("dsem_%d" % i) for i, k in enumerate(self.dma_sem_vals)}

        def make(e):
            ops = self.ops[e]

            def body(h):
                seen = {}
                for op in ops:
                    for d in op.deps:
                        if d.is_dma:
                            key, sem, val = ("d", d.sem), dsem[d.sem], d.val
                        else:
                            key, sem, val = ("e", d.eng), esem[d.eng], d.val
                        if seen.get(key, 0) >= val:
                            continue
                        seen[key] = val
                        h.wait_ge(sem, val)
                    ins = op.fn(h)
                    if op.is_dma:
                        if not isinstance(ins, (list, tuple)):
                            ins = [ins]
                        assert len(ins) == op.ninc, (len(ins), op.ninc)
                        for i_ in ins:
                            i_.then_inc(dsem[op.sem], 16)
                    elif op.signals:
                        ins.then_inc(esem[e], 1)
            return body

        block.tensor(make("pe"))
        block.scalar(make("act"))
        block.vector(make("dve"))
        block.gpsimd(make("pool"))
        block.sync(make("sp"))


class Arena:
    def __init__(self, big, lo, hi):
        self.big, self.lo, self.hi, self.off = big, lo, hi, lo

    def reset(self):
        self.off = self.lo

    def take(self, name, shape, dtype, buf=None):
        esz = 4 if dtype == F32 else (1 if dtype == U8 else 2)
        n = 1
        for s in shape[1:]:
            n *= s
        nb = n * esz
        off = (self.off + 63) // 64 * 64
        assert off + nb <= self.hi, ("SBUF arena overflow", name, off + nb, self.hi)
        self.off = off + nb
        ap = self.big[:, off:off + nb].bitcast(dtype)
        if len(shape) == 3:
            ap = ap.rearrange("p (a b) -> p a b", b=shape[2])
        elif len(shape) == 4:
            ap = ap.rearrange("p (a b c) -> p a b c", b=shape[2], c=shape[3])
        return V(ap, buf if buf is not None else Buf(name))


def bc(ap, n):
    return ap.unsqueeze(len(ap.shape)).broadcast_to(list(ap.shape) + [n])


def build(ntok=NTOK_FULL, link=LINK_FULL, depth=DEPTH, debug=False):
    assert ntok % 512 == 0 and link % 512 == 0
    NT = ntok // 512
    nc = bass.Bass("TRN2", target_bir_lowering=False)
    P = Prog()

    def din(name, shape, dt=F32):
        return nc.dram_tensor(name, list(shape), dt, kind="ExternalInput").ap()

    def dscr(name, shape, dt):
        kind = "ExternalOutput" if (debug and name in debug) else "Internal"
        return nc.dram_tensor(name, list(shape), dt, kind=kind).ap()

    x_d = din("x", [ntok, D])
    flag_d = din("flag", [128, 1])
    vecs_d = din("vecs", [V_ROWS, 128])
    consts_d = din("consts", [128, 768])
    rowb_d = din("rowb", [128, DEPTH * 160])
    w_ada_d = din("w_ada", [DEPTH, D, 9 * D])
    wsrc = {
        "f1u": din("ffn1_up", [DEPTH, D, 2 * FF]),
        "f1d": din("ffn1_down", [DEPTH, FF, D]),
        "win": din("w_in", [DEPTH, D, IN_COLS]),
        "pa": din("w_proj_ssd", [DEPTH, DI, D]),
        "pb": din("w_proj_conv", [DEPTH, D, D]),
        "wo": din("w_out", [DEPTH, D, D]),
        "f2u": din("ffn2_up", [DEPTH, D, 2 * FF]),
        "f2d": din("ffn2_down", [DEPTH, FF, D]),
    }
    y_d = nc.dram_tensor("y", [ntok, D], F32, kind="ExternalOutput").ap()

    wbf = {k: dscr("wb_" + k, list(v.shape), BF16) for k, v in wsrc.items()}
    diagc_d = dscr("diagc", [DEPTH, 8, 128, KCF * 128], BF16)
    H1_d = dscr("H1", [D, ntok], F32)
    HN_d = dscr("HN", [D, ntok], F32)
    ZS_d = dscr("ZS", [DI, ntok], F32)
    XBCP_d = dscr("XBCP", [CD, ntok], BF16)
    XSBC_d = dscr("XSBC", [CD, ntok], BF16)
    DT_d = dscr("DT", [ntok, 64], F32)
    VP_d = dscr("VP", [D, ntok], BF16)
    GA_d = dscr("GA", [D, ntok], F32)
    GB_d = dscr("GB", [D, ntok], F32)
    YF_d = dscr("YF", [ntok, DI], F32)
    Y_d = dscr("Y", [DI, ntok], F32)

    def fm(ap):
        return ap.rearrange("(c p) t -> p c t", p=128)

    avail = nc.sbuf_bytes_remaining() if callable(nc.sbuf_bytes_remaining) else nc.sbuf_bytes_remaining
    SB_BYTES = (int(avail) // 64) * 64 - 256
    big = nc.alloc_sbuf_tensor("big", [128, SB_BYTES], U8)
    PS = nc.alloc_psum_tensor("ps", [128, 8, 512], F32)
    banks = [V(PS[:, i, :], Buf("bank%d" % i)) for i in range(8)]
    bank_ctr = [0]

    def nextbank():
        b = banks[bank_ctr[0] % 8]
        bank_ctr[0] += 1
        return b

    pers = Arena(big, 0, 16384)
    CONST = pers.take("const", [128, 768], F32)
    IDENT, TRIU, TRIL, NEGF32, NEGB32, ONES = [CONST.ap[:, i * 128:(i + 1) * 128] for i in range(6)]
    CBF = pers.take("constbf", [128, 768], BF16)
    IDENTB, NEGFB, NEGBB, TRIUB, TRILB, ONESB = [CBF.ap[:, i * 128:(i + 1) * 128] for i in range(6)]
    VEC = pers.take("vec", [128, V_ROWS], F32)
    ROWB = pers.take("rowb", [128, DEPTH * 160], F32)
    ANEG = pers.take("aneg", [128, DEPTH * 64], F32)
    MODS = pers.take("mods", [128, DEPTH * 72], F32)
    DER = pers.take("der", [128, DEPTH * 72], F32)
    CACT = pers.take("cact", [128, 8, 2], F32)
    FLAG = pers.take("flag", [128, 1], F32)
    arena = Arena(big, 16384, SB_BYTES)

    def vcol(r):
        return VEC.ap[:, r:r + 1]

    def dma(eng, out, in_, reads, writes, sem, nsplit=1, split_axis=1):
        if nsplit == 1:
            P.add(eng, lambda h: h.dma_start(out=out, in_=in_), reads, writes, dma_sem=sem)
            return
        n = out.shape[split_axis]
        step = (n + nsplit - 1) // nsplit
        pieces = []
        for a in range(0, n, step):
            b = min(n, a + step)
            idx = [slice(None)] * len(out.shape)
            idx[split_axis] = slice(a, b)
            pieces.append((out[tuple(idx)], in_[tuple(idx)]))

        def fn(h):
            return [h.dma_start(out=o, in_=i) for o, i in pieces]
        P.add(eng, fn, reads, writes, dma_sem=sem, ninc=len(pieces))

    def mm(outv, out_ap, lhsT, rhs, start, stop, reads):
        P.add("pe", lambda h: h.matmul(out_ap, lhsT, rhs, start=start, stop=stop),
              [r.buf for r in reads], [outv.buf])

    def tr(outv, out_ap, in_ap, ident, reads):
        P.add("pe", lambda h: h.transpose(out_ap, in_ap, ident), [r.buf for r in reads], [outv.buf])

    def act(out_ap, in_ap, func, reads, writes, bias=None, scale=None):
        kw = {}
        if bias is not None:
            kw["bias"] = bias
        if scale is not None:
            kw["scale"] = scale
        P.add("act", lambda h: h.activation(out=out_ap, in_=in_ap, func=func, **kw),
              [r.buf for r in reads], [w.buf for w in writes])

    def tt(eng, out_ap, in0, in1, op, reads, writes):
        P.add(eng, lambda h: h.tensor_tensor(out=out_ap, in0=in0, in1=in1, op=op),
              [r.buf for r in reads], [w.buf for w in writes])

    def ts(eng, out_ap, in0, s1, op0, reads, writes, s2=None, op1=None):
        if op1 is None:
            P.add(eng, lambda h: h.tensor_scalar(out=out_ap, in0=in0, scalar1=s1, scalar2=None, op0=op0),
                  [r.buf for r in reads], [w.buf for w in writes])
        else:
            P.add(eng, lambda h: h.tensor_scalar(out=out_ap, in0=in0, scalar1=s1, scalar2=s2, op0=op0, op1=op1),
                  [r.buf for r in reads], [w.buf for w in writes])

    def stt(out_ap, in0, scalar, op0, in1, op1, reads, writes):
        P.add("dve", lambda h: h.scalar_tensor_tensor(out=out_ap, in0=in0, scalar=scalar, op0=op0, in1=in1, op1=op1),
              [r.buf for r in reads], [w.buf for w in writes])

    def cp(eng, out_ap, in_ap, reads, writes):
        if eng == "act":
            act(out_ap, in_ap, AF.Copy, reads, writes)
        else:
            P.add(eng, lambda h: h.tensor_copy(out=out_ap, in_=in_ap),
                  [r.buf for r in reads], [w.buf for w in writes])

    def memset(eng, v, ap, val):
        P.add(eng, lambda h: h.memset(ap, val), [], [v.buf])

    class Ring:
        def __init__(self, ar, name, n, shape, dtype):
            self.vs = [ar.take("%s%d" % (name, i), shape, dtype) for i in range(n)]
            self.i = 0
            self.name = name

        def next(self):
            k = self.i % len(self.vs)
            self.i += 1
            return self.vs[k], (self.name, k)

    class WStream:
        def __init__(self, slots, name):
            self.slots, self.name = slots, name
            self.sched, self.loaded, self.cur = [], 0, 0

        def extend(self, items):
            self.sched.extend(items)

        def _load(self, k):
            slot = self.slots[k % len(self.slots)]
            if hasattr(self.sched[k], "gen"):
                self.sched[k].gen(slot)
                return
            pieces = self.sched[k](slot)

            def fn(h):
                return [h.dma_start(out=o, in_=i) for o, i in pieces]
            P.add("sp", fn, [], [slot.buf], dma_sem=(self.name, k % len(self.slots)), ninc=len(pieces))

        def next(self):
            k = self.cur
            self.cur += 1
            want = min(len(self.sched), k + len(self.slots))
            while self.loaded < want:
                self._load(self.loaded)
                self.loaded += 1
            return self.slots[k % len(self.slots)]

    arena.reset()
    dma("sp", CONST.ap, consts_d, [], [CONST.buf], "c0")
    dma("sp", ROWB.ap, rowb_d, [], [ROWB.buf], "c1")
    dma("sp", FLAG.ap, flag_d, [], [FLAG.buf], "c2")
    VST = arena.take("vst", [128, 9, 128], F32)
    dma("sp", VST.ap, vecs_d.rearrange("(b p) f -> p b f", p=128), [], [VST.buf], "c3")
    cp("dve", CBF.ap[:, 0:128], IDENT, [CONST], [CBF])
    cp("dve", CBF.ap[:, 128:256], NEGF32, [CONST], [CBF])
    cp("dve", CBF.ap[:, 256:384], NEGB32, [CONST], [CBF])
    cp("dve", CBF.ap[:, 384:512], TRIU, [CONST], [CBF])
    cp("dve", CBF.ap[:, 512:640], TRIL, [CONST], [CBF])
    cp("dve", CBF.ap[:, 640:768], ONES, [CONST], [CBF])
    for b0 in range(0, 9, 4):
        nb = min(4, 9 - b0)
        bk = nextbank()
        for b in range(nb):
            tr(bk, bk.ap[:, b * 128:(b + 1) * 128], VST.ap[:, b0 + b, :], IDENT, [VST, CONST])
        cp("dve", VEC.ap[:, b0 * 128:(b0 + nb) * 128], bk.ap[:, 0:nb * 128], [bk], [VEC])
    act(CACT.ap[:, :, 0], VEC.ap[:, V_C:V_C + 8], AF.Silu, [VEC], [CACT])
    act(CACT.ap[:, :, 1], VEC.ap[:, V_C:V_C + 8], AF.Silu, [VEC], [CACT])
    for l in range(depth):
        act(ANEG.ap[:, l * 64:(l + 1) * 64], ROWB.ap[:, l * 160 + 64:l * 160 + 128], AF.Exp, [ROWB], [ANEG])
    ts("dve", ANEG.ap, ANEG.ap, -1.0, ALU.mult, [ANEG], [ANEG])
    WA = [arena.take("wa%d" % i, [128, 8, 1024], F32) for i in range(2)]
    wai = 0
    for l in range(depth):
        bk = nextbank()
        for q in range(9):
            wa = WA[wai % 2]
            dma("sp", wa.ap, w_ada_d[l].rearrange("(k p) n -> p k n", p=128)[:, :, q * 1024:(q + 1) * 1024],
                [], [wa.buf], ("wa", wai % 2))
            wai += 1
            for mc in range(8):
                col = (q * 8 + mc) * 2
                for kc in range(8):
                    mm(bk, bk.ap[:, col:col + 2], wa.ap[:, kc, mc * 128:(mc + 1) * 128], CACT.ap[:, kc, :],
                       kc == 0, kc == 7, [wa, CACT])
        vb = V_L0 + l * V_LSZ
        tt("dve", MODS.ap[:, l * 72:(l + 1) * 72], bk.ap[:, 0:144:2], VEC.ap[:, vb + V_BADA:vb + V_BADA + 72], ALU.add,
           [bk, VEC], [MODS])
        m0 = l * 72
        for si, (nrm, half) in enumerate(((V_N1, True), (V_N2, False), (V_N3, True))):
            sh = MODS.ap[:, m0 + si * 24:m0 + si * 24 + 8]
            sc = MODS.ap[:, m0 + si * 24 + 8:m0 + si * 24 + 16]
            g = MODS.ap[:, m0 + si * 24 + 16:m0 + si * 24 + 24]
            d0 = l * 72 + si * 24
            stt(DER.ap[:, d0:d0 + 8], sc, 1.0, ALU.add, VEC.ap[:, vb + nrm:vb + nrm + 8], ALU.mult, [MODS, VEC], [DER])
            cp("dve", DER.ap[:, d0 + 8:d0 + 16], sh, [MODS], [DER])
            ts("dve", DER.ap[:, d0 + 16:d0 + 24], g, 0.5 if half else 1.0, ALU.mult, [MODS], [DER])

    def dcol(l, si, which, fc):
        c = l * 72 + si * 24 + which * 8 + fc
        return DER.ap[:, c:c + 1]

    CW = 2048
    ceng = ["dve", "act", "pool"]
    cast_cnt = [0]

    def cast_list(l, keys):
        out = []
        for k in keys:
            src, dst = wsrc[k][l], wbf[k][l]
            Kd, Nd = src.shape
            for kc in range(Kd // 128):
                for c0 in range(0, Nd, CW):
                    out.append((src, dst, kc, c0, min(CW, Nd - c0)))
        return out

    def do_cast(desc, r32, rbf, engs=("dve", "act", "pool"), steng="pool"):
        src, dst, kc, c0, w = desc
        s32, sem32 = r32.next()
        sbf, sembf = rbf.next()
        dma("sp", s32.ap[:, 0:w], src[kc * 128:(kc + 1) * 128, c0:c0 + w], [], [s32.buf], sem32)
        cp(engs[cast_cnt[0] % len(engs)], sbf.ap[:, 0:w], s32.ap[:, 0:w], [s32], [sbf])
        dma(steng, dst[kc * 128:(kc + 1) * 128, c0:c0 + w], sbf.ap[:, 0:w], [sbf.buf], [], sembf)
        cast_cnt[0] += 1

    def do_diag(l, fc, ring):
        vb_ = V_L0 + l * V_LSZ
        sv, sem = ring.next()
        for k in range(KCF):
            ts("dve", sv.ap[:, k, :], IDENT, vcol(vb_ + V_CCW + k * 8 + fc), ALU.mult, [CONST, VEC], [sv])
        dma("pool", diagc_d[l, fc].rearrange("p (k j) -> p k j", j=128), sv.ap, [sv.buf], [], sem)

    cst32 = Ring(arena, "cst32", 4, [128, CW], F32)
    cstbf = Ring(arena, "cstbf", 4, [128, CW], BF16)
    for l in range(depth):
        for desc in cast_list(l, ("f1u", "f1d", "win", "pa", "pb", "wo", "f2u", "f2d")):
            do_cast(desc, cst32, cstbf)
    deferred_cast = []
    deferred_diag = []
    P.barrier()

    def slab_up(key, l, j0, nj):
        def f(slot):
            w = wbf[key][l].rearrange("(k p) n -> p k n", p=128)
            dst = slot.ap.bitcast(BF16)[:, 0:8 * 1024].rearrange("p (k t c) -> p k t c", t=2, c=512)
            return [(dst[:, :, 0, 0:nj * 128], w[:, :, j0 * 128:(j0 + nj) * 128]),
                    (dst[:, :, 1, 0:nj * 128], w[:, :, FF + j0 * 128:FF + (j0 + nj) * 128])]
        return f

    def slab_cols(key, l, c0, ncols, KC):
        def f(slot):
            w = wbf[key][l].rearrange("(k p) n -> p k n", p=128)
            dst = slot.ap.bitcast(BF16)[:, 0:KC * ncols].rearrange("p (k c) -> p k c", c=ncols)
            if KC > 8:
                h = KC // 2
                return [(dst[:, 0:h, :], w[:, 0:h, c0:c0 + ncols]), (dst[:, h:KC, :], w[:, h:KC, c0:c0 + ncols])]
            return [(dst, w[:, :, c0:c0 + ncols])]
        return f

    def slab_pair(key, l, c0a, c0b, ncols):
        def f(slot):
            w = wbf[key][l].rearrange("(k p) n -> p k n", p=128)
            dst = slot.ap.bitcast(BF16)[:, 0:8 * 2 * ncols].rearrange("p (k t c) -> p k t c", t=2, c=ncols)
            return [(dst[:, :, 0, :], w[:, :, c0a:c0a + ncols]), (dst[:, :, 1, :], w[:, :, c0b:c0b + ncols])]
        return f

    def slab_diag(l, fc):
        def f(slot):
            raise RuntimeError("generator slab")

        def gen(slot):
            vb_ = V_L0 + l * V_LSZ
            dst = slot.ap.bitcast(BF16)[:, 0:KCF * 128].rearrange("p (k j) -> p k j", j=128)
            for k in range(KCF):
                ts("dve", dst[:, k, :], IDENTB, vcol(vb_ + V_CCW + k * 8 + fc), ALU.mult, [CBF, VEC], [slot])
        f.gen = gen
        return f

    def wview(slot, KC, ncols):
        return slot.ap.bitcast(BF16)[:, 0:KC * ncols].rearrange("p (k c) -> p k c", c=ncols)

    def wview2(slot, ncols):
        return slot.ap.bitcast(BF16)[:, 0:8 * 2 * ncols].rearrange("p (k t c) -> p k t c", t=2, c=ncols)

    def rstd_from(bk, RSTD, TMP, nfeat):
        act(TMP.ap, bk.ap, AF.Ln, [bk], [TMP], bias=EPS, scale=1.0 / nfeat)
        act(bk.ap, TMP.ap, AF.Exp, [TMP], [bk], scale=-0.5)

    def norm_mod(Hs, U, l, si, cm):
        bk = nextbank()
        for fc in range(8):
            sq, _ = cm["sqb"].next()
            act(sq.ap, Hs.ap[:, fc, :], AF.Square, [Hs], [sq])
            mm(bk, bk.ap, ONESB, sq.ap, fc == 0, fc == 7, [sq, CBF])
        rstd_from(bk, cm["rstd"], cm["tmp0"], D)
        for fc in range(8):
            t, _ = cm["tmpn"].next()
            stt(t.ap, Hs.ap[:, fc, :], dcol(l, si, 0, fc), ALU.mult, bk.ap, ALU.mult, [Hs, bk, DER], [t])
            act(U.ap[:, fc, :], t.ap, AF.Identity, [t, DER], [U], bias=dcol(l, si, 1, fc))

    def ffn(Hs, U, HID, l, si, ws, cm):
        j = 0
        while j < 22:
            nj = min(4, 22 - j)
            slot = ws.next()
            wv = wview2(slot, 512)
            for jj in range(nj):
                ba, bb = nextbank(), nextbank()
                for kc in range(8):
                    mm(ba, ba.ap, wv[:, kc, 0, jj * 128:(jj + 1) * 128], U.ap[:, kc, :], kc == 0, kc == 7, [slot, U])
                for kc in range(8):
                    mm(bb, bb.ap, wv[:, kc, 1, jj * 128:(jj + 1) * 128], U.ap[:, kc, :], kc == 0, kc == 7, [slot, U])
                sa, _ = cm["silu"].next()
                act(sa.ap, ba.ap, AF.Silu, [ba], [sa])
                tt("dve", HID.ap[:, j + jj, :], sa.ap, bb.ap, ALU.mult, [sa, bb], [HID])
            j += nj
        for m0 in range(0, 8, 2):
            slot = ws.next()
            wv = wview(slot, 22, 256)
            for mm_ in range(2):
                mc = m0 + mm_
                bk = nextbank()
                for kc in range(22):
                    mm(bk, bk.ap, wv[:, kc, mm_ * 128:(mm_ + 1) * 128], HID.ap[:, kc, :], kc == 0, kc == 21, [slot, HID])
                stt(Hs.ap[:, mc, :], bk.ap, dcol(l, si, 2, mc), ALU.mult, Hs.ap[:, mc, :], ALU.add, [bk, Hs, DER], [Hs])

    def ffn_sched(key_u, key_d, l):
        s = []
        j = 0
        while j < 22:
            nj = min(4, 22 - j)
            s.append(slab_up(key_u, l, j, nj))
            j += nj
        for m0 in range(0, 8, 2):
            s.append(slab_cols(key_d, l, m0 * 128, 256, 22))
        return s

    def common(ar, nh=2):
        cm = {}
        cm["H32"] = [ar.take("h32_%d" % i, [128, 8, 512], F32) for i in range(nh)]
        cm["U"] = ar.take("u", [128, 8, 512], BF16)
        cm["HID"] = ar.take("hid", [128, 22, 512], BF16)
        cm["sqb"] = Ring(ar, "sqb", 3, [128, 512], BF16)
        cm["rstd"] = None
        cm["tmp0"] = ar.take("tmp0", [128, 512], F32)
        cm["tmpn"] = Ring(ar, "tmpn", 2, [128, 512], F32)
        cm["silu"] = Ring(ar, "silu", 3, [128, 512], F32)
        cm["wslots"] = [ar.take("wslot%d" % i, [128, 16384], U8) for i in range(3)]
        return cm

    def sweep_A(l):
        arena.reset()
        cm = common(arena)
        XT = arena.take("xt", [128, 4, 1024], F32)
        st32 = Ring(arena, "st32", 2, [128, 4, 512], F32)
        stbf = Ring(arena, "stbf", 2, [128, 4, 512], BF16)
        stdt = Ring(arena, "stdt", 2, [128, 4, 64], F32)
        if l == 0 and (deferred_cast or deferred_diag):
            c32 = Ring(arena, "c32A", 2, [128, CW], F32)
            cbf = Ring(arena, "cbfA", 2, [128, CW], BF16)
            dgr = Ring(arena, "dgsA", 1, [128, KCF, 128], BF16)
            per_tile = (len(deferred_cast) + NT - 1) // NT
            per_tile_d = (len(deferred_diag) + NT - 1) // NT
        ws = WStream(cm["wslots"], "wA")
        for i in range(NT):
            s = ffn_sched("f1u", "f1d", l)
            for q in range(4):
                s.append(slab_cols("win", l, q * 512, 512, 8))
            for q in range(6):
                s.append(slab_cols("win", l, OFF_XBC + q * 512, 512, 8))
            s.append(slab_cols("win", l, OFF_DT, 64, 8))
            for q in range(2):
                s.append(slab_pair("win", l, OFF_GLU + q * 512, OFF_GLU + D + q * 512, 512))
            for q in range(4):
                s.append(slab_cols("win", l, OFF_GATE + q * 512, 512, 8))
            ws.extend(s)
        U, HID = cm["U"], cm["HID"]

        def load_h(i):
            Hs = cm["H32"][i % 2]
            t0 = i * 512
            if l == 0:
                dma("sp", XT.ap, x_d.rearrange("(n p) f -> p n f", p=128)[:, i * 4:(i + 1) * 4, :], [], [XT.buf], "xt")
                for fc in range(8):
                    bk = nextbank()
                    for q in range(4):
                        tr(bk, bk.ap[:, q * 128:(q + 1) * 128], XT.ap[:, q, fc * 128:(fc + 1) * 128], IDENT, [XT, CONST])
                    cp("act" if fc % 2 else "dve", Hs.ap[:, fc, :], bk.ap, [bk], [Hs])
            else:
                dma("sp", Hs.ap, fm(HN_d)[:, :, t0:t0 + 512], [], [Hs.buf], ("h32", i % 2), nsplit=2)

        load_h(0)
        for i in range(NT):
            Hs = cm["H32"][i % 2]
            t0 = i * 512
            norm_mod(Hs, U, l, 0, cm)
            if i + 1 < NT:
                load_h(i + 1)
            ffn(Hs, U, HID, l, 0, ws, cm)
            dma("pool", fm(H1_d)[:, :, t0:t0 + 512], Hs.ap, [Hs.buf], [], ("h1st", i % 2), nsplit=2)
            if l == 0 and (deferred_cast or deferred_diag):
                for _ in range(per_tile):
                    if deferred_cast:
                        do_cast(deferred_cast.pop(0), c32, cbf, engs=("dve", "act"), steng="sp")
                for _ in range(per_tile_d):
                    if deferred_diag:
                        do_diag(*deferred_diag.pop(0), dgr)
            norm_mod(Hs, U, l, 1, cm)
            for q in range(4):
                slot = ws.next()
                wv = wview(slot, 8, 512)
                sv, sem = st32.next()
                for mm_ in range(4):
                    bk = nextbank()
                    for kc in range(8):
                        mm(bk, bk.ap, wv[:, kc, mm_ * 128:(mm_ + 1) * 128], U.ap[:, kc, :], kc == 0, kc == 7, [slot, U])
                    act(sv.ap[:, mm_, :], bk.ap, AF.Silu, [bk], [sv])
                dma("pool", fm(ZS_d)[:, q * 4:(q + 1) * 4, t0:t0 + 512], sv.ap, [sv.buf], [], sem)
            for q in range(6):
                slot = ws.next()
                wv = wview(slot, 8, 512)
                sv, sem = stbf.next()
                for mm_ in range(4):
                    bk = nextbank()
                    for kc in range(8):
                        mm(bk, bk.ap, wv[:, kc, mm_ * 128:(mm_ + 1) * 128], U.ap[:, kc, :], kc == 0, kc == 7, [slot, U])
                    cp("dve", sv.ap[:, mm_, :], bk.ap, [bk], [sv])
                dma("pool", fm(XBCP_d)[:, q * 4:(q + 1) * 4, t0:t0 + 512], sv.ap, [sv.buf], [], sem)
            slot = ws.next()
            wv = wview(slot, 8, 64)
            bk = nextbank()
            for q in range(4):
                for kc in range(8):
                    mm(bk, bk.ap[:, q * 64:(q + 1) * 64], U.ap[:, kc, q * 128:(q + 1) * 128], wv[:, kc, :], kc == 0, kc == 7,
                       [slot, U])
            sv, sem = stdt.next()
            cp("dve", sv.ap, bk.ap[:, 0:256].rearrange("p (q c) -> p q c", c=64), [bk], [sv])
            dma("pool", DT_d.rearrange("(n p) c -> p n c", p=128)[:, i * 4:(i + 1) * 4, :], sv.ap, [sv.buf], [], sem)
            for q in range(2):
                slot = ws.next()
                wv = wview2(slot, 512)
                sv, sem = stbf.next()
                for mm_ in range(4):
                    ba, bg = nextbank(), nextbank()
                    for kc in range(8):
                        mm(ba, ba.ap, wv[:, kc, 0, mm_ * 128:(mm_ + 1) * 128], U.ap[:, kc, :], kc == 0, kc == 7, [slot, U])
                    for kc in range(8):
                        mm(bg, bg.ap, wv[:, kc, 1, mm_ * 128:(mm_ + 1) * 128], U.ap[:, kc, :], kc == 0, kc == 7, [slot, U])
                    sg, _ = cm["silu"].next()
                    act(sg.ap, bg.ap, AF.Sigmoid, [bg], [sg])
                    tt("dve", sv.ap[:, mm_, :], sg.ap, ba.ap, ALU.mult, [sg, ba], [sv])
                dma("pool", fm(VP_d)[:, q * 4:(q + 1) * 4, t0:t0 + 512], sv.ap, [sv.buf], [], sem)
            for q in range(4):
                slot = ws.next()
                wv = wview(slot, 8, 512)
                sv, sem = st32.next()
                for mm_ in range(4):
                    bk = nextbank()
                    for kc in range(8):
                        mm(bk, bk.ap, wv[:, kc, mm_ * 128:(mm_ + 1) * 128], U.ap[:, kc, :], kc == 0, kc == 7, [slot, U])
                    act(sv.ap[:, mm_, :], bk.ap, AF.Sigmoid, [bk], [sv])
                dst = GA_d if q < 2 else GB_d
                qq = q % 2
                dma("pool", fm(dst)[:, qq * 4:(qq + 1) * 4, t0:t0 + 512], sv.ap, [sv.buf], [], sem)
        P.barrier()

    def sweep_S(l, d):
        fwd = d == 0
        arena.reset()
        vb = V_L0 + l * V_LSZ
        XS2 = [arena.take("xsbc%d" % i, [128, 24, 512], BF16) for i in range(2)]
        DTT = arena.take("dtt", [128, 4, 64], F32)
        DTX = arena.take("dtx", [128, 4, 32], F32)
        DTP2 = [arena.take("dtp%d" % i, [128, 4, 32], F32) for i in range(2)]
        DA2 = [arena.take("da%d" % i, [128, 4, 32], F32) for i in range(2)]
        DAH2 = [arena.take("dah%d" % i, [128, 4, 32], BF16) for i in range(2)]
        DAL2 = [arena.take("dal%d" % i, [128, 4, 32], BF16) for i in range(2)]
        XSDT = Ring(arena, "xsdt", 2, [128, 2048], BF16)
        BTOK = Ring(arena, "btok", 2, [128, 4, 128], BF16)
        CBTS = Ring(arena, "cbts", 2, [128, 512], BF16)
        SMALL = Ring(arena, "small", 3, [128, 5, 32], F32)
        ERING = Ring(arena, "e", 5, [128, 4, 128], BF16)
        GRING = Ring(arena, "g", 6, [128, 4, 128], BF16)
        XSD = Ring(arena, "xsd", 4, [128, 512], BF16)
        T1R = Ring(arena, "t1", 2, [128, 512], F32)
        T2R = Ring(arena, "t2", 2, [128, 512], F32)
        HT = Ring(arena, "ht", 4, [128, 512], F32)
        H32s = arena.take("hstate", [128, 2048], F32)
        HBF = arena.take("hbf", [128, 2048], BF16)
        if fwd:
            XP = arena.take("xp", [128, 24, 516], BF16)
            DG = arena.take("dg", [128, 24, KS, 128], BF16)
            DX = Ring(arena, "dx", 2, [128, 2048], F32)
            YST = Ring(arena, "yst", 3, [128, 512], F32)
        else:
            YFW = Ring(arena, "yfw", 2, [128, 2048], F32)
            YFM2 = [arena.take("yfm%d" % i, [128, 16, 512], F32) for i in range(2)]
        TRI = TRIU if fwd else TRIL
        TRIB = TRIUB if fwd else TRILB
        NEGM = NEGFB if fwd else NEGBB
        rot = [0]
        RB = [banks[0], banks[1]]

        def rb():
            b_ = RB[rot[0] % 2]
            rot[0] += 1
            return b_
        BK_A = [banks[2], banks[7], banks[5], banks[3]]
        BK_Y1 = [banks[4], banks[6]]
        HG = [Buf("hg%d" % g_) for g_ in range(4)]
        HBG = [Buf("hbg%d" % g_) for g_ in range(4)]
        dtb = ROWB.ap[:, l * 160 + d * 32:l * 160 + d * 32 + 32]
        aneg = ANEG.ap[:, l * 64 + d * 32:l * 64 + d * 32 + 32]
        dsk = ROWB.ap[:, l * 160 + 128:l * 160 + 160]

        P.add("dve", lambda h: h.memset(H32s.ap, 0.0), [], HG)
        P.add("dve", lambda h: h.memset(HBF.ap, 0.0), [], HBG)
        if fwd:
            for j in range(24):
                for k in range(KS):
                    ts("dve" if (j + k) % 2 else "pool", DG.ap[:, j, k, :], IDENT, vcol(vb + V_SCW + k * 24 + j), ALU.mult,
                       [CONST, VEC], [DG])
        tiles = list(range(NT)) if fwd else list(range(NT - 1, -1, -1))
        chunks = [0, 1, 2, 3] if fwd else [3, 2, 1, 0]
        seq = [(ti, i, q) for ti, i in enumerate(tiles) for q in chunks]
        ctxs = {}

        def tile_dma(ti, i):
            t0 = i * 512
            XS = XS2[ti % 2]
            dma("sp", DTT.ap, DT_d.rearrange("(n p) c -> p n c", p=128)[:, i * 4:(i + 1) * 4, :], [], [DTT.buf], "dtt")
            if fwd:
                lo, hi = t0 - 2, t0 + 514
                clo, chi = max(lo, 0), min(hi, ntok)
                if clo > lo:
                    memset("dve", XP, XP.ap[:, :, 0:clo - lo], 0.0)
                if chi < hi:
                    memset("dve", XP, XP.ap[:, :, 516 - (hi - chi):516], 0.0)
                dma("sp", XP.ap[:, :, clo - lo:chi - lo], fm(XBCP_d)[:, :, clo:chi], [], [XP.buf], "xp", nsplit=3)
            else:
                dma("sp", XS.ap, fm(XSBC_d)[:, :, t0:t0 + 512], [], [XS.buf], ("xs", ti % 2), nsplit=3)

        def tile_load(ti, i):
            t0 = i * 512
            XS = XS2[ti % 2]
            DTP, DA, DAH, DAL = DTP2[ti % 2], DA2[ti % 2], DAH2[ti % 2], DAL2[ti % 2]
            if fwd and t0 + 512 == link:
                ts("dve", XP.ap[:, :, 514:516], XP.ap[:, :, 514:516], FLAG.ap[:, 0:1], ALU.mult, [XP, FLAG], [XP])
            tt("dve", DTX.ap, DTT.ap[:, :, d * 32:d * 32 + 32], dtb.unsqueeze(1).broadcast_to([128, 4, 32]), ALU.add,
               [DTT, ROWB], [DTX])
            act(DTX.ap, DTX.ap, AF.Exp, [DTX], [DTX])
            act(DTP.ap, DTX.ap, AF.Ln, [DTX], [DTP], bias=1.0)
            tt("dve", DA.ap, DTP.ap, aneg.unsqueeze(1).broadcast_to([128, 4, 32]), ALU.mult, [DTP, ANEG], [DA])
            cp("dve", DAH.ap, DA.ap, [DA], [DAH])
            tt("dve", DAL.ap, DA.ap, DAH.ap, ALU.subtract, [DA, DAH], [DAL])
            if fwd:
                for j in range(24):
                    bk = rb()
                    for k in range(KS):
                        mm(bk, bk.ap, DG.ap[:, j, k, :], XP.ap[:, j, k:k + 512], k == 0, k == KS - 1, [DG, XP])
                    act(XS.ap[:, j, :], bk.ap, AF.Silu, [bk, VEC], [XS], bias=vcol(vb + V_SCB + j))
                dma("pool", fm(XSBC_d)[:, :, t0:t0 + 512], XS.ap, [XS.buf], [], ("xsst", ti % 2), nsplit=3)

        def prep(n, piece):
            ti, i, q = seq[n]
            XS = XS2[ti % 2]
            DTP, DA = DTP2[ti % 2], DA2[ti % 2]
            tq = slice(q * 128, (q + 1) * 128)
            tok0 = i * 512 + q * 128
            if piece == 0:
                c = {}
                c.update(XS=XS, DAH=DAH2[ti % 2], DAL=DAL2[ti % 2], q=q, tq=tq, tok0=tok0, ti=ti, i=i)
                c["g"] = {}
                ctxs[n] = c
                sm, _ = SMALL.next()
                c["sm"] = sm
                NACS, EACS, ETOT, DEC, TMPS = [sm.ap[:, k_, :] for k_ in range(5)]
                bs = rb()
                mm(bs, bs.ap[:, 0:32], TRI, DA.ap[:, q, :], True, True, [CONST, DA])
                mm(bs, bs.ap[:, 32:64], ONES, DA.ap[:, q, :], True, True, [CONST, DA])
                ts("dve", NACS, bs.ap[:, 0:32], -1.0, ALU.mult, [bs], [sm])
                act(EACS, bs.ap[:, 0:32], AF.Exp, [bs], [sm])
                act(ETOT, bs.ap[:, 32:64], AF.Exp, [bs], [sm])
                tt("dve", TMPS, bs.ap[:, 32:64], NACS, ALU.add, [bs, sm], [sm])
                act(DEC, TMPS, AF.Exp, [sm], [sm])
                xsdt, _ = XSDT.next()
                c["xsdt"] = xsdt
                if fwd:
                    dx, _ = DX.next()
                    c["dx"] = dx
                return
            c = ctxs[n]
            if piece in (1, 2):
                half = piece - 1
                if (not fwd) and half == 0:
                    yfw, semy = YFW.next()
                    dma("sp", yfw.ap, YF_d[tok0:tok0 + 128, :], [], [yfw.buf], semy)
                    c["yfw"] = yfw
                xsdt = c["xsdt"]
                bk = rb()
                bkb = bk.ap.bitcast(BF16)
                for jj in range(8):
                    tr(bk, bkb[:, jj * 128:(jj + 1) * 128], XS.ap[:, half * 8 + jj, tq], IDENTB, [XS, CBF])
                src = bkb.rearrange("p (h e) -> p h e", e=64)
                tt("dve", xsdt.ap[:, half * 1024:(half + 1) * 1024].rearrange("p (h e) -> p h e", e=64), src,
                   bc(DTP.ap[:, q, half * 16:(half + 1) * 16], 64), ALU.mult, [bk, DTP], [xsdt])
                if fwd:
                    dx = c["dx"]
                    tt("dve", dx.ap[:, half * 1024:(half + 1) * 1024].rearrange("p (h e) -> p h e", e=64), src,
                       bc(dsk[:, half * 16:(half + 1) * 16], 64), ALU.mult, [bk, ROWB], [dx])
                return
            btok, _ = BTOK.next()
            c["btok"] = btok
            bk = rb()
            bkb = bk.ap.bitcast(BF16)
            for g in range(4):
                tr(bk, bkb[:, g * 128:(g + 1) * 128], XS.ap[:, 16 + g, tq], IDENTB, [XS, CBF])
            cp("act", btok.ap, bkb[:, 0:512].rearrange("p (g n) -> p g n", n=128), [bk], [btok])
            bcb = rb()
            for g in range(4):
                mm(bcb, bcb.ap[:, g * 128:(g + 1) * 128], XS.ap[:, 16 + g, tq], XS.ap[:, 20 + g, tq], True, True, [XS])
            cbt, _ = CBTS.next()
            cp("act", cbt.ap, bcb.ap, [bcb], [cbt])
            c["cbt"] = cbt

        def abc(n, s):
            c = ctxs[n]
            q, sm, xsdt = c["q"], c["sm"], c["xsdt"]
            NACS, EACS, ETOT, DEC, TMPS = [sm.ap[:, k_, :] for k_ in range(5)]
            if s % 2 == 0:
                g = s // 2
                gs = slice(g * 512, (g + 1) * 512)
                xsd, _ = XSD.next()
                tt("pool", xsd.ap.rearrange("p (h e) -> p h e", e=64), xsdt.ap[:, gs].rearrange("p (h e) -> p h e", e=64),
                   bc(DEC[:, g * 8:(g + 1) * 8], 64), ALU.mult, [xsdt, sm], [xsd])
                ht, _ = HT.next()
                P.add("pool", (lambda h, o=ht.ap.rearrange("p (h e) -> p h e", e=64),
                               i0=H32s.ap[:, gs].rearrange("p (h e) -> p h e", e=64),
                               i1=bc(ETOT[:, g * 8:(g + 1) * 8], 64): h.tensor_tensor(out=o, in0=i0, in1=i1, op=ALU.mult)),
                      [HG[g], sm.buf], [ht.buf])
                c["xsd%d" % g] = xsd
                c["ht%d" % g] = ht
            e_, _ = ERING.next()
            g_, _ = GRING.next()
            g = s // 2
            ba = BK_A[(n * 8 + s) % 4]
            for hq in range(4):
                h_ = s * 4 + hq
                o_ = ba.ap[:, hq * 128:(hq + 1) * 128]
                P.add("pe", (lambda h, o=o_, l_=c["DAH"].ap[:, q, h_:h_ + 1].broadcast_to([128, 128]):
                             h.matmul(o, l_, TRIB, start=True, stop=False)), [c["DAH"].buf, CBF.buf], [ba.buf])
                P.add("pe", (lambda h, o=o_, l_=c["DAL"].ap[:, q, h_:h_ + 1].broadcast_to([128, 128]):
                             h.matmul(o, l_, TRIB, start=False, stop=False)), [c["DAL"].buf, CBF.buf], [ba.buf])
                P.add("pe", (lambda h, o=o_: h.matmul(o, IDENTB, NEGM, start=False, stop=True)), [CBF.buf], [ba.buf])
            for hq in range(4):
                h_ = s * 4 + hq
                P.add("act", (lambda h, o=e_.ap[:, hq, :], i_=ba.ap[:, hq * 128:(hq + 1) * 128], b_=NACS[:, h_:h_ + 1]:
                              h.activation(out=o, in_=i_, func=AF.Exp, bias=b_)), [ba.buf, sm.buf], [e_.buf])
            tt("dve", g_.ap, e_.ap, c["cbt"].ap[:, g * 128:(g + 1) * 128].unsqueeze(1).broadcast_to([128, 4, 128]), ALU.mult,
               [e_, c["cbt"]], [g_])
            c["g"][s] = g_

        def y1(n, s):
            c = ctxs[n]
            xsdt = c["xsdt"]
            g = s // 2
            by1 = BK_Y1[(n * 4 + g) % 2]
            for hq in range(4):
                h_ = s * 4 + hq
                hh = h_ % 8
                g_ = c["g"][s]
                mm(by1, by1.ap[:, hh * 64:(hh + 1) * 64], g_.ap[:, hq, :], xsdt.ap[:, h_ * 64:(h_ + 1) * 64], True, True,
                   [g_, xsdt])
            del c["g"][s]

        def epiB(n, g):
            c = ctxs[n]
            sm, tok0 = c["sm"], c["tok0"]
            EACS = sm.ap[:, 1, :]
            gs = slice(g * 512, (g + 1) * 512)
            by1 = BK_Y1[(n * 4 + g) % 2]
            BK_Y2, BK_ST = rb(), rb()
            P.add("pe", (lambda h, o=BK_Y2.ap, l_=c["XS"].ap[:, 20 + g, c["tq"]], r_=HBF.ap[:, gs]:
                         h.matmul(o, l_, r_, start=True, stop=True)), [c["XS"].buf, HBG[g]], [BK_Y2.buf])
            xsd = c["xsd%d" % g]
            mm(BK_ST, BK_ST.ap, c["btok"].ap[:, g, :], xsd.ap, True, True, [c["btok"], xsd])
            t1, _ = T1R.next()
            tt("dve", t1.ap.rearrange("p (h e) -> p h e", e=64), BK_Y2.ap.rearrange("p (h e) -> p h e", e=64),
               bc(EACS[:, g * 8:(g + 1) * 8], 64), ALU.mult, [BK_Y2, sm], [t1])
            t2, _ = T2R.next()
            tt("dve", t2.ap, t1.ap, by1.ap, ALU.add, [t1, by1], [t2])
            ht = c["ht%d" % g]
            P.add("dve", (lambda h, o=H32s.ap[:, gs], i0=ht.ap, i1=BK_ST.ap: h.tensor_tensor(out=o, in0=i0, in1=i1, op=ALU.add)),
                  [ht.buf, BK_ST.buf], [HG[g]])
            if (not fwd) and tok0 == link:
                P.add("dve", (lambda h, o=H32s.ap[:, gs]: h.tensor_scalar(out=o, in0=o, scalar1=FLAG.ap[:, 0:1], scalar2=None,
                                                                        op0=ALU.mult)), [HG[g], FLAG.buf], [HG[g]])
            if fwd:
                ys, semy = YST.next()
                tt("pool", ys.ap, t2.ap, c["dx"].ap[:, gs], ALU.add, [t2, c["dx"]], [ys])
                dma("sp", YF_d[tok0:tok0 + 128, gs], ys.ap, [ys.buf], [], semy)
            else:
                tt("pool", t2.ap, t2.ap, c["yfw"].ap[:, gs], ALU.add, [t2, c["yfw"]], [t2])
                c["t2_%d" % g] = t2

        def epiC(n, g):
            c = ctxs[n]
            gs = slice(g * 512, (g + 1) * 512)
            P.add("act", (lambda h, o=HBF.ap[:, gs], i_=H32s.ap[:, gs]: h.activation(out=o, in_=i_, func=AF.Copy)),
                  [HG[g]], [HBG[g]])
            if not fwd:
                YFM = YFM2[c["ti"] % 2]
                t2 = c["t2_%d" % g]
                bt = rb()
                for fb in range(4):
                    tr(bt, bt.ap[:, fb * 128:(fb + 1) * 128], t2.ap[:, fb * 128:(fb + 1) * 128], IDENT, [t2, CONST])
                cp("act", YFM.ap[:, g * 4:(g + 1) * 4, c["tq"]], bt.ap.rearrange("p (f t) -> p f t", t=128), [bt], [YFM])
            if g == 3:
                if (not fwd) and c["q"] == chunks[-1]:
                    t0 = c["i"] * 512
                    dma("sp", fm(Y_d)[:, :, t0:t0 + 512], YFM2[c["ti"] % 2].ap, [YFM2[c["ti"] % 2].buf], [],
                        ("yfmst", c["ti"] % 2), nsplit=2)

        tile_dma(0, tiles[0])
        tile_load(0, tiles[0])
        for pc in range(4):
            prep(0, pc)
        flat = [(n, s) for n in range(len(seq)) for s in range(8)]
        pend = []

        def run_due(idx):
            keep = []
            for due, fn in pend:
                if due <= idx:
                    fn()
                else:
                    keep.append((due, fn))
            pend[:] = keep

        LAG = 3
        for idx, (n, s) in enumerate(flat):
            abc(n, s)
            run_due(idx)
            pend.append((idx + LAG, (lambda n=n, s=s: y1(n, s))))
            if s % 2 == 1:
                g = s // 2
                pend.append((idx + LAG, (lambda n=n, g=g: epiB(n, g))))
                pend.append((idx + LAG + 1, (lambda n=n, g=g: epiC(n, g))))
            if s == 5 and seq[n][2] == chunks[0] and seq[n][0] + 1 < len(tiles):
                tile_dma(seq[n][0] + 1, tiles[seq[n][0] + 1])
            if n + 1 < len(seq) and s in (0, 2, 4, 6):
                if s == 0 and seq[n + 1][0] != seq[n][0]:
                    tile_load(seq[n + 1][0], seq[n + 1][1])
                prep(n + 1, s // 2)
        for k in range(len(flat), len(flat) + 6):
            run_due(k)
        assert not pend
        P.barrier()

    def sweep_C(l):
        arena.reset()
        last = l == depth - 1
        vb = V_L0 + l * V_LSZ
        cm = common(arena, nh=1)
        BIGT = arena.take("bigt", [128, 16, 512], F32)
        CV = arena.take("cv", [128, 8, 512], F32)
        RSTD2 = arena.take("rstd2", [128, 512], F32)
        TMP02 = arena.take("tmp02", [128, 512], F32)
        VPT = arena.take("vpt", [128, 8, 542], BF16)
        LY = Ring(arena, "ly", 2, [128, 512], F32)
        LZ = Ring(arena, "lz", 2, [128, 512], F32)
        LG = Ring(arena, "lg", 3, [128, 512], F32)
        U, HID = cm["U"], cm["HID"]
        hid_flat = HID.ap.rearrange("p a b -> p (a b)")
        YN = V(hid_flat[:, 0:16 * 512].rearrange("p (a b) -> p a b", b=512), HID.buf)
        MB = V(hid_flat[:, 0:8 * 512].rearrange("p (a b) -> p a b", b=512), HID.buf)
        GY = BIGT
        M1 = V(BIGT.ap[:, 8:16, :], BIGT.buf)
        ws = WStream(cm["wslots"], "wC")
        for i in range(NT):
            s = []
            for fc in range(8):
                s.append(slab_diag(l, fc))
            for m0 in range(0, 8, 2):
                s.append(slab_cols("pa", l, m0 * 128, 256, 16))
            for q in range(2):
                s.append(slab_cols("pb", l, q * 512, 512, 8))
            for q in range(2):
                s.append(slab_cols("wo", l, q * 512, 512, 8))
            s += ffn_sched("f2u", "f2d", l)
            ws.extend(s)

        for i in range(NT):
            t0 = i * 512
            Hs = cm["H32"][0]
            dma("sp", Hs.ap, fm(H1_d)[:, :, t0:t0 + 512], [], [Hs.buf], ("h32", 0), nsplit=2)
            lo, hi = t0 - 15, t0 + 527
            clo, chi = max(lo, 0), min(hi, ntok)
            if clo > lo:
                memset("dve", VPT, VPT.ap[:, :, 0:clo - lo], 0.0)
            if chi < hi:
                memset("dve", VPT, VPT.ap[:, :, 542 - (hi - chi):542], 0.0)
            dma("sp", VPT.ap[:, :, clo - lo:chi - lo], fm(VP_d)[:, :, clo:chi], [], [VPT.buf], "vpt")
            if t0 + 512 == link:
                ts("dve", VPT.ap[:, :, 527:542], VPT.ap[:, :, 527:542], FLAG.ap[:, 0:1], ALU.mult, [VPT, FLAG], [VPT])
            for f2 in range(16):
                ly, sy = LY.next()
                lz, sz = LZ.next()
                dma("sp", ly.ap, fm(Y_d)[:, f2, t0:t0 + 512], [], [ly.buf], sy)
                dma("sp", lz.ap, fm(ZS_d)[:, f2, t0:t0 + 512], [], [lz.buf], sz)
                tt("pool", GY.ap[:, f2, :], ly.ap, lz.ap, ALU.mult, [ly, lz], [GY])
            for fc in range(8):
                slot = ws.next()
                wv = slot.ap.bitcast(BF16)[:, 0:KCF * 128].rearrange("p (k j) -> p k j", j=128)
                bk = nextbank()
                for k in range(KCF):
                    mm(bk, bk.ap, wv[:, k, :], VPT.ap[:, fc, k:k + 512], k == 0, k == KCF - 1, [slot, VPT])
                act(CV.ap[:, fc, :], bk.ap, AF.Identity, [bk, VEC], [CV], bias=vcol(vb + V_CCB + fc))
            bk = nextbank()
            for fc in range(16):
                sq, _ = cm["sqb"].next()
                act(sq.ap, GY.ap[:, fc, :], AF.Square, [GY], [sq])
                mm(bk, bk.ap, ONESB, sq.ap, fc == 0, fc == 15, [sq, CBF])
            rstd_from(bk, cm["rstd"], cm["tmp0"], DI)
            for fc in range(16):
                stt(YN.ap[:, fc, :], GY.ap[:, fc, :], vcol(vb + V_SNRM + fc), ALU.mult, bk.ap, ALU.mult,
                    [GY, bk, VEC], [YN])
            b1, b2 = nextbank(), nextbank()
            for fc in range(8):
                mm(b1, b1.ap, ONES, CV.ap[:, fc, :], fc == 0, fc == 7, [CV, CONST])
            for fc in range(8):
                sq, _ = cm["sqb"].next()
                act(sq.ap, CV.ap[:, fc, :], AF.Square, [CV], [sq])
                mm(b2, b2.ap, ONESB, sq.ap, fc == 0, fc == 7, [sq, CBF])
            MEAN, _ = cm["tmpn"].next()
            act(MEAN.ap, b1.ap, AF.Copy, [b1], [MEAN], scale=1.0 / D)
            MSQ, _ = cm["tmpn"].next()
            tt("dve", MSQ.ap, MEAN.ap, MEAN.ap, ALU.mult, [MEAN], [MSQ])
            stt(MSQ.ap, b2.ap, 1.0 / D, ALU.mult, MSQ.ap, ALU.subtract, [b2, MSQ], [MSQ])
            act(TMP02.ap, MSQ.ap, AF.Ln, [MSQ], [TMP02], bias=EPS)
            act(RSTD2.ap, TMP02.ap, AF.Exp, [TMP02], [RSTD2], scale=-0.5)
            for m0 in range(0, 8, 2):
                slot = ws.next()
                wv = wview(slot, 16, 256)
                for mm_ in range(2):
                    mc = m0 + mm_
                    bk = nextbank()
                    for kc in range(16):
                        mm(bk, bk.ap, wv[:, kc, mm_ * 128:(mm_ + 1) * 128], YN.ap[:, kc, :], kc == 0, kc == 15, [slot, YN])
                    lg, sg = LG.next()
                    dma("sp", lg.ap, fm(GA_d)[:, mc, t0:t0 + 512], [], [lg.buf], sg)
                    tt("dve", M1.ap[:, mc, :], lg.ap, bk.ap, ALU.mult, [lg, bk], [M1])
            for fc in range(8):
                t, _ = cm["silu"].next()
                tt("dve", t.ap, CV.ap[:, fc, :], MEAN.ap, ALU.subtract, [CV, MEAN], [t])
                stt(t.ap, t.ap, vcol(vb + V_LNG + fc), ALU.mult, RSTD2.ap, ALU.mult, [t, RSTD2, VEC], [t])
                act(U.ap[:, fc, :], t.ap, AF.Silu, [t, VEC], [U], bias=vcol(vb + V_LNB + fc))
            for q in range(2):
                slot = ws.next()
                wv = wview(slot, 8, 512)
                for mm_ in range(4):
                    mc = q * 4 + mm_
                    bk = nextbank()
                    for kc in range(8):
                        mm(bk, bk.ap, wv[:, kc, mm_ * 128:(mm_ + 1) * 128], U.ap[:, kc, :], kc == 0, kc == 7, [slot, U])
                    lg, sg = LG.next()
                    dma("sp", lg.ap, fm(GB_d)[:, mc, t0:t0 + 512], [], [lg.buf], sg)
                    t, _ = cm["silu"].next()
                    tt("dve", t.ap, lg.ap, bk.ap, ALU.mult, [lg, bk], [t])
                    tt("pool", MB.ap[:, mc, :], t.ap, M1.ap[:, mc, :], ALU.add, [t, M1], [MB])
            for q in range(2):
                slot = ws.next()
                wv = wview(slot, 8, 512)
                for mm_ in range(4):
                    mc = q * 4 + mm_
                    bk = nextbank()
                    for kc in range(8):
                        mm(bk, bk.ap, wv[:, kc, mm_ * 128:(mm_ + 1) * 128], MB.ap[:, kc, :], kc == 0, kc == 7, [slot, MB])
                    stt(Hs.ap[:, mc, :], bk.ap, dcol(l, 1, 2, mc), ALU.mult, Hs.ap[:, mc, :], ALU.add, [bk, Hs, DER], [Hs])
            norm_mod(Hs, U, l, 2, cm)
            ffn(Hs, U, HID, l, 2, ws, cm)
            if not last:
                dma("pool", fm(HN_d)[:, :, t0:t0 + 512], Hs.ap, [Hs.buf], [], ("hnst", i % 2), nsplit=2)
            else:
                bk = nextbank()
                for fc in range(8):
                    sq, _ = cm["sqb"].next()
                    act(sq.ap, Hs.ap[:, fc, :], AF.Square, [Hs], [sq])
                    mm(bk, bk.ap, ONESB, sq.ap, fc == 0, fc == 7, [sq, CBF])
                rstd_from(bk, cm["rstd"], cm["tmp0"], D)
                OUTF = CV
                for fc in range(8):
                    stt(OUTF.ap[:, fc, :], Hs.ap[:, fc, :], vcol(V_FN + fc), ALU.mult, bk.ap, ALU.mult,
                        [Hs, bk, VEC], [OUTF])
                OT = V(BIGT.ap[:, 8:16, :].rearrange("p a b -> p (a b)").rearrange("p (q f) -> p q f", f=1024), BIGT.buf)
                for q in range(4):
                    for half in range(2):
                        bk = nextbank()
                        for ff in range(4):
                            fc = half * 4 + ff
                            tr(bk, bk.ap[:, ff * 128:(ff + 1) * 128], OUTF.ap[:, fc, q * 128:(q + 1) * 128], IDENT,
                               [OUTF, CONST])
                        cp("act" if half else "dve", OT.ap[:, q, half * 512:(half + 1) * 512], bk.ap, [bk], [OT])
                dma("pool", y_d.rearrange("(n p) f -> p n f", p=128)[:, i * 4:(i + 1) * 4, :], OT.ap, [OT.buf], [], "yout")
        P.barrier()

    for l in range(depth):
        sweep_A(l)
        sweep_S(l, 0)
        sweep_S(l, 1)
        sweep_C(l)
    P.barrier()
    for e in ENGS:
        P.add(e, lambda h: h.nop())
    with nc.Block() as block:
        P.emit(nc, block)
    return nc, P


def make_consts():
    c = np.zeros((128, 768), np.float32)
    c[:, 0:128] = np.eye(128, dtype=np.float32)
    c[:, 128:256] = np.triu(np.ones((128, 128), np.float32))
    c[:, 256:384] = np.tril(np.ones((128, 128), np.float32))
    c[:, 384:512] = -30000.0 * np.tril(np.ones((128, 128), np.float32), -1)
    c[:, 512:640] = -30000.0 * np.triu(np.ones((128, 128), np.float32), 1)
    c[:, 640:768] = 1.0
    return c


def make_vecs(c, inp):
    rows = np.zeros((V_ROWS, 128), np.float32)
    rows[V_C:V_C + 8] = np.asarray(c, np.float32).reshape(8, 128)
    rows[V_FN:V_FN + 8] = np.asarray(inp["final_norm"], np.float32).reshape(8, 128)
    for l in range(DEPTH):
        b = V_L0 + l * V_LSZ
        rows[b + V_BADA:b + V_BADA + 72] = inp["b_ada"][l].reshape(72, 128)
        rows[b + V_N1:b + V_N1 + 8] = inp["ffn1_norm"][l].reshape(8, 128)
        rows[b + V_N2:b + V_N2 + 8] = inp["mix_norm"][l].reshape(8, 128)
        rows[b + V_N3:b + V_N3 + 8] = inp["ffn2_norm"][l].reshape(8, 128)
        rows[b + V_SCW:b + V_SCW + 120] = inp["ssm_conv_w"][l].reshape(KS * 24, 128)
        rows[b + V_SCB:b + V_SCB + 24] = inp["ssm_conv_b"][l].reshape(24, 128)
        rows[b + V_SNRM:b + V_SNRM + 16] = inp["ssm_norm"][l].reshape(16, 128)
        rows[b + V_CCW:b + V_CCW + 248] = inp["conf_conv_w"][l].reshape(KCF * 8, 128)
        rows[b + V_CCB:b + V_CCB + 8] = inp["conf_conv_b"][l].reshape(8, 128)
        rows[b + V_LNG:b + V_LNG + 8] = inp["conf_ln_g"][l].reshape(8, 128)
        rows[b + V_LNB:b + V_LNB + 8] = inp["conf_ln_b"][l].reshape(8, 128)
    return rows


def make_rowb(inp):
    r = np.zeros((DEPTH, 160), np.float32)
    for l in range(DEPTH):
        r[l, 0:64] = inp["dt_bias"][l].reshape(64)
        r[l, 64:128] = inp["a_log"][l].reshape(64)
        r[l, 128:160] = inp["d_skip"][l].reshape(32)
    return np.ascontiguousarray(np.broadcast_to(r.reshape(1, DEPTH * 160), (128, DEPTH * 160)))


_CACHE = {}


def kernel(**inputs):
    inp = {k: np.asarray(v) for k, v in inputs.items()}
    if "nc" not in _CACHE:
        _CACHE["nc"] = build()[0]
    nc = _CACHE["nc"]
    xp, xs = inp["x_prompt"], inp["x_sample"]
    consts = make_consts()
    rowb = make_rowb(inp)
    shared = {"consts": consts, "rowb": rowb, "w_ada": inp["w_ada"],
              "ffn1_up": inp["ffn1_up"], "ffn1_down": inp["ffn1_down"], "w_in": inp["w_in"],
              "w_proj_ssd": inp["w_proj_ssd"], "w_proj_conv": inp["w_proj_conv"], "w_out": inp["w_out"],
              "ffn2_up": inp["ffn2_up"], "ffn2_down": inp["ffn2_down"]}
    in_maps = []
    for core in range(NCORES):
        m = dict(shared)
        if core < 4:
            m["x"] = np.ascontiguousarray(xp[core])
            m["vecs"] = make_vecs(inp["c_prompt"][core], inp)
            m["flag"] = np.ones((128, 1), np.float32)
        else:
            b = core - 4
            xx = np.zeros((NTOK_FULL, D), np.float32)
            xx[:LINK_FULL] = xs[b]
            m["x"] = xx
            m["vecs"] = make_vecs(inp["c_sample"][b], inp)
            m["flag"] = np.zeros((128, 1), np.float32)
        in_maps.append(m)
    res = run_bass_kernel_spmd(nc, in_maps, core_ids=list(range(NCORES)))
    y_prompt = np.stack([np.asarray(res.results[c]["y"], np.float32) for c in range(4)], axis=0)
    y_sample = np.stack([np.asarray(res.results[4 + b]["y"], np.float32)[:LINK_FULL] for b in range(4)], axis=0)
    return (y_prompt, y_sample)
```
